# Optimizing a Trainium2 kernel written in Bass

```python
import jax, jax.numpy as jnp
from jax import lax
import numpy as np

D_MODEL = 1024
BATCH = 4
SEQ = 8192
DEPTH = 2

N_MIXERS = 2
MLSTM_HEADS = 4
MLSTM_DK = 128
MLSTM_DV = D_MODEL // MLSTM_HEADS
MLSTM_CHUNK = 128
QK_WIDTH = MLSTM_HEADS * MLSTM_DK
V_WIDTH = MLSTM_HEADS * MLSTM_DV
N_GATE_COLS = 4 * MLSTM_HEADS
MLSTM_IN_WIDTH = 2 * QK_WIDTH + 2 * V_WIDTH + N_GATE_COLS
FORGET_BIAS = 3.0
POOL_WINDOWS = (2, 4, 8, 16)
POOL_GROUPS = len(POOL_WINDOWS)
POOL_GROUP_WIDTH = D_MODEL // POOL_GROUPS
D_FF = 4 * D_MODEL
PLE_DIM = 256
N_MLSTM_LAYERS = (DEPTH + N_MIXERS - 1) // N_MIXERS
N_POOL_LAYERS = DEPTH // N_MIXERS
EPS = 1e-6

kernel_name = 'hybrid_mlstm_pool_encoder'


def rmsnorm(x, gain):
    x32 = x.astype(jnp.float32)
    y = x32 * lax.rsqrt(jnp.mean(jnp.square(x32), axis=-1, keepdims=True) + EPS)
    return (y * gain.astype(jnp.float32)).astype(x.dtype)


def mlstm_scan(q, k, v, i_pre, f_pre):
    B, H, S, DK = q.shape
    DV = v.shape[-1]
    L = MLSTM_CHUNK
    NC = S // L

    def chunks(t):
        return jnp.moveaxis(t.reshape((B, H, NC, L) + t.shape[3:]), 2, 0)

    qc, kc, vc = chunks(q), chunks(k), chunks(v)
    ic = chunks(i_pre)
    bc = jnp.cumsum(chunks(jax.nn.log_sigmoid(f_pre)), axis=-1)
    lower = jnp.tril(jnp.ones((L, L), dtype=bool))

    def step(carry, xs):
        C, n, m = carry
        q_, k_, v_, i_, b_ = xs
        b_last = b_[..., -1]
        log_d = jnp.where(lower, b_[..., :, None] - b_[..., None, :] + i_[..., None, :], -jnp.inf)
        log_inter = b_ + m[..., None]
        m_t = jnp.maximum(log_inter, jnp.max(log_d, axis=-1))
        d = jnp.exp(log_d - m_t[..., None])
        inter = jnp.exp(log_inter - m_t)
        s = jnp.einsum('bhtd,bhsd->bhts', q_, k_) * d
        num = jnp.einsum('bhts,bhsv->bhtv', s, v_) + inter[..., None] * jnp.einsum('bhtd,bhvd->bhtv', q_, C)
        den = jnp.sum(s, axis=-1) + inter * jnp.einsum('bhtd,bhd->bht', q_, n)
        h = num / jnp.maximum(jnp.abs(den), jnp.exp(-m_t))[..., None]
        log_w = b_last[..., None] - b_ + i_
        m_new = jnp.maximum(b_last + m, jnp.max(log_w, axis=-1))
        w = jnp.exp(log_w - m_new[..., None])
        decay = jnp.exp(b_last + m - m_new)
        C = decay[..., None, None] * C + jnp.einsum('bhsv,bhsd->bhvd', w[..., None] * v_, k_)
        n = decay[..., None] * n + jnp.einsum('bhs,bhsd->bhd', w, k_)
        return (C, n, m_new), h

    init = (jnp.zeros((B, H, DV, DK), jnp.float32),
            jnp.zeros((B, H, DK), jnp.float32),
            jnp.zeros((B, H), jnp.float32))
    _, hs = lax.scan(step, init, (qc, kc, vc, ic, bc))
    return jnp.moveaxis(hs, 0, 2).reshape(B, H, S, DV)


def mlstm_mixer(xn, w_in, b_gates, head_norm, w_out):
    B, S, _ = xn.shape
    u = xn @ w_in
    q, k, v, o, g = jnp.split(u, [QK_WIDTH, 2 * QK_WIDTH, 2 * QK_WIDTH + V_WIDTH,
                                  2 * QK_WIDTH + 2 * V_WIDTH], axis=-1)
    q = q.reshape(B, S, MLSTM_HEADS, MLSTM_DK).transpose(0, 2, 1, 3).astype(jnp.float32)
    k = (k.reshape(B, S, MLSTM_HEADS, MLSTM_DK).transpose(0, 2, 1, 3).astype(jnp.float32)
         * (MLSTM_DK ** -0.5))
    v = v.reshape(B, S, MLSTM_HEADS, MLSTM_DV).transpose(0, 2, 1, 3).astype(jnp.float32)
    g = g.reshape(B, S, 4, MLSTM_HEADS).astype(jnp.float32) + b_gates.astype(jnp.float32)
    g = g.transpose(2, 0, 3, 1)
    h_fwd = mlstm_scan(q, k, v, g[0], g[1])
    flip = lambda t: jnp.flip(t, axis=2)
    h_bwd = flip(mlstm_scan(flip(q), flip(k), flip(v), flip(g[2]), flip(g[3])))
    h = (h_fwd + h_bwd).transpose(0, 2, 1, 3)
    h = h * lax.rsqrt(jnp.mean(jnp.square(h), axis=-1, keepdims=True) + EPS)
    h = h * head_norm.astype(jnp.float32).reshape(MLSTM_HEADS, MLSTM_DV)
    h = h.reshape(B, S, V_WIDTH) * jax.nn.sigmoid(o.astype(jnp.float32))
    return h.astype(xn.dtype) @ w_out


def pool_mixer(xn, w_in, w_grp, scale, w_out):
    B, S, D = xn.shape
    u = (xn @ w_in).astype(jnp.float32)
    cs = jnp.concatenate([jnp.zeros((B, 1, D), jnp.float32), jnp.cumsum(u, axis=1)], axis=1)
    t = jnp.arange(S)
    pooled = []
    for gi, win in enumerate(POOL_WINDOWS):
        c0, c1 = gi * POOL_GROUP_WIDTH, (gi + 1) * POOL_GROUP_WIDTH
        lo = jnp.clip(t - win // 2, 0, S)
        hi = jnp.clip(t + win - win // 2, 0, S)
        csg = cs[:, :, c0:c1]
        mean = (csg[:, hi] - csg[:, lo]) / (hi - lo).astype(jnp.float32)[None, :, None]
        pooled.append(mean)
    y = jnp.concatenate(pooled, axis=-1) - u
    y = jnp.einsum('bsgc,gcd->bsgd', y.reshape(B, S, POOL_GROUPS, POOL_GROUP_WIDTH),
                   w_grp.astype(jnp.float32)).reshape(B, S, D)
    y = y * scale.astype(jnp.float32)
    return y.astype(xn.dtype) @ w_out


def sqrelu_mlp(xn, w1, w2):
    return jnp.square(jax.nn.relu(xn @ w1)) @ w2


def setup_inputs(seed: int = 0) -> dict:
    key = jax.random.key(seed)
    ks = jax.random.split(key, 24)

    def nrm(k, shape, fan_in):
        return jax.random.normal(k, shape, jnp.float32) * (fan_in ** -0.5)

    def gain(k, shape):
        return 1.0 + 0.05 * jax.random.normal(k, shape, jnp.float32)

    nA, nB = N_MLSTM_LAYERS, N_POOL_LAYERS
    gate_base = jnp.array([0.0, FORGET_BIAS, 0.0, FORGET_BIAS], jnp.float32)[None, :, None]
    return {
        'x': jax.random.normal(ks[0], (BATCH, SEQ, D_MODEL), jnp.float32),
        'p': jax.random.normal(ks[1], (DEPTH, BATCH, SEQ, PLE_DIM), jnp.float32),
        'norm_mix': gain(ks[2], (DEPTH, D_MODEL)),
        'norm_mlp': gain(ks[3], (DEPTH, D_MODEL)),
        'norm_ple': gain(ks[4], (DEPTH, D_MODEL)),
        'norm_final': gain(ks[5], (D_MODEL,)),
        'mlstm_w_in': nrm(ks[6], (nA, D_MODEL, MLSTM_IN_WIDTH), D_MODEL),
        'mlstm_b_gates': gate_base + 0.1 * jax.random.normal(ks[7], (nA, 4, MLSTM_HEADS), jnp.float32),
        'mlstm_head_norm': gain(ks[8], (nA, V_WIDTH)),
        'mlstm_w_out': nrm(ks[9], (nA, V_WIDTH, D_MODEL), V_WIDTH),
        'pool_w_in': nrm(ks[10], (nB, D_MODEL, D_MODEL), D_MODEL),
        'pool_w_grp': nrm(ks[11], (nB, POOL_GROUPS, POOL_GROUP_WIDTH, POOL_GROUP_WIDTH), POOL_GROUP_WIDTH),
        'pool_scale': gain(ks[12], (nB, D_MODEL)),
        'pool_w_out': nrm(ks[13], (nB, D_MODEL, D_MODEL), D_MODEL),
        'mlp_w1': nrm(ks[14], (DEPTH, D_MODEL, D_FF), D_MODEL),
        'mlp_w2': nrm(ks[15], (DEPTH, D_FF, D_MODEL), D_FF),
        'ple_w': nrm(ks[16], (DEPTH, PLE_DIM, D_MODEL), PLE_DIM),
        'ple_gate_w': nrm(ks[17], (DEPTH, D_MODEL, D_MODEL), D_MODEL),
        'ple_gate_b': 0.02 * jax.random.normal(ks[18], (DEPTH, D_MODEL), jnp.float32),
    }


def reference(x, p, norm_mix, norm_mlp, norm_ple, norm_final,
              mlstm_w_in, mlstm_b_gates, mlstm_head_norm, mlstm_w_out,
              pool_w_in, pool_w_grp, pool_scale, pool_w_out,
              mlp_w1, mlp_w2, ple_w, ple_gate_w, ple_gate_b):
    h = x
    for i in range(DEPTH):
        xn = rmsnorm(h, norm_mix[i])
        j = i // N_MIXERS
        if i % N_MIXERS == 0:
            mix = mlstm_mixer(xn, mlstm_w_in[j], mlstm_b_gates[j], mlstm_head_norm[j], mlstm_w_out[j])
        else:
            mix = pool_mixer(xn, pool_w_in[j], pool_w_grp[j], pool_scale[j], pool_w_out[j])
        h = h + mix
        h = h + sqrelu_mlp(rmsnorm(h, norm_mlp[i]), mlp_w1[i], mlp_w2[i])
        gate = jax.nn.sigmoid((rmsnorm(h, norm_ple[i]) @ ple_gate_w[i] + ple_gate_b[i]).astype(jnp.float32))
        h = h + (gate * (p[i] @ ple_w[i]).astype(jnp.float32)).astype(h.dtype)
    return rmsnorm(h, norm_final)
```

```python
import numpy as np
import concourse.bass as bass
import concourse.mybir as mybir
from concourse.bass_utils import run_bass_kernel_spmd

F32 = mybir.dt.float32
BF16 = mybir.dt.bfloat16
AF = mybir.ActivationFunctionType
ALU = mybir.AluOpType
AX = mybir.AxisListType

D = 1024
S = 8192
NB = 4
H = 4
DK = 128
DV = 256
DFF = 4096
PLE = 256
EPS = 1e-6
TOK_OWN = 4096
NCH_OWN = 32
LN_KSCALE = float(-0.5 * np.log(128.0))

ENGS = ("pe", "act", "dve", "pool", "sp")


class Res:
    __slots__ = ("name", "lw", "rd", "rd_dma", "sem", "ndma", "excl")

    def __init__(self, name, excl=False):
        self.name = name
        self.excl = excl
        self.lw = None
        self.rd = {}
        self.rd_dma = []
        self.sem = None
        self.ndma = 0


class V:
    __slots__ = ("res", "ap")

    def __init__(self, res, ap):
        self.res = res
        self.ap = ap

    def __getitem__(self, key):
        return V(self.res, self.ap[key])

    def re(self, s, **kw):
        return V(self.res, self.ap.rearrange(s, **kw))

    def bc(self, shape):
        return V(self.res, self.ap.to_broadcast(shape))

    def bitcast(self, dt):
        return V(self.res, self.ap.bitcast(dt))


class Op:
    __slots__ = ("eng", "fn", "deps", "is_dma", "chan", "dval", "needs_inc", "incval", "tag")

    def __init__(self, eng, fn):
        self.eng = eng
        self.fn = fn
        self.deps = []
        self.is_dma = False
        self.chan = None
        self.dval = 0
        self.needs_inc = False
        self.incval = 0


class Sched:
    def __init__(self):
        self.ops = {e: [] for e in ENGS}
        self.chans = []
        self.tag = ""

    def add(self, eng, fn, reads=(), writes=(), chan=None, nowaw=False):
        op = Op(eng, fn)
        op.tag = self.tag
        deps = {}
        is_dma = chan is not None
        xr = [v for v in reads if any(r.excl for r in v.res)]
        if xr:
            writes = list(writes) + xr

        def dep(d, raw):
            if d is None or d is op:
                return
            if (not is_dma) and (not d.is_dma) and d.eng == eng and eng == "pe" and not raw:
                return
            deps[id(d)] = d

        for v in reads:
            for r in v.res:
                dep(r.lw, True)
        for v in writes:
            for r in v.res:
                if not (nowaw and r.lw is not None and r.lw.is_dma and r.lw.chan is chan):
                    dep(r.lw, False)
                for x in r.rd.values():
                    dep(x, False)
                for x in r.rd_dma:
                    dep(x, False)
        op.deps = list(deps.values())
        if is_dma:
            op.is_dma = True
            op.chan = chan
            if chan.sem is None:
                chan.sem = True
                self.chans.append(chan)
            chan.ndma += 1
            op.dval = 16 * chan.ndma
        for v in reads:
            for r in v.res:
                if is_dma:
                    r.rd_dma.append(op)
                else:
                    r.rd[eng] = op
        for v in writes:
            for r in v.res:
                r.lw = op
                r.rd = {}
                r.rd_dma = []
        self.ops[eng].append(op)
        return op

    def emit(self, nc, stack):
        for e in ENGS:
            for op in self.ops[e]:
                for d in op.deps:
                    if not d.is_dma:
                        d.needs_inc = True
        import os
        tagmap = {} if os.environ.get("KTAGS") else None
        self.tagmap = tagmap
        esem = {e: stack.enter_context(nc.semaphore("es_" + e)) for e in ENGS}
        for i, c in enumerate(self.chans):
            c.sem = stack.enter_context(nc.semaphore("dc%d_%s" % (i, c.name)))
        for e in ENGS:
            cnt = 0
            for op in self.ops[e]:
                if op.needs_inc and not op.is_dma:
                    cnt += 1
                    op.incval = cnt
        handles = {"pe": "tensor", "act": "scalar", "dve": "vector", "pool": "gpsimd", "sp": "sync"}
        block = stack.enter_context(nc.Block())
        final_waits = [(c.sem, 16 * c.ndma) for c in self.chans]

        def mk(e):
            def body(eng):
                seen = {}
                for op in self.ops[e]:
                    waits = {}
                    for d in op.deps:
                        if d.is_dma:
                            s, v = d.chan.sem, d.dval
                        else:
                            s, v = esem[d.eng], d.incval
                        k = id(s)
                        if seen.get(k, 0) >= v:
                            continue
                        if k not in waits or waits[k][1] < v:
                            waits[k] = (s, v)
                    for s, v in waits.values():
                        eng.wait_ge(s, v)
                        seen[id(s)] = v
                    ins = op.fn(eng)
                    if tagmap is not None:
                        try:
                            tagmap[ins.ins.name] = op.tag
                        except Exception:
                            pass
                    if op.is_dma:
                        ins.then_inc(op.chan.sem, 16)
                    elif op.needs_inc:
                        ins.then_inc(esem[e], 1)
                if e == "sp":
                    for s, v in final_waits:
                        eng.wait_ge(s, v)
            return body

        for e in ENGS:
            getattr(block, handles[e])(mk(e))


class _Stop(Exception):
    pass


class Builder:
    def dump(self, v, n, bf16=False):
        if self.dbg is None:
            return
        i = self.ndump
        self.ndump += 1
        r = Res("dump%d" % i)
        self.dma("pool" if bf16 else "sp", self.dbg[i, :, 0:n], v.ap, r, reads=[v])
        return i

    def check_stop(self, name):
        if self.stop == name:
            raise _Stop()

    def __init__(self, n_main_blocks=8, do_prepass=True, do_l1=True, debug=False, stop=None, pre_blocks=15):
        self.stop = stop
        self.pre_blocks = pre_blocks
        self.ndump = 0
        self.n_main = n_main_blocks
        self.do_prepass = do_prepass
        self.do_l1 = do_l1
        self.debug = debug
        self.nc = bass.Bass("TRN2", target_bir_lowering=False)
        self.sc = Sched()
        self.sb_off = 0
        self._bank = 0

    def alloc(self, name, nbytes, nres=1):
        nbytes = (nbytes + 31) // 32 * 32
        off = self.sb_off
        self.sb_off += nbytes
        return off, [Res("%s%d" % (name, i)) for i in range(nres)]

    def view(self, off, nbytes, dt, res, shape=None):
        a = self.SB[:, off // 4:(off + nbytes) // 4]
        if dt is not F32:
            a = a.bitcast(dt)
        v = V(res, a)
        if shape is not None:
            v = v.re(shape[0], **shape[1])
        return v

    def tile(self, name, dt, free_elems, nsub=1):
        esz = 4 if dt is F32 else 2
        off, res = self.alloc(name, nsub * free_elems * esz, nsub)
        subs = [self.view(off + i * free_elems * esz, free_elems * esz, dt, [res[i]]) for i in range(nsub)]
        whole = self.view(off, nsub * free_elems * esz, dt, res)
        return subs, whole, off

    def bank(self):
        b = self._bank
        self._bank = (self._bank + 1) % 4
        return self.PSB[b]

    def mm(self, out, lhsT, rhs, start=True, stop=True):
        self.sc.add("pe", lambda e, o=out.ap, l=lhsT.ap, r=rhs.ap: e.matmul(o, lhsT=l, rhs=r, start=start, stop=stop),
                    reads=[lhsT, rhs], writes=[out])

    def tr(self, out, in_, ident):
        self.sc.add("pe", lambda e, o=out.ap, i=in_.ap, d=ident.ap: e.transpose(o, i, d),
                    reads=[in_, ident], writes=[out])

    def act(self, out, in_, func, bias=None, scale=None, eng="act"):
        reads = [in_]
        kw = {}
        if bias is not None:
            if isinstance(bias, V):
                reads.append(bias)
                kw["bias"] = bias.ap
            else:
                kw["bias"] = bias
        if scale is not None:
            if isinstance(scale, V):
                reads.append(scale)
                kw["scale"] = scale.ap
            else:
                kw["scale"] = scale
        self.sc.add("act", lambda e, o=out.ap, i=in_.ap: e.activation(o, i, func, **kw), reads=reads, writes=[out])

    def tt(self, eng, out, in0, in1, op):
        self.sc.add(eng, lambda e, o=out.ap, a=in0.ap, b=in1.ap: e.tensor_tensor(o, a, b, op),
                    reads=[in0, in1], writes=[out])

    def ts(self, eng, out, in0, s1, op0, s2=None, op1=None):
        reads = [in0]
        a1 = s1
        if isinstance(s1, V):
            reads.append(s1)
            a1 = s1.ap
        a2 = s2
        if isinstance(s2, V):
            reads.append(s2)
            a2 = s2.ap
        if op1 is None:
            self.sc.add(eng, lambda e, o=out.ap, a=in0.ap: e.tensor_scalar(o, a, a1, None, op0), reads=reads, writes=[out])
        else:
            self.sc.add(eng, lambda e, o=out.ap, a=in0.ap: e.tensor_scalar(o, a, a1, a2, op0, op1), reads=reads, writes=[out])

    def stt(self, eng, out, in0, scalar, in1, op0, op1):
        reads = [in0, in1]
        sa = scalar
        if isinstance(scalar, V):
            reads.append(scalar)
            sa = scalar.ap
        self.sc.add(eng, lambda e, o=out.ap, a=in0.ap, b=in1.ap: e.scalar_tensor_tensor(o, a, sa, b, op0, op1),
                    reads=reads, writes=[out])

    def copy(self, eng, out, in_):
        if eng == "act":
            self.sc.add(eng, lambda e, o=out.ap, i=in_.ap: e.copy(o, i), reads=[in_], writes=[out])
        else:
            self.sc.add(eng, lambda e, o=out.ap, i=in_.ap: e.tensor_copy(o, i), reads=[in_], writes=[out])

    def memset(self, eng, out, val):
        self.sc.add(eng, lambda e, o=out.ap: e.memset(o, val), writes=[out])

    def reduce_sum(self, eng, out, in_):
        self.sc.add(eng, lambda e, o=out.ap, i=in_.ap: e.tensor_reduce(o, i, AX.X, ALU.add), reads=[in_], writes=[out])

    def recip(self, out, in_):
        self.sc.add("dve", lambda e, o=out.ap, i=in_.ap: e.reciprocal(o, i), reads=[in_], writes=[out])

    def dma(self, eng, out, in_, chan, reads=(), writes=(), nowaw=False, slow=False):
        kw = {"allow_slow_non_contiguous": True} if slow else {}
        self.sc.add(eng, lambda e, o=out, i=in_: e.dma_start(out=o, in_=i, **kw), reads=reads, writes=writes,
                    chan=chan, nowaw=nowaw)

    def build(self):
        from contextlib import ExitStack
        nc = self.nc
        dt = nc.dram_tensor

        def din(name, shape):
            return dt(name, shape, F32, kind="ExternalInput").ap()

        x = din("x", [S, D])
        p = din("p", [2, TOK_OWN + 128, PLE])
        norm_mix = din("norm_mix", [2, D])
        norm_mlp = din("norm_mlp", [2, D])
        norm_ple = din("norm_ple", [2, D])
        norm_final = din("norm_final", [D])
        w_in = din("mlstm_w_in", [D, 3088])
        b_gates = din("mlstm_b_gates", [16])
        head_norm = din("mlstm_head_norm", [D])
        w_out = din("mlstm_w_out", [D, D])
        pool_w_in = din("pool_w_in", [D, D])
        pool_w_grp = din("pool_w_grp", [4, 256, 256])
        pool_scale = din("pool_scale", [D])
        pool_w_out = din("pool_w_out", [D, D])
        mlp_w1 = din("mlp_w1", [2, D, DFF])
        mlp_w2 = din("mlp_w2", [2, DFF, D])
        ple_w = din("ple_w", [2, PLE, D])
        ple_gate_w = din("ple_gate_w", [2, D, D])
        ple_gate_b = din("ple_gate_b", [2, D])
        ptab = din("ptab", [16, 128, 128])
        out = dt("out", [TOK_OWN, D], F32, kind="ExternalOutput").ap()
        self.dbg = None
        if self.debug:
            self.dbg = dt("dbg", [16, 128, 4096], F32, kind="ExternalOutput").ap()

        NSLAB = 64
        wscr = dt("wscr", [NSLAB, 128, 4096], BF16, kind="Internal").ap()
        snap = dt("snap", [9, 128, 4 * 257], F32, kind="Internal").ap()
        self.kvscr = dt("kvscr", [33, 128, 512 + 4 * 257], BF16, kind="Internal").ap()

        st = ExitStack()
        self.st = st
        with st:
            SBYTES = 212000
            sbt = st.enter_context(nc.sbuf_tensor("sb", [128, SBYTES // 4], F32))
            self.SB = sbt[:, :]
            pst = st.enter_context(nc.psum_tensor("ps", [128, 8, 512], F32))
            self.PS = pst
            self.PSB = [V([Res("bank%d" % i, excl=True)], pst[:, i, :]) for i in range(8)]
            try:
                self._build_body(x, p, norm_mix, norm_mlp, norm_ple, norm_final, w_in, b_gates, head_norm, w_out,
                                 pool_w_in, pool_w_grp, pool_scale, pool_w_out, mlp_w1, mlp_w2, ple_w, ple_gate_w,
                                 ple_gate_b, ptab, out, wscr, snap)
            except _Stop:
                pass
            assert self.sb_off <= SBYTES, self.sb_off
            self.sc.emit(nc, st)
        return nc

    def _build_body(self, x, p, norm_mix, norm_mlp, norm_ple, norm_final, w_in, b_gates, head_norm, w_out,
                    pool_w_in, pool_w_grp, pool_scale, pool_w_out, mlp_w1, mlp_w2, ple_w, ple_gate_w,
                    ple_gate_b, ptab, out, wscr, snap):
        B = self
        sc = self.sc
        PSB = self.PSB

        slabs = []

        def slab_cols(key, W, c0, w):
            K = W.shape[0]
            kc = K // 128
            pieces = []
            if kc > 8:
                for k0 in range(0, kc, 8):
                    pieces.append((k0 * w, 8, w, W[k0 * 128:(k0 + 8) * 128, c0:c0 + w].rearrange("(kc p) w -> p kc w", p=128)))
            else:
                pieces.append((0, kc, w, W[:, c0:c0 + w].rearrange("(kc p) w -> p kc w", p=128)))
            slabs.append((key, pieces, kc * w, kc, w))

        slab_cols("in_k", w_in, 512, 512)
        slab_cols("in_v0", w_in, 1024, 512)
        slab_cols("in_v1", w_in, 1536, 512)
        slab_cols("in_g", w_in, 3072, 16)
        slab_cols("in_q", w_in, 0, 512)
        slab_cols("in_o0", w_in, 2048, 512)
        slab_cols("in_o1", w_in, 2560, 512)
        slab_cols("out0", w_out, 0, 512)
        slab_cols("out1", w_out, 512, 512)
        for l in range(2):
            for j in range(8):
                slab_cols("w1_%d_%d" % (l, j), mlp_w1[l], j * 512, 512)
            for j in range(8):
                slab_cols("w2_%d_%d" % (l, j), mlp_w2[l], j * 128, 128)
            slab_cols("pg_%d_0" % l, ple_gate_w[l], 0, 512)
            slab_cols("pg_%d_1" % l, ple_gate_w[l], 512, 512)
            slab_cols("pw_%d" % l, ple_w[l], 0, 1024)
        slab_cols("pi0", pool_w_in, 0, 512)
        slab_cols("pi1", pool_w_in, 512, 512)
        slabs.append(("grp", [(0, 8, 256, pool_w_grp.rearrange("g (kc p) w -> p (g kc) w", p=128))], 2048, 8, 256))
        slab_cols("po0", pool_w_out, 0, 512)
        slab_cols("po1", pool_w_out, 512, 512)
        slab_idx = {s[0]: i for i, s in enumerate(slabs)}
        assert len(slabs) <= 64
        slab_res = [Res("slab_" + s[0]) for s in slabs]
        NGRP = 8
        grp_res = [Res("pgrp%d" % i) for i in range(NGRP)]

        def grp_of(i):
            if i < 4:
                return 0
            return 1 + min(NGRP - 2, (i - 4) * (NGRP - 1) // (len(slabs) - 4))

        def emit_prologue(i0=0, i1=None, pace=None):
            for i, (key, pieces, nel, kc, w) in enumerate(slabs):
                if i < i0 or (i1 is not None and i >= i1):
                    continue
                g = grp_res[grp_of(i)]
                for (eo, kcn, ww, src) in pieces:
                    dst = wscr[i, :, eo:eo + kcn * ww].rearrange("p (kc w) -> p kc w", w=ww)
                    B.dma("pool", dst, src, g, reads=([pace] if pace is not None else []), writes=[V([g], None)], nowaw=True)

        c_id32, _, _ = B.tile("id32", F32, 128)
        c_idb, _, _ = B.tile("idb", BF16, 128)
        c_onesb, _, _ = B.tile("onesb", BF16, 128)
        c_ones32, _, _ = B.tile("ones32", F32, 128)
        c_triF, _, _ = B.tile("triF", F32, 128)
        c_triB, _, _ = B.tile("triB", F32, 128)
        c_mF, _, _ = B.tile("mF", BF16, 128)
        c_mB, _, _ = B.tile("mB", BF16, 128)
        c_pt, c_pt_all, _ = B.tile("ptab", BF16, 128, nsub=16)
        id32, idb, onesb, ones32, triF, triB, mF, mB = (c_id32[0], c_idb[0], c_onesb[0], c_ones32[0], c_triF[0],
                                                       c_triB[0], c_mF[0], c_mB[0])
        _, vecs, _ = B.tile("vecs", F32, 8 * 11)
        vecs3 = vecs.re("p (n c) -> p n c", c=8)
        _, vraw, _ = B.tile("vraw", F32, 8 * 11)
        vraw3 = vraw.re("p (n c) -> p n c", c=8)
        _, bg, _ = B.tile("bg", F32, 16)
        _, hnb, _ = B.tile("hnb", BF16, 1024)
        _, ones4, _ = B.tile("ones4", F32, 4)
        _, mhalf, _ = B.tile("mhalf", F32, 1)
        _, rcol, _ = B.tile("rcol", F32, 4)
        _, rcolH, _ = B.tile("rcolH", F32, 4)
        _, rcolP, _ = B.tile("rcolP", F32, 4)
        pace_t = [B.tile("pace%d" % i, F32, 1)[1] for i in range(16)]

        hbuf = [B.tile("hA", F32, 512, nsub=8), B.tile("hB", F32, 512, nsub=8)]
        xs = [B.tile("xs%d" % i, F32, 1024)[0][0] for i in range(2)]
        pstg = [B.tile("pstg%d" % i, F32, 256)[0][0] for i in range(2)]
        pT = [B.tile("pT%d" % i, BF16, 512, nsub=2)[0] for i in range(2)]
        xnT, xnT_all, _ = B.tile("xnT", BF16, 512, nsub=8)
        sq, sq_all, _ = B.tile("sq", BF16, 512, nsub=8)
        _, rb, _ = B.tile("rb", F32, 512)
        xnH, _, _ = B.tile("xnH", BF16, 128, nsub=8)
        _, ostage_full, _ = B.tile("ostage", F32, 1028)
        ostage = ostage_full[:, 0:1024]
        ost_res = [Res("ostA"), Res("ostB")]
        ost_half = [V([ost_res[0]], ostage_full.ap[:, 0:512]), V([ost_res[1]], ostage_full.ap[:, 512:1024])]
        E32 = [B.tile("E32_%d" % d, F32, 4 * 257)[1] for d in range(2)]
        Cbf = [B.tile("Cbf_%d" % d, BF16, 4 * 257)[1] for d in range(2)]
        _, dprevF, _ = B.tile("dprevF", F32, 4)
        snapstg = V(ost_res, ostage_full.ap)
        ring = [B.tile("ring%d" % i, BF16, 4096)[1] for i in range(4)]
        arena_off = self.sb_off
        GR = 1024
        ARENA = 212000 - arena_off
        ARENA = ARENA // GR * GR
        ngr = ARENA // GR
        ares = [Res("ar%d" % i) for i in range(ngr)]
        self.sb_off += ARENA

        class Arena:
            def __init__(s):
                s.off = 0

            def reset(s, o=0):
                s.off = o

            def at(s, o, nbytes, dtp, shape=None):
                g0 = o // GR
                g1 = (o + nbytes - 1) // GR
                v = B.view(arena_off + o, nbytes, dtp, ares[g0:g1 + 1])
                if shape is not None:
                    v = v.re(shape[0], **shape[1])
                return v

            def get(s, nbytes, dtp, shape=None):
                nb = (nbytes + 31) // 32 * 32
                o = s.off
                s.off += nb
                assert s.off <= ARENA, (s.off, ARENA)
                g0 = o // GR
                g1 = (o + nb - 1) // GR
                v = B.view(arena_off + o, nbytes, dtp, ares[g0:g1 + 1])
                if shape is not None:
                    v = v.re(shape[0], **shape[1])
                return v

        ar = Arena()
        qT = [ar.get(512 * 2, BF16) for _ in range(4)]
        kT = [ar.get(512 * 2, BF16) for _ in range(4)]
        ktok = [ar.get(512 * 2, BF16) for _ in range(4)]
        vext = [ar.get(4 * 257 * 2, BF16, ("p (h c) -> p h c", dict(c=257))) for _ in range(4)]
        og = [ar.get(1024 * 2, BF16) for _ in range(4)]
        sTm_off = ar.off
        sTm = [[ar.get(512 * 2, BF16) for _ in range(4)] for _ in range(2)]
        vp = [[ar.get(4 * 257 * 2, BF16, ("p (h c) -> p h c", dict(c=257))) for _ in range(2)] for _ in range(2)]
        hs = [ar.get(1024 * 4, F32) for _ in range(4)]
        tmpA = ar.get(1024 * 4, F32)
        hg = [ar.get(1024 * 2, BF16) for _ in range(2)]
        hg_extra = hg
        hgT_off = ar.off
        hgT = [ar.get(512 * 2, BF16) for _ in range(8)]
        hgT_all3 = ar.at(hgT_off, 8 * 1024, BF16, ("p (c t) -> p c t", dict(t=512)))
        gall = ar.get(4 * 16 * 4, F32, ("p (c g) -> p c g", dict(g=16)))
        sm = {}
        for nm in ("ef", "sp", "warg", "w", "thr", "dec"):
            sm[nm] = ar.get(32 * 4, F32, ("p (c d h) -> p c d h", dict(d=2, h=4)))
        rr = [ar.get(4 * 4, F32) for _ in range(2)]
        ssq4 = ar.get(16 * 4, F32, ("p (c h) -> p c h", dict(h=4)))
        rn4 = ar.get(16 * 4, F32, ("p (c h) -> p c h", dict(h=4)))
        mixer_end = ar.off
        ar.reset(0)
        h1T = [ar.get(512 * 2, BF16) for _ in range(32)]
        rtmp = [ar.get(512 * 4, F32) for _ in range(2)]
        gt = [ar.get(512 * 2, BF16) for _ in range(8)]
        mlp_end = ar.off
        ar.reset(0)
        pre_k = ar.get(4096 * 2, BF16)
        pre_v0 = ar.get(4096 * 2, BF16)
        pre_v1 = ar.get(4096 * 2, BF16)
        pre_g = ar.get(128 * 2, BF16)
        pp_ktok2 = [[ar.get(512 * 2, BF16) for _ in range(4)] for _ in range(2)]
        pp_vext2 = [[ar.get(4 * 257 * 2, BF16, ("p (h c) -> p h c", dict(c=257))) for _ in range(4)] for _ in range(2)]
        pp_vp = [ar.get(4 * 257 * 2, BF16, ("p (h c) -> p h c", dict(c=257))) for _ in range(2)]
        pp_gall2 = [ar.get(4 * 16 * 4, F32, ("p (c g) -> p c g", dict(g=16))) for _ in range(2)]
        pp_sm2 = []
        for _ in range(2):
            d_ = {}
            for nm in ("ef", "sp", "warg", "w", "thr", "dec"):
                d_[nm] = ar.get(32 * 4, F32, ("p (c d h) -> p c d h", dict(d=2, h=4)))
            pp_sm2.append(d_)
        pp_xn2 = [ar.get(512 * 2, BF16) for _ in range(8)]
        ar.reset(0)
        u = [ar.get(1024 * 2, BF16) for _ in range(6)]
        yT = [ar.get(512 * 2, BF16) for _ in range(8)]
        zT = [ar.get(512 * 2, BF16) for _ in range(8)]
        ar.reset(max(mixer_end, mlp_end, ar.off))
        uprev = ar.get(1024 * 2, BF16)

        ps7 = self.PS[:, 7, :]
        ps6 = self.PS[:, 6, :]
        r7 = PSB[7].res
        r6 = PSB[6].res
        p_cum = V(r7, ps7[:, 0:32].rearrange("p (d c h) -> p d c h", d=2, h=4))
        p_tot = V(r7, ps7[:, 32:64].rearrange("p (d c h) -> p d c h", d=2, h=4))
        p_den = [V(r7, ps7[:, 64 + 4 * d:68 + 4 * d]) for d in range(2)]
        p_dn = [V(r7, ps7[:, 72 + 4 * d:76 + 4 * d]) for d in range(2)]
        PS_ST = PSB[6]
        PS_O = [V(PSB[0].res + PSB[1].res, self.PS[:, 0:2, :]), V(PSB[2].res + PSB[3].res, self.PS[:, 2:4, :])]
        PS_DC = [PSB[4], PSB[5]]

        B.memset("pool", ones32, 1.0)
        B.memset("pool", onesb, 1.0)
        B.memset("pool", ones4, 1.0)
        B.memset("pool", mhalf, -0.5)
        B.memset("pool", id32, 1.0)
        sc.add("pool", lambda e, a=id32.ap: e.affine_select(out=a, in_=a, pattern=[[-1, 128]], compare_op=ALU.is_equal,
                                                          fill=0.0, base=0, channel_multiplier=1),
               reads=[id32], writes=[id32])
        B.copy("pool", idb, id32)
        B.memset("pool", triF, 1.0)
        sc.add("pool", lambda e, a=triF.ap: e.affine_select(out=a, in_=a, pattern=[[1, 128]], compare_op=ALU.is_ge,
                                                          fill=0.0, base=0, channel_multiplier=-1),
               reads=[triF], writes=[triF])
        B.memset("pool", triB, 1.0)
        sc.add("pool", lambda e, a=triB.ap: e.affine_select(out=a, in_=a, pattern=[[-1, 128]], compare_op=ALU.is_ge,
                                                          fill=0.0, base=0, channel_multiplier=1),
               reads=[triB], writes=[triB])
        B.copy("pool", mF, triF)
        B.copy("pool", mB, triB)
        B.dma("pool", c_pt_all.re("p (n c) -> p n c", c=128).ap, ptab.rearrange("n p c -> p n c"), c_pt_all.res[0], writes=[c_pt_all])
        vec_srcs = [norm_mix[0], norm_mix[1], norm_mlp[0], norm_mlp[1], norm_ple[0], norm_ple[1], norm_final,
                    head_norm, pool_scale, ple_gate_b[0], ple_gate_b[1]]
        for i, vsrc in enumerate(vec_srcs):
            B.dma("sp", vraw3.ap[:, i, :], vsrc.rearrange("(c p) -> p c", p=128), vraw.res[0], writes=[vraw], nowaw=True, slow=True)
        B.dma("sp", bg.ap, b_gates.partition_broadcast(128), bg.res[0], writes=[bg], slow=True)
        B.dma("pool", hnb.ap, head_norm.partition_broadcast(128), hnb.res[0], writes=[hnb], slow=True)
        B.ts("dve", vecs3[:, 0:7, :], vraw3[:, 0:7, :], 1.0, ALU.mult)
        B.ts("dve", vecs3[:, 7:8, :], vraw3[:, 7:8, :], 1.0, ALU.mult)
        B.ts("dve", vecs3[:, 8:11, :], vraw3[:, 8:11, :], 1.0, ALU.mult)
        for d in range(2):
            B.memset("pool", E32[d], 0.0)
            B.memset("pool", Cbf[d], 0.0)
        B.copy("pool", dprevF, ones4)
        if self.do_prepass and self.pre_blocks == 15:
            emit_prologue(0, 4)
        else:
            emit_prologue()
        if self.stop == "init":
            B.dump(id32, 128)
            B.dump(triF, 128)
            B.dump(triB, 128)
            B.dump(mF, 128, bf16=True)
            B.dump(vecs, 88)
            B.dump(bg, 16)
            B.dump(c_pt_all, 2048, bf16=True)

        ring_state = {"n": 0}

        def use_slab(key):
            i = slab_idx[key]
            nel = slabs[i][2]
            slot = ring[ring_state["n"] % 4]
            ring_state["n"] += 1
            g = grp_res[grp_of(i)]
            B.dma("sp", slot.ap[:, 0:nel], wscr[i, :, 0:nel], slot.res[0], reads=[V([g], None)], writes=[slot])
            kc, w = slabs[i][3], slabs[i][4]
            return slot[:, 0:nel].re("p (kc w) -> p kc w", w=w)

        def load_resident(key, dstv):
            i = slab_idx[key]
            nel = slabs[i][2]
            g = grp_res[grp_of(i)]
            B.dma("sp", dstv.ap[:, 0:nel], wscr[i, :, 0:nel], dstv.res[0], reads=[V([g], None)], writes=[dstv])
            kc, w = slabs[i][3], slabs[i][4]
            return dstv[:, 0:nel].re("p (kc w) -> p kc w", w=w)

        if self.stop == "init":
            for key in ("in_k", "w2_0_3", "grp", "pw_1"):
                sl = use_slab(key)
                i = slab_idx[key]
                B.dump(V(sl.res, ring[(ring_state["n"] - 1) % 4].ap), 4096, bf16=True)
            raise _Stop()
        xseq = []
        if self.do_prepass:
            for pb_ in range(self.pre_blocks, 0, -1):
                xseq += [pb_ * 512 + c_ * 128 for c_ in range(4)]
        for blk_ in range(self.n_main + 1):
            xseq += [blk_ * 512 + c_ * 128 for c_ in range(4 if blk_ < self.n_main else 1)]
        xq = {"issued": 0, "used": 0}

        def x_issue():
            i = xq["issued"]
            if i >= len(xseq):
                return
            xb = xs[i % 2]
            t0_ = xseq[i]
            B.dma("sp", xb.ap, x[t0_: t0_ + 128, :], xb.res[0], writes=[xb])
            xq["issued"] = i + 1

        def load_block_T(hT, tok0, nch, hwhole):
            hw3 = hwhole.re("p (c t) -> p c t", t=512)
            for ch in range(nch):
                i = xq["used"]
                assert xseq[i] == tok0 + ch * 128, (xseq[i], tok0, ch)
                while xq["issued"] < i + 1:
                    x_issue()
                xb = xs[i % 2]
                xq["used"] = i + 1
                for half in range(2):
                    bk = B.bank()
                    for c4 in range(4):
                        c = half * 4 + c4
                        B.tr(bk[:, c4 * 128:(c4 + 1) * 128], xb[:, c * 128:(c + 1) * 128], id32)
                    eng = "act" if half == 0 else "dve"
                    dstv = V(sum([hT[half * 4 + c4].res for c4 in range(4)], []),
                             hw3.ap[:, half * 4:half * 4 + 4, ch * 128:(ch + 1) * 128])
                    B.copy(eng, dstv, bk.re("p (c t) -> p c t", t=128))
                while xq["issued"] < min(len(xseq), i + 3):
                    x_issue()

        def norm_piece(hT, gi, T, dst, c):
            B.ts("dve", dst[c][:, 0:T], hT[c][:, 0:T], vecs3[:, gi, c:c + 1], ALU.mult)
            B.act(sq[c][:, 0:T], hT[c][:, 0:T], AF.Square)

        def norm(hT, gi, T, dst, want_rb=True, rc=None, pieces_done=False, defer=False):
            nch = T // 128
            if not pieces_done:
                for c in range(8):
                    norm_piece(hT, gi, T, dst, c)

            def stats():
                if want_rb:
                    bk = B.bank()
                    for c in range(8):
                        B.mm(bk[:, 0:T], onesb, sq[c][:, 0:T], start=(c == 0), stop=(c == 7))
                    B.act(rb[:, 0:T], bk[:, 0:T], AF.Sqrt, bias=EPS, scale=1.0 / 1024.0)
                    B.recip(rb[:, 0:T], rb[:, 0:T])
                if rc is not None:
                    bk = B.bank()
                    for ch in range(nch):
                        for c in range(8):
                            B.mm(bk[:, ch:ch + 1], sq[c][:, ch * 128:(ch + 1) * 128], onesb[:, 0:1], start=(c == 0), stop=(c == 7))
                    B.act(rc[:, 0:nch], bk[:, 0:nch], AF.Identity, bias=EPS, scale=1.0 / 1024.0)
                    B.tt("pool", rc[:, 0:nch], rc[:, 0:nch], V(mhalf.res, mhalf.ap.to_broadcast([128, nch])), ALU.pow)
            if defer:
                return stats
            stats()

        def linA(slabv, m0, nm, rhs_list, T, evac):
            kcn = len(rhs_list)
            for m in range(nm):
                bk = B.bank()
                for kc in range(kcn):
                    B.mm(bk[:, 0:T], slabv[:, kc, m * 128:(m + 1) * 128], rhs_list[kc][:, 0:T],
                         start=(kc == 0), stop=(kc == kcn - 1))
                evac(m0 + m, bk[:, 0:T])

        def linB(slabv, lhs_list, ch, ncols, evac, bk=None):
            kcn = len(lhs_list)
            if bk is None:
                bk = B.bank()
            for kc in range(kcn):
                B.mm(bk[:, 0:ncols], lhs_list[kc][:, ch * 128:(ch + 1) * 128], slabv[:, kc, 0:ncols],
                     start=(kc == 0), stop=(kc == kcn - 1))
            evac(bk[:, 0:ncols])

        def gate_prep(gallv, smd, nch):
            g5 = gallv.re("p c (d k h) -> p c d k h", d=2, k=2)
            ipre = g5[:, 0:nch, :, 0, :]
            fpre = g5[:, 0:nch, :, 1, :]
            ef, sp_, warg, w_, thr, dec = (smd[k][:, 0:nch] for k in ("ef", "sp", "warg", "w", "thr", "dec"))
            B.act(ef, fpre, AF.Exp, scale=-1.0)
            B.act(sp_, ef, AF.Ln, bias=1.0)
            cum = V(p_cum.res, p_cum.ap[:, :, 0:nch, :])
            tot = V(p_tot.res, p_tot.ap[:, :, 0:nch, :])
            for d in range(2):
                B.mm(cum[:, d], triF if d == 0 else triB, sp_[:, :, d, :])
                B.mm(tot[:, d], ones32, sp_[:, :, d, :])
            cum_cdh = V(cum.res, cum.ap.rearrange("p d c h -> p c d h"))
            tot_cdh = V(tot.res, tot.ap.rearrange("p d c h -> p c d h"))
            B.tt("dve", warg, ipre, cum_cdh, ALU.add)
            B.act(w_, warg, AF.Exp, bias=LN_KSCALE)
            B.act(thr, cum_cdh, AF.Exp)
            B.act(dec, tot_cdh, AF.Exp, scale=-1.0)

        def state_mm(d, kt, vpv):
            for hp in range(2):
                bk = PS_DC[hp]
                for h2 in range(2):
                    h = hp * 2 + h2
                    B.mm(bk[:, h2 * 256:(h2 + 1) * 256], kt[:, h * 128:(h + 1) * 128], vpv[:, h, 0:256])
                    B.mm(p_dn[d][:, h:h + 1], kt[:, h * 128:(h + 1) * 128], vpv[:, h, 256:257])

        def state_dve(d, dprev):
            E3 = E32[d].re("p (h c) -> p h c", c=257)
            for hp in range(2):
                bk = PS_DC[hp]
                for h2 in range(2):
                    h = hp * 2 + h2
                    B.stt("dve", E3[:, h, 0:256], E3[:, h, 0:256], dprev[:, h:h + 1], bk[:, h2 * 256:(h2 + 1) * 256],
                          ALU.mult, ALU.add)
            En = E3[:, :, 256]
            B.tt("dve", En, En, dprev, ALU.mult)
            B.tt("dve", En, En, p_dn[d], ALU.add)

        def state_update(d, kt, vpv, dprev):
            state_mm(d, kt, vpv)
            state_dve(d, dprev)

        snapd = [Res("snapd%d" % i) for i in range(9)]
        kvscr = self.kvscr
        kvdk = [Res("kvdk%d" % i) for i in range(33)]
        kvdv = [Res("kvdv%d" % i) for i in range(33)]
        kvstk_ch = [[Res("kvstk%d_%d" % (a_, b_)) for b_ in range(4)] for a_ in range(2)]
        kvstv_ch = [[Res("kvstv%d_%d" % (a_, b_)) for b_ in range(4)] for a_ in range(2)]
        kvldk_ch = [Res("kvldk%d" % b_) for b_ in range(4)]
        kvldv_ch = [Res("kvldv%d" % b_) for b_ in range(4)]
        use_kv_cache = self.do_prepass
        if self.do_prepass:
            wk = load_resident("in_k", pre_k)
            wv0 = load_resident("in_v0", pre_v0)
            wv1 = load_resident("in_v1", pre_v1)
            wg = load_resident("in_g", pre_g)
            for par in range(2):
                for i in range(4):
                    B.memset("pool", pp_vext2[par][i][:, :, 256:257], 1.0)
            sc.tag = "pre"
            pxn = [xnT, pp_xn2]
            prc = [rcol, rcolP]
            pst = {"dprev": ones4}

            def stageA(pb, defer=False):
                par = pb % 2
                load_block_T(hbuf[par][0], pb * 512, 4, hbuf[par][1])
                return norm(hbuf[par][0], 0, 512, pxn[par], want_rb=False, rc=prc[par], defer=defer)

            def stageB_chunk(pb, ch):
                par = pb % 2
                xn_ = pxn[par]
                rs = prc[par][:, ch:ch + 1]
                kt_, ve_, ga_ = pp_ktok2[par], pp_vext2[par], pp_gall2[par]
                linB(wk, xn_, ch, 512, lambda b_: B.act(kt_[ch], b_, AF.Copy, scale=rs))
                linB(wv0, xn_, ch, 512, lambda b_: B.ts("dve", ve_[ch][:, 0:2, 0:256], b_.re("p (h c) -> p h c", c=256), rs, ALU.mult))
                linB(wv1, xn_, ch, 512, lambda b_: B.act(ve_[ch][:, 2:4, 0:256], b_.re("p (h c) -> p h c", c=256), AF.Copy, scale=rs))
                bk = B.bank()
                for kc in range(8):
                    B.mm(bk[:, 0:16], xn_[kc][:, ch * 128:(ch + 1) * 128], wg[:, kc, :], start=(kc == 0), stop=(kc == 7))
                B.stt("dve", ga_[:, ch, :], bk[:, 0:16], rs, bg, ALU.mult, ALU.add)
                cg = pb * 4 + ch
                if cg <= 4 * self.n_main:
                    B.dma("sp", kvscr[cg, :, 0:512], kt_[ch].ap, kvstk_ch[par][ch], reads=[kt_[ch]], writes=[V([kvdk[cg]], None)])
                    B.dma("sp", kvscr[cg, :, 512:1540], ve_[ch].ap.rearrange("p h c -> p (h c)"), kvstv_ch[par][ch],
                          reads=[ve_[ch]], writes=[V([kvdv[cg]], None)])

            def stageC_vp(pb, ch):
                par = pb % 2
                vpv = pp_vp[ch % 2]
                wv_ = pp_sm2[par]["w"][:, ch, 1, :]
                B.tt("dve", vpv, pp_vext2[par][ch], V(wv_.res, wv_.ap.unsqueeze(2).to_broadcast([128, 4, 257])), ALU.mult)

            def stageC_chunk(pb, ch):
                par = pb % 2
                cg = pb * 4 + ch
                vpv = pp_vp[ch % 2]
                state_update(1, pp_ktok2[par][ch], vpv, pst["dprev"])
                pst["dprev"] = pp_sm2[par]["dec"][:, ch, 1, :]
                si = None
                if cg == 4 * self.n_main + 1:
                    si = 8
                elif cg % 4 == 0 and cg <= 4 * self.n_main:
                    si = cg // 4 - 1
                if si is not None:
                    E3 = E32[1].re("p (h c) -> p h c", c=257)
                    S3 = snapstg.re("p (h c) -> p h c", c=257)
                    dpv = pst["dprev"]
                    for h in range(4):
                        B.act(S3[:, h, :], E3[:, h, :], AF.Copy, scale=dpv[:, h:h + 1])
                    B.dma("sp", snap[si], snapstg.ap, snapstg.res[0], reads=[snapstg], writes=[V([snapd[si]], None)])

            PB = self.pre_blocks
            stageA(PB)
            if PB - 1 >= 1:
                stageA(PB - 1)
            for ch in range(4):
                stageB_chunk(PB, ch)
            gate_prep(pp_gall2[PB % 2], pp_sm2[PB % 2], 4)
            stageC_vp(PB, 3)
            nsl = len(slabs)
            for pb in range(PB - 1, 0, -1):
                fin = None
                if pb - 1 >= 1:
                    fin = stageA(pb - 1, defer=True)
                if PB == 15:
                    k = PB - 1 - pb
                    pc = pace_t[k]
                    B.memset("dve", pc, 0.0)
                    i0_ = 4 + 4 * k
                    i1_ = nsl if pb == 1 else min(nsl, 4 + 4 * (k + 1))
                    emit_prologue(i0_, i1_, pace=pc)
                for i in range(4):
                    stageC_chunk(pb + 1, 3 - i)
                    if i < 3:
                        stageC_vp(pb + 1, 2 - i)
                    stageB_chunk(pb, i)
                    if i == 0 and fin is not None:
                        fin()
                gate_prep(pp_gall2[pb % 2], pp_sm2[pb % 2], 4)
                stageC_vp(pb, 3)
            for i in range(4):
                stageC_chunk(1, 3 - i)
                if i < 3:
                    stageC_vp(1, 2 - i)
            if self.stop == "prepass":
                dpv = pst["dprev"]
                E3 = E32[1].re("p (h c) -> p h c", c=257)
                B.tt("dve", E3, E3, V(dpv.res, dpv.ap.unsqueeze(2).to_broadcast([128, 4, 257])), ALU.mult)
                B.dump(E32[1], 1028)
                raise _Stop()

        def mlp_and_ple(hT, layer, T, tok0, nch, pTl, pstage_tok0):
            sc.tag = "L%d.mlp" % layer
            norm(hT, 2 + layer, T, xnT, pieces_done=True)
            def p_load(ch):
                pb_ = pstg[ch % 2]
                B.dma("pool", pb_.ap, p[layer, pstage_tok0 + ch * 128: pstage_tok0 + (ch + 1) * 128, :], pb_.res[0], writes=[pb_])

            def p_transpose(ch):
                pb_ = pstg[ch % 2]
                bk = B.bank()
                for c2 in range(2):
                    B.tr(bk[:, c2 * 128:(c2 + 1) * 128], pb_[:, c2 * 128:(c2 + 1) * 128], id32)
                for c2 in range(2):
                    B.copy("act", pTl[c2][:, ch * 128:(ch + 1) * 128], bk[:, c2 * 128:(c2 + 1) * 128])
            for ch in range(min(2, nch)):
                p_load(ch)
            B.check_stop("l0i2")
            for j in range(8):
                if j == 1:
                    B.check_stop("l0i3")
                if j == 4:
                    B.check_stop("l0i4")
                if j >= 5:
                    B.check_stop("l0i%d" % j)
                sl = use_slab("w1_%d_%d" % (layer, j))

                def ev(m, b_):
                    r_ = rtmp[m % 2]
                    B.stt("dve", r_[:, 0:T], b_, 0.0, rb[:, 0:T], ALU.max, ALU.mult)
                    B.act(h1T[m][:, 0:T], r_[:, 0:T], AF.Square)
                linA(sl, j * 4, 4, xnT, T, ev)
                if j == 1:
                    for ch in range(min(2, nch)):
                        p_transpose(ch)
                        if ch + 2 < nch:
                            p_load(ch + 2)
                if j == 3:
                    for ch in range(2, nch):
                        p_transpose(ch)
            B.check_stop("l0j")
            for j in range(8):
                sl = use_slab("w2_%d_%d" % (layer, j))
                def evw2(m, b_):
                    B.tt("dve", hT[m][:, 0:T], b_, hT[m][:, 0:T], ALU.add)
                    norm_piece(hT, 4 + layer, T, xnT, m)
                linA(sl, j, 1, h1T, T, evw2)
            B.check_stop("l0k")
            if self.stop == "l0b0" and layer == 0:
                B.dump(V(sum([h_.res for h_ in hT], []), hbuf[0][1].ap), 4096)
            sc.tag = "L%d.ple" % layer
            norm(hT, 4 + layer, T, xnT, pieces_done=True)
            B.check_stop("l0l")
            def evg(m, b_):
                t_ = rtmp[m % 2]
                B.tt("dve", t_[:, 0:T], b_, rb[:, 0:T], ALU.mult)
                B.act(gt[m][:, 0:T], t_[:, 0:T], AF.Sigmoid, bias=vecs3[:, 9 + layer, m:m + 1])

            def ev2(m, b_):
                t_ = rtmp[m % 2]
                B.tt("dve", t_[:, 0:T], b_, gt[m][:, 0:T], ALU.mult)
                B.tt("dve", hT[m][:, 0:T], t_[:, 0:T], hT[m][:, 0:T], ALU.add)
                if layer == 1:
                    B.act(sq[m][:, 0:T], hT[m][:, 0:T], AF.Square)
                    B.ts("dve", hT[m][:, 0:T], hT[m][:, 0:T], vecs3[:, 6, m:m + 1], ALU.mult)
            slg = [use_slab("pg_%d_0" % layer), None]
            slw = use_slab("pw_%d" % layer)
            for m in range(8):
                j = m // 4
                if slg[j] is None:
                    slg[j] = use_slab("pg_%d_%d" % (layer, j))
                bk = B.bank()
                for kc in range(8):
                    B.mm(bk[:, 0:T], slg[j][:, kc, (m % 4) * 128:(m % 4 + 1) * 128], xnT[kc][:, 0:T], start=(kc == 0), stop=(kc == 7))
                evg(m, bk[:, 0:T])
                bk2 = B.bank()
                for kc in range(2):
                    B.mm(bk2[:, 0:T], slw[:, kc, m * 128:(m + 1) * 128], pTl[kc][:, 0:T], start=(kc == 0), stop=(kc == 1))
                ev2(m, bk2[:, 0:T])
            B.check_stop("l0n")

        def layer0(hT, blk, nch, snap_i, first, hwhole):
            sc.tag = "L0.load"
            T = 128 * nch
            tok0 = blk * 512
            load_block_T(hT, tok0, nch, hwhole)
            if self.do_prepass:
                B.dma("sp", E32[1].ap, snap[snap_i], E32[1].res[0], reads=[V([snapd[snap_i]], None)], writes=[E32[1]])
            B.check_stop("l0a")
            norm(hT, 0, T, xnT, want_rb=True, rc=rcol)
            sc.tag = "L0.proj"
            B.check_stop("l0b")
            sl = use_slab("in_q")
            linA(sl, 0, 4, xnT, T, lambda m, b_: B.tt("dve", qT[m][:, 0:T], b_, rb[:, 0:T], ALU.mult))
            sl = use_slab("in_k")
            linA(sl, 0, 4, xnT, T, lambda m, b_: B.tt("dve", kT[m][:, 0:T], b_, rb[:, 0:T], ALU.mult))
            if use_kv_cache and blk >= 1:
                for ch in range(nch):
                    cg = blk * 4 + ch
                    B.dma("sp", ktok[ch].ap, kvscr[cg, :, 0:512], kvldk_ch[ch], reads=[V([kvdk[cg]], None)], writes=[ktok[ch]])
                    B.dma("sp", vext[ch].ap.rearrange("p h c -> p (h c)"), kvscr[cg, :, 512:1540], kvldv_ch[ch],
                          reads=[V([kvdv[cg]], None)], writes=[vext[ch]])
            else:
                for ch in range(nch):
                    linB(sl, xnT, ch, 512, lambda b_, ch=ch: B.act(ktok[ch], b_, AF.Copy, scale=rcol[:, ch:ch + 1]))
                sl = use_slab("in_v0")
                for ch in range(nch):
                    linB(sl, xnT, ch, 512, lambda b_, ch=ch: B.ts("dve", vext[ch][:, 0:2, 0:256], b_.re("p (h c) -> p h c", c=256), rcol[:, ch:ch + 1], ALU.mult))
                sl = use_slab("in_v1")
                for ch in range(nch):
                    linB(sl, xnT, ch, 512, lambda b_, ch=ch: B.act(vext[ch][:, 2:4, 0:256], b_.re("p (h c) -> p h c", c=256), AF.Copy, scale=rcol[:, ch:ch + 1]))
                for ch in range(nch):
                    B.memset("dve", vext[ch][:, :, 256:257], 1.0)
            B.check_stop("l0c")
            sl = use_slab("in_g")
            for ch in range(nch):
                bk = B.bank()
                for kc in range(8):
                    B.mm(bk[:, 0:16], xnT[kc][:, ch * 128:(ch + 1) * 128], sl[:, kc, :], start=(kc == 0), stop=(kc == 7))
                B.stt("dve", gall[:, ch, :], bk[:, 0:16], rcol[:, ch:ch + 1], bg, ALU.mult, ALU.add)
            B.check_stop("l0d")
            gate_prep(gall, sm, nch)
            B.check_stop("l0e")
            sc.tag = "L0.scores"
            for ch in range(nch):
                stb = PS_ST if ch % 2 == 0 else PSB[5]
                for h in range(4):
                    B.mm(stb[:, h * 128:(h + 1) * 128], kT[h][:, ch * 128:(ch + 1) * 128], qT[h][:, ch * 128:(ch + 1) * 128])
                st3 = stb.re("p (h t) -> p h t", t=128)
                B.tt("dve", sTm[0][ch].re("p (h t) -> p h t", t=128), st3,
                     V(mF.res, mF.ap.unsqueeze(1).to_broadcast([128, 4, 128])), ALU.mult)
                B.tt("dve", sTm[1][ch].re("p (h t) -> p h t", t=128), st3,
                     V(mB.res, mB.ap.unsqueeze(1).to_broadcast([128, 4, 128])), ALU.mult)
            B.check_stop("l0f")
            sc.tag = "L0.scan"
            dprev = [dprevF, ones4]
            slots = [(s_, d) for s_ in range(nch) for d in range(2)]

            def ch_of(s_, d):
                return s_ if d == 0 else nch - 1 - s_

            def emit_vp(s_, d):
                ch_ = ch_of(s_, d)
                vpv_ = vp[d][s_ % 2]
                wv_ = sm["w"][:, ch_, d, :]
                B.tt("pool", vpv_, vext[ch_], V(wv_.res, wv_.ap.unsqueeze(2).to_broadcast([128, 4, 257])), ALU.mult)

            def emit_c3(d, dpv):
                E3 = E32[d].re("p (h c) -> p h c", c=257)
                C3 = Cbf[d].re("p (h c) -> p h c", c=257)
                for h in range(4):
                    B.act(C3[:, h, :], E3[:, h, :], AF.Copy, scale=dpv[:, h:h + 1])

            sqj2 = [tmpA.bitcast(BF16)[:, 0:1024], tmpA.bitcast(BF16)[:, 1024:2048]]
            sqj_sel = {"v": sqj2}

            def hnorm_chunks(chs, bank_override, part="all", hgsel=None):
                if part in ("all", "ew"):
                    hnorm_ew(chs, hgsel)
                if part in ("all", "pe"):
                    hnorm_pe(chs, bank_override, hgsel)

            def hnorm_ew(chs, hgsel):
                sqs = sqj_sel["v"]
                for i_, ch_ in enumerate(chs):
                    B.act(sqs[i_ % len(sqs)], hs[ch_], AF.Square)
                for i_, ch_ in enumerate(chs):
                    B.reduce_sum("dve", ssq4[:, ch_, :], sqs[i_ % len(sqs)].re("p (h c) -> p h c", c=256))
                if len(chs) == nch:
                    B.act(rn4[:, 0:nch, :], ssq4[:, 0:nch, :], AF.Identity, bias=EPS, scale=1.0 / 256.0)
                    B.tt("pool", rn4[:, 0:nch, :], rn4[:, 0:nch, :], V(mhalf.res, mhalf.ap.unsqueeze(2).to_broadcast([128, nch, 4])), ALU.pow)
                else:
                    for ch_ in chs:
                        B.act(rn4[:, ch_, :], ssq4[:, ch_, :], AF.Identity, bias=EPS, scale=1.0 / 256.0)
                        B.tt("pool", rn4[:, ch_, :], rn4[:, ch_, :], V(mhalf.res, mhalf.ap.to_broadcast([128, 4])), ALU.pow)
                for i_, ch_ in enumerate(chs):
                    hs4_ = hs[ch_].re("p (h c) -> p h c", c=256)
                    hg_ = (hgsel or hg)[i_ % len(hgsel or hg)]
                    for h in range(4):
                        B.stt("dve", hg_[:, h * 256:(h + 1) * 256], hs4_[:, h, :], rn4[:, ch_, h:h + 1],
                              og[ch_][:, h * 256:(h + 1) * 256], ALU.mult, ALU.mult)

            def hnorm_pe(chs, bank_override, hgsel):
                for i_, ch_ in enumerate(chs):
                    hg_ = (hgsel or hg)[i_ % len(hgsel or hg)]
                    for half in range(2):
                        bk = bank_override if bank_override is not None else B.bank()
                        bkb = bk.bitcast(BF16)
                        for c4 in range(4):
                            c = half * 4 + c4
                            B.tr(bkb[:, c4 * 128:(c4 + 1) * 128], hg_[:, c * 128:(c + 1) * 128], idb)
                        dstv = V(sum([hgT[half * 4 + c4].res for c4 in range(4)], []),
                                 hgT_all3.ap[:, half * 4:half * 4 + 4, ch_ * 128:(ch_ + 1) * 128])
                        B.copy("act" if half == 0 else "dve", dstv, bkb[:, 0:512].re("p (c t) -> p c t", t=128))

            emit_c3(0, dprev[0])
            emit_c3(1, dprev[1])
            emit_vp(*slots[0])
            done = set()
            if nch == 4:
                opieces = [(0, 1), (0, 2), (1, 1), (1, 2), (0, 3), (0, 0), (1, 3), (1, 0)]
            else:
                opieces = [(j, ch_) for j in range(2) for ch_ in range(nch)]
            oslab = {}
            for si_, (s_, d) in enumerate(slots):
                ch = ch_of(s_, d)
                vpv = vp[d][s_ % 2]
                C3 = Cbf[d].re("p (h c) -> p h c", c=257)
                O = PS_O[d]
                for h in range(4):
                    o_ = O[:, h // 2, (h % 2) * 256:(h % 2 + 1) * 256]
                    l1 = sTm[d][ch][:, h * 128:(h + 1) * 128]
                    l2 = qT[h][:, ch * 128:(ch + 1) * 128]
                    B.mm(o_, l1, vpv[:, h, 0:256], start=True, stop=False)
                    B.mm(o_, l2, C3[:, h, 0:256], start=False, stop=True)
                    dn_ = p_den[d][:, h:h + 1]
                    B.mm(dn_, l1, vpv[:, h, 256:257], start=True, stop=False)
                    B.mm(dn_, l2, C3[:, h, 256:257], start=False, stop=True)
                B.act(rr[d], p_den[d], AF.Abs)
                B.tt("dve", rr[d], rr[d], sm["thr"][:, ch, d, :], ALU.max)
                B.recip(rr[d], rr[d])
                state_mm(d, ktok[ch], vpv)
                if si_ + 1 < len(slots):
                    emit_vp(*slots[si_ + 1])
                oj, och = opieces[si_]
                if oj not in oslab:
                    oslab[oj] = use_slab("in_o%d" % oj)
                linB(oslab[oj], xnT, och, 512,
                     lambda b_, oj=oj, och=och: B.act(og[och][:, oj * 512:(oj + 1) * 512], b_, AF.Sigmoid, scale=rcol[:, och:och + 1]),
                     bk=PSB[6])
                if oj == 1:
                    B.tt("pool", og[och], og[och], hnb, ALU.mult)
                state_dve(d, dprev[d])
                dprev[d] = sm["dec"][:, ch, d, :]
                if s_ + 1 < nch:
                    emit_c3(d, dprev[d])
                O4 = O.re("p b (h c) -> p (b h) c", c=256)
                hs4 = hs[ch].re("p (h c) -> p h c", c=256)
                if ch not in done:
                    for h in range(4):
                        B.act(hs4[:, h, :], O4[:, h, :], AF.Copy, scale=rr[d][:, h:h + 1])
                    done.add(ch)
                else:
                    for h in range(4):
                        B.stt("dve", hs4[:, h, :], O4[:, h, :], rr[d][:, h:h + 1], hs4[:, h, :], ALU.mult, ALU.add)
            B.copy("dve", dprevF, dprev[0])
            B.check_stop("l0g")
            if self.stop == "l0b0" and blk == 0:
                for ch in range(nch):
                    B.dump(hs[ch], 1024)
            sc.tag = "L0.hnorm"
            sqj_sel["v"] = [V(vp[d_][i_].res, vp[d_][i_].ap.rearrange("p h c -> p (h c)")[:, 0:1024]) for d_ in range(2) for i_ in range(2)]
            hg4 = [hg[0], hg[1], ar.at(sTm_off, 2048, BF16), ar.at(sTm_off + 2048, 2048, BF16)]
            hnorm_chunks(([1, 2, 3, 0] if nch == 4 else list(range(nch))), None, hgsel=hg4)
            sc.tag = "L0.wout"
            B.check_stop("l0h")
            for j in range(2):
                sl = use_slab("out%d" % j)
                def evo(m, b_):
                    B.tt("dve", hT[m][:, 0:T], b_, hT[m][:, 0:T], ALU.add)
                    norm_piece(hT, 2, T, xnT, m)
                linA(sl, j * 4, 4, hgT, T, evo)
            B.check_stop("l0i")
            if self.stop == "l0b0" and blk == 0:
                B.dump(V(sum([h_.res for h_ in hT], []), hbuf[0][1].ap), 4096)
            mlp_and_ple(hT, 0, T, tok0, nch, pT[0], tok0)
            if self.stop == "l0b0" and blk == 0:
                B.dump(V(sum([h_.res for h_ in hT], []), hbuf[0][1].ap), 4096)
                raise _Stop()

        def layer1(hT, blk, hnext, first_chunk_edge):
            sc.tag = "L1.proj"
            T = 512
            tok0 = blk * 512
            norm(hT, 1, T, xnT, want_rb=False, rc=rcol)
            norm(hnext, 1, 128, xnH, want_rb=False, rc=rcolH)
            if blk > 0:
                B.copy("dve", u[0], uprev)
            for j in range(2):
                sl = use_slab("pi%d" % j)
                for ch in range(4):
                    def evu(b_, ch=ch, j=j):
                        if ch % 2 == 0:
                            B.act(u[1 + ch][:, j * 512:(j + 1) * 512], b_, AF.Copy, scale=rcol[:, ch:ch + 1])
                        else:
                            B.ts("dve", u[1 + ch][:, j * 512:(j + 1) * 512], b_, rcol[:, ch:ch + 1], ALU.mult)
                    linB(sl, xnT, ch, 512, evu)
                linB(sl, xnH, 0, 512, lambda b_, j=j: B.act(u[5][:, j * 512:(j + 1) * 512], b_, AF.Copy, scale=rcolH[:, 0:1]))
            B.copy("dve", uprev, u[4])
            sc.tag = "L1.pool"
            for m in range(8):
                g = m // 2
                bk = B.bank()
                for ch in range(4):
                    o_ = bk[:, ch * 128:(ch + 1) * 128]
                    fs = slice(m * 128, (m + 1) * 128)
                    if blk == 0 and ch == 0:
                        B.mm(o_, u[1][:, fs], c_pt[12 + g], start=True, stop=False)
                    else:
                        B.mm(o_, u[ch][:, fs], c_pt[g], start=True, stop=False)
                        B.mm(o_, u[1 + ch][:, fs], c_pt[4 + g], start=False, stop=False)
                    B.mm(o_, u[2 + ch][:, fs], c_pt[8 + g], start=False, stop=True)
                B.copy("act" if m % 2 == 0 else "dve", yT[m], bk)
            sl = use_slab("grp")
            for g in range(4):
                for mo in range(2):
                    bk = B.bank()
                    for kc in range(2):
                        B.mm(bk, sl[:, g * 2 + kc, mo * 128:(mo + 1) * 128], yT[2 * g + kc], start=(kc == 0), stop=(kc == 1))
                    B.act(zT[2 * g + mo], bk, AF.Copy, scale=vecs3[:, 8, 2 * g + mo:2 * g + mo + 1])
            for j in range(2):
                sl = use_slab("po%d" % j)
                def evo1(m, b_):
                    B.tt("dve", hT[m][:, 0:T], b_, hT[m][:, 0:T], ALU.add)
                    norm_piece(hT, 3, T, xnT, m)
                linA(sl, j * 4, 4, zT, T, evo1)
            mlp_and_ple(hT, 1, T, tok0, 4, pT[1], tok0)
            sc.tag = "L1.final"
            bk = B.bank()
            for ch in range(4):
                for c in range(8):
                    B.mm(bk[:, ch:ch + 1], sq[c][:, ch * 128:(ch + 1) * 128], onesb[:, 0:1], start=(c == 0), stop=(c == 7))
            B.act(rcol[:, 0:4], bk[:, 0:4], AF.Identity, bias=EPS, scale=1.0 / 1024.0)
            B.tt("pool", rcol[:, 0:4], rcol[:, 0:4], V(mhalf.res, mhalf.ap.to_broadcast([128, 4])), ALU.pow)
            for ch in range(4):
                for half in range(2):
                    bk = B.bank()
                    for c4 in range(4):
                        c = half * 4 + c4
                        B.tr(bk[:, c4 * 128:(c4 + 1) * 128], hT[c][:, ch * 128:(ch + 1) * 128], id32)
                    if half == 0:
                        B.act(ost_half[0], bk, AF.Copy, scale=rcol[:, ch:ch + 1])
                    else:
                        B.ts("dve", ost_half[1], bk, rcol[:, ch:ch + 1], ALU.mult)
                    B.dma("sp", out[tok0 + ch * 128: tok0 + (ch + 1) * 128, half * 512:(half + 1) * 512], ost_half[half].ap,
                          ost_res[half], reads=[ost_half[half]])

        nmain = self.n_main
        for it in range(nmain + 1):
            hT = hbuf[it % 2][0]
            if it < nmain:
                layer0(hT, it, 4, it, it == 0, hbuf[it % 2][1])
            else:
                layer0(hT, it, 1, 8, False, hbuf[it % 2][1])
            if it >= 1 and self.do_l1:
                layer1(hbuf[(it - 1) % 2][0], it - 1, hT, it - 1 == 0)
        if self.debug:
            pass


_NC_CACHE = {}


def _make_ptab(flip):
    wins = (2, 4, 8, 16)
    n = 384
    tabs = np.zeros((16, 128, 128), np.float32)
    for g, w in enumerate(wins):
        for edge in (False, True):
            P = np.zeros((n, n), np.float32)
            for t in range(n):
                if not flip:
                    lo, hi = t - w // 2, t + w - w // 2
                else:
                    lo, hi = t - w // 2 + 1, t + w // 2 + 1
                if edge:
                    lo = max(lo, 128)
                cnt = hi - lo
                if cnt <= 0:
                    continue
                lo_c, hi_c = max(lo, 0), min(hi, n)
                P[lo_c:hi_c, t] = 1.0 / cnt
                P[t, t] -= 1.0
            if not edge:
                tabs[g] = P[0:128, 128:256]
                tabs[4 + g] = P[128:256, 128:256]
                tabs[8 + g] = P[256:384, 128:256]
            else:
                tabs[12 + g] = P[128:256, 128:256]
    return tabs


def kernel(x, p, norm_mix, norm_mlp, norm_ple, norm_final, mlstm_w_in, mlstm_b_gates, mlstm_head_norm,
           mlstm_w_out, pool_w_in, pool_w_grp, pool_scale, pool_w_out, mlp_w1, mlp_w2, ple_w, ple_gate_w,
           ple_gate_b):
    f = lambda a: np.ascontiguousarray(np.asarray(a, dtype=np.float32))
    x = f(x)
    p = f(p)
    if "nc" not in _NC_CACHE:
        _NC_CACHE["nc"] = Builder().build()
    nc = _NC_CACHE["nc"]
    w_in = f(mlstm_w_in)[0]
    w_in_flip = w_in.copy()
    w_in_flip[:, 3072:3080] = w_in[:, 3080:3088]
    w_in_flip[:, 3080:3088] = w_in[:, 3072:3080]
    bgt = f(mlstm_b_gates)[0]
    bg_n = bgt.reshape(16)
    bg_f = bgt[[2, 3, 0, 1]].reshape(16)
    common = {
        "norm_mix": f(norm_mix), "norm_mlp": f(norm_mlp), "norm_ple": f(norm_ple), "norm_final": f(norm_final),
        "mlstm_head_norm": f(mlstm_head_norm)[0], "mlstm_w_out": f(mlstm_w_out)[0],
        "pool_w_in": f(pool_w_in)[0], "pool_w_grp": f(pool_w_grp)[0], "pool_scale": f(pool_scale)[0],
        "pool_w_out": f(pool_w_out)[0], "mlp_w1": f(mlp_w1), "mlp_w2": f(mlp_w2), "ple_w": f(ple_w),
        "ple_gate_w": f(ple_gate_w), "ple_gate_b": f(ple_gate_b),
    }
    pt = [_make_ptab(False), _make_ptab(True)]
    in_maps = []
    for c in range(8):
        b, half = c // 2, c % 2
        if half == 0:
            xl = x[b]
            pl = p[:, b, 0:TOK_OWN + 128]
        else:
            xl = np.ascontiguousarray(x[b, ::-1])
            pl = np.ascontiguousarray(p[:, b, ::-1][:, 0:TOK_OWN + 128])
        m = dict(common)
        m["x"] = np.ascontiguousarray(xl)
        m["p"] = np.ascontiguousarray(pl)
        m["mlstm_w_in"] = w_in if half == 0 else w_in_flip
        m["mlstm_b_gates"] = bg_n if half == 0 else bg_f
        m["ptab"] = pt[half]
        in_maps.append(m)
    res = run_bass_kernel_spmd(nc, in_maps, core_ids=list(range(8)))
    outp = np.empty((NB, S, D), np.float32)
    for c in range(8):
        b, half = c // 2, c % 2
        o = res.results[c]["out"]
        if half == 0:
            outp[b, 0:TOK_OWN] = o
        else:
            outp[b, TOK_OWN:] = o[::-1]
    return outp
```

```python
import numpy as np
import concourse.bass as bass
import concourse.mybir as mybir
from concourse.bass_utils import run_bass_kernel_spmd

F32 = mybir.dt.float32
BF16 = mybir.dt.bfloat16
AF = mybir.ActivationFunctionType
ALU = mybir.AluOpType
AX = mybir.AxisListType

D = 1024
S = 8192
NB = 4
H = 4
DK = 128
DV = 256
DFF = 4096
PLE = 256
EPS = 1e-6
TOK_OWN = 4096
NCH_OWN = 32
LN_KSCALE = float(-0.5 * np.log(128.0))

ENGS = ("pe", "act", "dve", "pool", "sp")


class Res:
    __slots__ = ("name", "lw", "rd", "rd_dma", "sem", "ndma", "excl")

    def __init__(self, name, excl=False):
        self.name = name
        self.excl = excl
        self.lw = None
        self.rd = {}
        self.rd_dma = []
        self.sem = None
        self.ndma = 0


class V:
    __slots__ = ("res", "ap")

    def __init__(self, res, ap):
        self.res = res
        self.ap = ap

    def __getitem__(self, key):
        return V(self.res, self.ap[key])

    def re(self, s, **kw):
        return V(self.res, self.ap.rearrange(s, **kw))

    def bc(self, shape):
        return V(self.res, self.ap.to_broadcast(shape))

    def bitcast(self, dt):
        return V(self.res, self.ap.bitcast(dt))


class Op:
    __slots__ = ("eng", "fn", "deps", "is_dma", "chan", "dval", "needs_inc", "incval", "tag")

    def __init__(self, eng, fn):
        self.eng = eng
        self.fn = fn
        self.deps = []
        self.is_dma = False
        self.chan = None
        self.dval = 0
        self.needs_inc = False
        self.incval = 0


class Sched:
    def __init__(self):
        self.ops = {e: [] for e in ENGS}
        self.chans = []
        self.tag = ""

    def add(self, eng, fn, reads=(), writes=(), chan=None, nowaw=False):
        op = Op(eng, fn)
        op.tag = self.tag
        deps = {}
        is_dma = chan is not None
        xr = [v for v in reads if any(r.excl for r in v.res)]
        if xr:
            writes = list(writes) + xr

        def dep(d, raw):
            if d is None or d is op:
                return
            if (not is_dma) and (not d.is_dma) and d.eng == eng and eng == "pe" and not raw:
                return
            deps[id(d)] = d

        for v in reads:
            for r in v.res:
                dep(r.lw, True)
        for v in writes:
            for r in v.res:
                if not (nowaw and r.lw is not None and r.lw.is_dma and r.lw.chan is chan):
                    dep(r.lw, False)
                for x in r.rd.values():
                    dep(x, False)
                for x in r.rd_dma:
                    dep(x, False)
        op.deps = list(deps.values())
        if is_dma:
            op.is_dma = True
            op.chan = chan
            if chan.sem is None:
                chan.sem = True
                self.chans.append(chan)
            chan.ndma += 1
            op.dval = 16 * chan.ndma
        for v in reads:
            for r in v.res:
                if is_dma:
                    r.rd_dma.append(op)
                else:
                    r.rd[eng] = op
        for v in writes:
            for r in v.res:
                r.lw = op
                r.rd = {}
                r.rd_dma = []
        self.ops[eng].append(op)
        return op

    def emit(self, nc, stack):
        for e in ENGS:
            for op in self.ops[e]:
                for d in op.deps:
                    if not d.is_dma:
                        d.needs_inc = True
        import os
        tagmap = {} if os.environ.get("KTAGS") else None
        self.tagmap = tagmap
        esem = {e: stack.enter_context(nc.semaphore("es_" + e)) for e in ENGS}
        for i, c in enumerate(self.chans):
            c.sem = stack.enter_context(nc.semaphore("dc%d_%s" % (i, c.name)))
        for e in ENGS:
            cnt = 0
            for op in self.ops[e]:
                if op.needs_inc and not op.is_dma:
                    cnt += 1
                    op.incval = cnt
        handles = {"pe": "tensor", "act": "scalar", "dve": "vector", "pool": "gpsimd", "sp": "sync"}
        block = stack.enter_context(nc.Block())
        final_waits = [(c.sem, 16 * c.ndma) for c in self.chans]

        def mk(e):
            def body(eng):
                seen = {}
                for op in self.ops[e]:
                    waits = {}
                    for d in op.deps:
                        if d.is_dma:
                            s, v = d.chan.sem, d.dval
                        else:
                            s, v = esem[d.eng], d.incval
                        k = id(s)
                        if seen.get(k, 0) >= v:
                            continue
                        if k not in waits or waits[k][1] < v:
                            waits[k] = (s, v)
                    for s, v in waits.values():
                        eng.wait_ge(s, v)
                        seen[id(s)] = v
                    ins = op.fn(eng)
                    if tagmap is not None:
                        try:
                            tagmap[ins.ins.name] = op.tag
                        except Exception:
                            pass
                    if op.is_dma:
                        ins.then_inc(op.chan.sem, 16)
                    elif op.needs_inc:
                        ins.then_inc(esem[e], 1)
                if e == "sp":
                    for s, v in final_waits:
                        eng.wait_ge(s, v)
            return body

        for e in ENGS:
            getattr(block, handles[e])(mk(e))


class _Stop(Exception):
    pass


class Builder:
    def dump(self, v, n, bf16=False):
        if self.dbg is None:
            return
        i = self.ndump
        self.ndump += 1
        r = Res("dump%d" % i)
        self.dma("pool" if bf16 else "sp", self.dbg[i, :, 0:n], v.ap, r, reads=[v])
        return i

    def check_stop(self, name):
        if self.stop == name:
            raise _Stop()

    def __init__(self, n_main_blocks=8, do_prepass=True, do_l1=True, debug=False, stop=None, pre_blocks=15):
        self.stop = stop
        self.pre_blocks = pre_blocks
        self.ndump = 0
        self.n_main = n_main_blocks
        self.do_prepass = do_prepass
        self.do_l1 = do_l1
        self.debug = debug
        self.nc = bass.Bass("TRN2", target_bir_lowering=False)
        self.sc = Sched()
        self.sb_off = 0
        self._bank = 0

    def alloc(self, name, nbytes, nres=1):
        nbytes = (nbytes + 31) // 32 * 32
        off = self.sb_off
        self.sb_off += nbytes
        return off, [Res("%s%d" % (name, i)) for i in range(nres)]

    def view(self, off, nbytes, dt, res, shape=None):
        a = self.SB[:, off // 4:(off + nbytes) // 4]
        if dt is not F32:
            a = a.bitcast(dt)
        v = V(res, a)
        if shape is not None:
            v = v.re(shape[0], **shape[1])
        return v

    def tile(self, name, dt, free_elems, nsub=1):
        esz = 4 if dt is F32 else 2
        off, res = self.alloc(name, nsub * free_elems * esz, nsub)
        subs = [self.view(off + i * free_elems * esz, free_elems * esz, dt, [res[i]]) for i in range(nsub)]
        whole = self.view(off, nsub * free_elems * esz, dt, res)
        return subs, whole, off

    def bank(self):
        b = self._bank
        self._bank = (self._bank + 1) % 4
        return self.PSB[b]

    def mm(self, out, lhsT, rhs, start=True, stop=True):
        self.sc.add("pe", lambda e, o=out.ap, l=lhsT.ap, r=rhs.ap: e.matmul(o, lhsT=l, rhs=r, start=start, stop=stop),
                    reads=[lhsT, rhs], writes=[out])

    def tr(self, out, in_, ident):
        self.sc.add("pe", lambda e, o=out.ap, i=in_.ap, d=ident.ap: e.transpose(o, i, d),
                    reads=[in_, ident], writes=[out])

    def act(self, out, in_, func, bias=None, scale=None, eng="act"):
        reads = [in_]
        kw = {}
        if bias is not None:
            if isinstance(bias, V):
                reads.append(bias)
                kw["bias"] = bias.ap
            else:
                kw["bias"] = bias
        if scale is not None:
            if isinstance(scale, V):
                reads.append(scale)
                kw["scale"] = scale.ap
            else:
                kw["scale"] = scale
        self.sc.add("act", lambda e, o=out.ap, i=in_.ap: e.activation(o, i, func, **kw), reads=reads, writes=[out])

    def tt(self, eng, out, in0, in1, op):
        self.sc.add(eng, lambda e, o=out.ap, a=in0.ap, b=in1.ap: e.tensor_tensor(o, a, b, op),
                    reads=[in0, in1], writes=[out])

    def ts(self, eng, out, in0, s1, op0, s2=None, op1=None):
        reads = [in0]
        a1 = s1
        if isinstance(s1, V):
            reads.append(s1)
            a1 = s1.ap
        a2 = s2
        if isinstance(s2, V):
            reads.append(s2)
            a2 = s2.ap
        if op1 is None:
            self.sc.add(eng, lambda e, o=out.ap, a=in0.ap: e.tensor_scalar(o, a, a1, None, op0), reads=reads, writes=[out])
        else:
            self.sc.add(eng, lambda e, o=out.ap, a=in0.ap: e.tensor_scalar(o, a, a1, a2, op0, op1), reads=reads, writes=[out])

    def stt(self, eng, out, in0, scalar, in1, op0, op1):
        reads = [in0, in1]
        sa = scalar
        if isinstance(scalar, V):
            reads.append(scalar)
            sa = scalar.ap
        self.sc.add(eng, lambda e, o=out.ap, a=in0.ap, b=in1.ap: e.scalar_tensor_tensor(o, a, sa, b, op0, op1),
                    reads=reads, writes=[out])

    def copy(self, eng, out, in_):
        if eng == "act":
            self.sc.add(eng, lambda e, o=out.ap, i=in_.ap: e.copy(o, i), reads=[in_], writes=[out])
        else:
            self.sc.add(eng, lambda e, o=out.ap, i=in_.ap: e.tensor_copy(o, i), reads=[in_], writes=[out])

    def memset(self, eng, out, val):
        self.sc.add(eng, lambda e, o=out.ap: e.memset(o, val), writes=[out])

    def reduce_sum(self, eng, out, in_):
        self.sc.add(eng, lambda e, o=out.ap, i=in_.ap: e.tensor_reduce(o, i, AX.X, ALU.add), reads=[in_], writes=[out])

    def recip(self, out, in_):
        self.sc.add("dve", lambda e, o=out.ap, i=in_.ap: e.reciprocal(o, i), reads=[in_], writes=[out])

    def dma(self, eng, out, in_, chan, reads=(), writes=(), nowaw=False, slow=False):
        kw = {"allow_slow_non_contiguous": True} if slow else {}
        self.sc.add(eng, lambda e, o=out, i=in_: e.dma_start(out=o, in_=i, **kw), reads=reads, writes=writes,
                    chan=chan, nowaw=nowaw)

    def build(self):
        from contextlib import ExitStack
        nc = self.nc
        dt = nc.dram_tensor

        def din(name, shape):
            return dt(name, shape, F32, kind="ExternalInput").ap()

        x = din("x", [S, D])
        p = din("p", [2, TOK_OWN + 128, PLE])
        norm_mix = din("norm_mix", [2, D])
        norm_mlp = din("norm_mlp", [2, D])
        norm_ple = din("norm_ple", [2, D])
        norm_final = din("norm_final", [D])
        w_in = din("mlstm_w_in", [D, 3088])
        b_gates = din("mlstm_b_gates", [16])
        head_norm = din("mlstm_head_norm", [D])
        w_out = din("mlstm_w_out", [D, D])
        pool_w_in = din("pool_w_in", [D, D])
        pool_w_grp = din("pool_w_grp", [4, 256, 256])
        pool_scale = din("pool_scale", [D])
        pool_w_out = din("pool_w_out", [D, D])
        mlp_w1 = din("mlp_w1", [2, D, DFF])
        mlp_w2 = din("mlp_w2", [2, DFF, D])
        ple_w = din("ple_w", [2, PLE, D])
        ple_gate_w = din("ple_gate_w", [2, D, D])
        ple_gate_b = din("ple_gate_b", [2, D])
        ptab = din("ptab", [16, 128, 128])
        out = dt("out", [TOK_OWN, D], F32, kind="ExternalOutput").ap()
        self.dbg = None
        if self.debug:
            self.dbg = dt("dbg", [16, 128, 4096], F32, kind="ExternalOutput").ap()

        NSLAB = 64
        wscr = dt("wscr", [NSLAB, 128, 4096], BF16, kind="Internal").ap()
        snap = dt("snap", [9, 128, 4 * 257], F32, kind="Internal").ap()
        self.kvscr = dt("kvscr", [33, 128, 512 + 4 * 257], BF16, kind="Internal").ap()

        st = ExitStack()
        self.st = st
        with st:
            SBYTES = 212000
            sbt = st.enter_context(nc.sbuf_tensor("sb", [128, SBYTES // 4], F32))
            self.SB = sbt[:, :]
            pst = st.enter_context(nc.psum_tensor("ps", [128, 8, 512], F32))
            self.PS = pst
            self.PSB = [V([Res("bank%d" % i, excl=True)], pst[:, i, :]) for i in range(8)]
            try:
                self._build_body(x, p, norm_mix, norm_mlp, norm_ple, norm_final, w_in, b_gates, head_norm, w_out,
                                 pool_w_in, pool_w_grp, pool_scale, pool_w_out, mlp_w1, mlp_w2, ple_w, ple_gate_w,
                                 ple_gate_b, ptab, out, wscr, snap)
            except _Stop:
                pass
            assert self.sb_off <= SBYTES, self.sb_off
            self.sc.emit(nc, st)
        return nc

    def _build_body(self, x, p, norm_mix, norm_mlp, norm_ple, norm_final, w_in, b_gates, head_norm, w_out,
                    pool_w_in, pool_w_grp, pool_scale, pool_w_out, mlp_w1, mlp_w2, ple_w, ple_gate_w,
                    ple_gate_b, ptab, out, wscr, snap):
        B = self
        sc = self.sc
        PSB = self.PSB

        slabs = []

        def slab_cols(key, W, c0, w):
            K = W.shape[0]
            kc = K // 128
            pieces = []
            if kc > 8:
                for k0 in range(0, kc, 8):
                    pieces.append((k0 * w, 8, w, W[k0 * 128:(k0 + 8) * 128, c0:c0 + w].rearrange("(kc p) w -> p kc w", p=128)))
            else:
                pieces.append((0, kc, w, W[:, c0:c0 + w].rearrange("(kc p) w -> p kc w", p=128)))
            slabs.append((key, pieces, kc * w, kc, w))

        slab_cols("in_k", w_in, 512, 512)
        slab_cols("in_v0", w_in, 1024, 512)
        slab_cols("in_v1", w_in, 1536, 512)
        slab_cols("in_g", w_in, 3072, 16)
        slab_cols("in_q", w_in, 0, 512)
        slab_cols("in_o0", w_in, 2048, 512)
        slab_cols("in_o1", w_in, 2560, 512)
        slab_cols("out0", w_out, 0, 512)
        slab_cols("out1", w_out, 512, 512)
        for l in range(2):
            for j in range(8):
                slab_cols("w1_%d_%d" % (l, j), mlp_w1[l], j * 512, 512)
            for j in range(8):
                slab_cols("w2_%d_%d" % (l, j), mlp_w2[l], j * 128, 128)
            slab_cols("pg_%d_0" % l, ple_gate_w[l], 0, 512)
            slab_cols("pg_%d_1" % l, ple_gate_w[l], 512, 512)
            slab_cols("pw_%d" % l, ple_w[l], 0, 1024)
        slab_cols("pi0", pool_w_in, 0, 512)
        slab_cols("pi1", pool_w_in, 512, 512)
        slabs.append(("grp", [(0, 8, 256, pool_w_grp.rearrange("g (kc p) w -> p (g kc) w", p=128))], 2048, 8, 256))
        slab_cols("po0", pool_w_out, 0, 512)
        slab_cols("po1", pool_w_out, 512, 512)
        slab_idx = {s[0]: i for i, s in enumerate(slabs)}
        assert len(slabs) <= 64
        slab_res = [Res("slab_" + s[0]) for s in slabs]
        NGRP = 8
        grp_res = [Res("pgrp%d" % i) for i in range(NGRP)]

        def grp_of(i):
            if i < 4:
                return 0
            return 1 + min(NGRP - 2, (i - 4) * (NGRP - 1) // (len(slabs) - 4))

        def emit_prologue(i0=0, i1=None, pace=None):
            for i, (key, pieces, nel, kc, w) in enumerate(slabs):
                if i < i0 or (i1 is not None and i >= i1):
                    continue
                g = grp_res[grp_of(i)]
                for (eo, kcn, ww, src) in pieces:
                    dst = wscr[i, :, eo:eo + kcn * ww].rearrange("p (kc w) -> p kc w", w=ww)
                    B.dma("pool", dst, src, g, reads=([pace] if pace is not None else []), writes=[V([g], None)], nowaw=True)

        c_id32, _, _ = B.tile("id32", F32, 128)
        c_idb, _, _ = B.tile("idb", BF16, 128)
        c_onesb, _, _ = B.tile("onesb", BF16, 128)
        c_ones32, _, _ = B.tile("ones32", F32, 128)
        c_triF, _, _ = B.tile("triF", F32, 128)
        c_triB, _, _ = B.tile("triB", F32, 128)
        c_mF, _, _ = B.tile("mF", BF16, 128)
        c_mB, _, _ = B.tile("mB", BF16, 128)
        c_pt, c_pt_all, _ = B.tile("ptab", BF16, 128, nsub=16)
        id32, idb, onesb, ones32, triF, triB, mF, mB = (c_id32[0], c_idb[0], c_onesb[0], c_ones32[0], c_triF[0],
                                                       c_triB[0], c_mF[0], c_mB[0])
        _, vecs, _ = B.tile("vecs", F32, 8 * 11)
        vecs3 = vecs.re("p (n c) -> p n c", c=8)
        _, vraw, _ = B.tile("vraw", F32, 8 * 11)
        vraw3 = vraw.re("p (n c) -> p n c", c=8)
        _, bg, _ = B.tile("bg", F32, 16)
        _, hnb, _ = B.tile("hnb", BF16, 1024)
        _, ones4, _ = B.tile("ones4", F32, 4)
        _, mhalf, _ = B.tile("mhalf", F32, 1)
        _, rcol, _ = B.tile("rcol", F32, 4)
        _, rcolH, _ = B.tile("rcolH", F32, 4)
        _, rcolP, _ = B.tile("rcolP", F32, 4)
        pace_t = [B.tile("pace%d" % i, F32, 1)[1] for i in range(16)]

        hbuf = [B.tile("hA", F32, 512, nsub=8), B.tile("hB", F32, 512, nsub=8)]
        xs = [B.tile("xs%d" % i, F32, 1024)[0][0] for i in range(2)]
        pstg = [B.tile("pstg%d" % i, F32, 256)[0][0] for i in range(2)]
        pT = [B.tile("pT%d" % i, BF16, 512, nsub=2)[0] for i in range(2)]
        xnT, xnT_all, _ = B.tile("xnT", BF16, 512, nsub=8)
        sq, sq_all, _ = B.tile("sq", BF16, 512, nsub=8)
        _, rb, _ = B.tile("rb", F32, 512)
        xnH, _, _ = B.tile("xnH", BF16, 128, nsub=8)
        _, ostage_full, _ = B.tile("ostage", F32, 1028)
        ostage = ostage_full[:, 0:1024]
        ost_res = [Res("ostA"), Res("ostB")]
        ost_half = [V([ost_res[0]], ostage_full.ap[:, 0:512]), V([ost_res[1]], ostage_full.ap[:, 512:1024])]
        E32 = [B.tile("E32_%d" % d, F32, 4 * 257)[1] for d in range(2)]
        Cbf = [B.tile("Cbf_%d" % d, BF16, 4 * 257)[1] for d in range(2)]
        _, dprevF, _ = B.tile("dprevF", F32, 4)
        snapstg = V(ost_res, ostage_full.ap)
        ring = [B.tile("ring%d" % i, BF16, 4096)[1] for i in range(4)]
        arena_off = self.sb_off
        GR = 1024
        ARENA = 212000 - arena_off
        ARENA = ARENA // GR * GR
        ngr = ARENA // GR
        ares = [Res("ar%d" % i) for i in range(ngr)]
        self.sb_off += ARENA

        class Arena:
            def __init__(s):
                s.off = 0

            def reset(s, o=0):
                s.off = o

            def at(s, o, nbytes, dtp, shape=None):
                g0 = o // GR
                g1 = (o + nbytes - 1) // GR
                v = B.view(arena_off + o, nbytes, dtp, ares[g0:g1 + 1])
                if shape is not None:
                    v = v.re(shape[0], **shape[1])
                return v

            def get(s, nbytes, dtp, shape=None):
                nb = (nbytes + 31) // 32 * 32
                o = s.off
                s.off += nb
                assert s.off <= ARENA, (s.off, ARENA)
                g0 = o // GR
                g1 = (o + nb - 1) // GR
                v = B.view(arena_off + o, nbytes, dtp, ares[g0:g1 + 1])
                if shape is not None:
                    v = v.re(shape[0], **shape[1])
                return v

        ar = Arena()
        qT = [ar.get(512 * 2, BF16) for _ in range(4)]
        kT = [ar.get(512 * 2, BF16) for _ in range(4)]
        ktok = [ar.get(512 * 2, BF16) for _ in range(4)]
        vext = [ar.get(4 * 257 * 2, BF16, ("p (h c) -> p h c", dict(c=257))) for _ in range(4)]
        og = [ar.get(1024 * 2, BF16) for _ in range(4)]
        sTm_off = ar.off
        sTm = [[ar.get(512 * 2, BF16) for _ in range(4)] for _ in range(2)]
        vp = [[ar.get(4 * 257 * 2, BF16, ("p (h c) -> p h c", dict(c=257))) for _ in range(2)] for _ in range(2)]
        hs = [ar.get(1024 * 4, F32) for _ in range(4)]
        tmpA = ar.get(1024 * 4, F32)
        hg = [ar.get(1024 * 2, BF16) for _ in range(2)]
        hg_extra = hg
        hgT_off = ar.off
        hgT = [ar.get(512 * 2, BF16) for _ in range(8)]
        hgT_all3 = ar.at(hgT_off, 8 * 1024, BF16, ("p (c t) -> p c t", dict(t=512)))
        gall = ar.get(4 * 16 * 4, F32, ("p (c g) -> p c g", dict(g=16)))
        sm = {}
        for nm in ("ef", "sp", "warg", "w", "thr", "dec"):
            sm[nm] = ar.get(32 * 4, F32, ("p (c d h) -> p c d h", dict(d=2, h=4)))
        rr = [ar.get(4 * 4, F32) for _ in range(2)]
        ssq4 = ar.get(16 * 4, F32, ("p (c h) -> p c h", dict(h=4)))
        rn4 = ar.get(16 * 4, F32, ("p (c h) -> p c h", dict(h=4)))
        mixer_end = ar.off
        ar.reset(0)
        h1T = [ar.get(512 * 2, BF16) for _ in range(32)]
        rtmp = [ar.get(512 * 4, F32) for _ in range(2)]
        gt = [ar.get(512 * 2, BF16) for _ in range(8)]
        mlp_end = ar.off
        ar.reset(0)
        pre_k = ar.get(4096 * 2, BF16)
        pre_v0 = ar.get(4096 * 2, BF16)
        pre_v1 = ar.get(4096 * 2, BF16)
        pre_g = ar.get(128 * 2, BF16)
        pp_ktok2 = [[ar.get(512 * 2, BF16) for _ in range(4)] for _ in range(2)]
        pp_vext2 = [[ar.get(4 * 257 * 2, BF16, ("p (h c) -> p h c", dict(c=257))) for _ in range(4)] for _ in range(2)]
        pp_vp = [ar.get(4 * 257 * 2, BF16, ("p (h c) -> p h c", dict(c=257))) for _ in range(2)]
        pp_gall2 = [ar.get(4 * 16 * 4, F32, ("p (c g) -> p c g", dict(g=16))) for _ in range(2)]
        pp_sm2 = []
        for _ in range(2):
            d_ = {}
            for nm in ("ef", "sp", "warg", "w", "thr", "dec"):
                d_[nm] = ar.get(32 * 4, F32, ("p (c d h) -> p c d h", dict(d=2, h=4)))
            pp_sm2.append(d_)
        pp_xn2 = [ar.get(512 * 2, BF16) for _ in range(8)]
        ar.reset(0)
        u = [ar.get(1024 * 2, BF16) for _ in range(6)]
        yT = [ar.get(512 * 2, BF16) for _ in range(8)]
        zT = [ar.get(512 * 2, BF16) for _ in range(8)]
        ar.reset(max(mixer_end, mlp_end, ar.off))
        uprev = ar.get(1024 * 2, BF16)

        ps7 = self.PS[:, 7, :]
        ps6 = self.PS[:, 6, :]
        r7 = PSB[7].res
        r6 = PSB[6].res
        p_cum = V(r7, ps7[:, 0:32].rearrange("p (d c h) -> p d c h", d=2, h=4))
        p_tot = V(r7, ps7[:, 32:64].rearrange("p (d c h) -> p d c h", d=2, h=4))
        p_den = [V(r7, ps7[:, 64 + 4 * d:68 + 4 * d]) for d in range(2)]
        p_dn = [V(r7, ps7[:, 72 + 4 * d:76 + 4 * d]) for d in range(2)]
        PS_ST = PSB[6]
        PS_O = [V(PSB[0].res + PSB[1].res, self.PS[:, 0:2, :]), V(PSB[2].res + PSB[3].res, self.PS[:, 2:4, :])]
        PS_DC = [PSB[4], PSB[5]]

        B.memset("pool", ones32, 1.0)
        B.memset("pool", onesb, 1.0)
        B.memset("pool", ones4, 1.0)
        B.memset("pool", mhalf, -0.5)
        B.memset("pool", id32, 1.0)
        sc.add("pool", lambda e, a=id32.ap: e.affine_select(out=a, in_=a, pattern=[[-1, 128]], compare_op=ALU.is_equal,
                                                          fill=0.0, base=0, channel_multiplier=1),
               reads=[id32], writes=[id32])
        B.copy("pool", idb, id32)
        B.memset("pool", triF, 1.0)
        sc.add("pool", lambda e, a=triF.ap: e.affine_select(out=a, in_=a, pattern=[[1, 128]], compare_op=ALU.is_ge,
                                                          fill=0.0, base=0, channel_multiplier=-1),
               reads=[triF], writes=[triF])
        B.memset("pool", triB, 1.0)
        sc.add("pool", lambda e, a=triB.ap: e.affine_select(out=a, in_=a, pattern=[[-1, 128]], compare_op=ALU.is_ge,
                                                          fill=0.0, base=0, channel_multiplier=1),
               reads=[triB], writes=[triB])
        B.copy("pool", mF, triF)
        B.copy("pool", mB, triB)
        B.dma("pool", c_pt_all.re("p (n c) -> p n c", c=128).ap, ptab.rearrange("n p c -> p n c"), c_pt_all.res[0], writes=[c_pt_all])
        vec_srcs = [norm_mix[0], norm_mix[1], norm_mlp[0], norm_mlp[1], norm_ple[0], norm_ple[1], norm_final,
                    head_norm, pool_scale, ple_gate_b[0], ple_gate_b[1]]
        for i, vsrc in enumerate(vec_srcs):
            B.dma("act", vraw3.ap[:, i, :], vsrc.rearrange("(c p) -> p c", p=128), vraw.res[0], writes=[vraw], nowaw=True, slow=True)
        B.dma("act", bg.ap, b_gates.partition_broadcast(128), bg.res[0], writes=[bg], slow=True)
        B.dma("pool", hnb.ap, head_norm.partition_broadcast(128), hnb.res[0], writes=[hnb], slow=True)
        B.ts("dve", vecs3[:, 0:7, :], vraw3[:, 0:7, :], 1.0, ALU.mult)
        B.ts("dve", vecs3[:, 7:8, :], vraw3[:, 7:8, :], 1.0, ALU.mult)
        B.ts("dve", vecs3[:, 8:11, :], vraw3[:, 8:11, :], 1.0, ALU.mult)
        for d in range(2):
            B.memset("pool", E32[d], 0.0)
            B.memset("pool", Cbf[d], 0.0)
        B.copy("pool", dprevF, ones4)
        if self.do_prepass and self.pre_blocks == 15:
            emit_prologue(0, 4)
        else:
            emit_prologue()
        if self.stop == "init":
            B.dump(id32, 128)
            B.dump(triF, 128)
            B.dump(triB, 128)
            B.dump(mF, 128, bf16=True)
            B.dump(vecs, 88)
            B.dump(bg, 16)
            B.dump(c_pt_all, 2048, bf16=True)

        ring_state = {"n": 0}

        def use_slab(key):
            i = slab_idx[key]
            nel = slabs[i][2]
            slot = ring[ring_state["n"] % 4]
            ring_state["n"] += 1
            g = grp_res[grp_of(i)]
            B.dma("sp", slot.ap[:, 0:nel], wscr[i, :, 0:nel], slot.res[0], reads=[V([g], None)], writes=[slot])
            kc, w = slabs[i][3], slabs[i][4]
            return slot[:, 0:nel].re("p (kc w) -> p kc w", w=w)

        def load_resident(key, dstv):
            i = slab_idx[key]
            nel = slabs[i][2]
            g = grp_res[grp_of(i)]
            B.dma("sp", dstv.ap[:, 0:nel], wscr[i, :, 0:nel], dstv.res[0], reads=[V([g], None)], writes=[dstv])
            kc, w = slabs[i][3], slabs[i][4]
            return dstv[:, 0:nel].re("p (kc w) -> p kc w", w=w)

        if self.stop == "init":
            for key in ("in_k", "w2_0_3", "grp", "pw_1"):
                sl = use_slab(key)
                i = slab_idx[key]
                B.dump(V(sl.res, ring[(ring_state["n"] - 1) % 4].ap), 4096, bf16=True)
            raise _Stop()
        xseq = []
        if self.do_prepass:
            for pb_ in range(self.pre_blocks, 0, -1):
                xseq += [pb_ * 512 + c_ * 128 for c_ in range(4)]
        for blk_ in range(self.n_main + 1):
            xseq += [blk_ * 512 + c_ * 128 for c_ in range(4 if blk_ < self.n_main else 1)]
        xq = {"issued": 0, "used": 0}

        def x_issue():
            i = xq["issued"]
            if i >= len(xseq):
                return
            xb = xs[i % 2]
            t0_ = xseq[i]
            B.dma("sp", xb.ap, x[t0_: t0_ + 128, :], xb.res[0], writes=[xb])
            xq["issued"] = i + 1

        def load_block_T(hT, tok0, nch, hwhole):
            hw3 = hwhole.re("p (c t) -> p c t", t=512)
            for ch in range(nch):
                i = xq["used"]
                assert xseq[i] == tok0 + ch * 128, (xseq[i], tok0, ch)
                while xq["issued"] < i + 1:
                    x_issue()
                xb = xs[i % 2]
                xq["used"] = i + 1
                for half in range(2):
                    bk = B.bank()
                    for c4 in range(4):
                        c = half * 4 + c4
                        B.tr(bk[:, c4 * 128:(c4 + 1) * 128], xb[:, c * 128:(c + 1) * 128], id32)
                    eng = "act" if half == 0 else "dve"
                    dstv = V(sum([hT[half * 4 + c4].res for c4 in range(4)], []),
                             hw3.ap[:, half * 4:half * 4 + 4, ch * 128:(ch + 1) * 128])
                    B.copy(eng, dstv, bk.re("p (c t) -> p c t", t=128))
                while xq["issued"] < min(len(xseq), i + 3):
                    x_issue()

        def norm_piece(hT, gi, T, dst, c):
            B.ts("dve", dst[c][:, 0:T], hT[c][:, 0:T], vecs3[:, gi, c:c + 1], ALU.mult)
            B.act(sq[c][:, 0:T], hT[c][:, 0:T], AF.Square)

        def norm(hT, gi, T, dst, want_rb=True, rc=None, pieces_done=False, defer=False):
            nch = T // 128
            if not pieces_done:
                for c in range(8):
                    norm_piece(hT, gi, T, dst, c)

            def stats():
                if want_rb:
                    bk = B.bank()
                    for c in range(8):
                        B.mm(bk[:, 0:T], onesb, sq[c][:, 0:T], start=(c == 0), stop=(c == 7))
                    B.act(rb[:, 0:T], bk[:, 0:T], AF.Sqrt, bias=EPS, scale=1.0 / 1024.0)
                    B.recip(rb[:, 0:T], rb[:, 0:T])
                if rc is not None:
                    bk = B.bank()
                    for ch in range(nch):
                        for c in range(8):
                            B.mm(bk[:, ch:ch + 1], sq[c][:, ch * 128:(ch + 1) * 128], onesb[:, 0:1], start=(c == 0), stop=(c == 7))
                    B.act(rc[:, 0:nch], bk[:, 0:nch], AF.Identity, bias=EPS, scale=1.0 / 1024.0)
                    B.tt("pool", rc[:, 0:nch], rc[:, 0:nch], V(mhalf.res, mhalf.ap.to_broadcast([128, nch])), ALU.pow)
            if defer:
                return stats
            stats()

        def linA(slabv, m0, nm, rhs_list, T, evac):
            kcn = len(rhs_list)
            for m in range(nm):
                bk = B.bank()
                for kc in range(kcn):
                    B.mm(bk[:, 0:T], slabv[:, kc, m * 128:(m + 1) * 128], rhs_list[kc][:, 0:T],
                         start=(kc == 0), stop=(kc == kcn - 1))
                evac(m0 + m, bk[:, 0:T])

        def linB(slabv, lhs_list, ch, ncols, evac, bk=None):
            kcn = len(lhs_list)
            if bk is None:
                bk = B.bank()
            for kc in range(kcn):
                B.mm(bk[:, 0:ncols], lhs_list[kc][:, ch * 128:(ch + 1) * 128], slabv[:, kc, 0:ncols],
                     start=(kc == 0), stop=(kc == kcn - 1))
            evac(bk[:, 0:ncols])

        def gate_prep(gallv, smd, nch):
            g5 = gallv.re("p c (d k h) -> p c d k h", d=2, k=2)
            ipre = g5[:, 0:nch, :, 0, :]
            fpre = g5[:, 0:nch, :, 1, :]
            ef, sp_, warg, w_, thr, dec = (smd[k][:, 0:nch] for k in ("ef", "sp", "warg", "w", "thr", "dec"))
            B.act(ef, fpre, AF.Exp, scale=-1.0)
            B.act(sp_, ef, AF.Ln, bias=1.0)
            cum = V(p_cum.res, p_cum.ap[:, :, 0:nch, :])
            tot = V(p_tot.res, p_tot.ap[:, :, 0:nch, :])
            for d in range(2):
                B.mm(cum[:, d], triF if d == 0 else triB, sp_[:, :, d, :])
                B.mm(tot[:, d], ones32, sp_[:, :, d, :])
            cum_cdh = V(cum.res, cum.ap.rearrange("p d c h -> p c d h"))
            tot_cdh = V(tot.res, tot.ap.rearrange("p d c h -> p c d h"))
            B.tt("dve", warg, ipre, cum_cdh, ALU.add)
            B.act(w_, warg, AF.Exp, bias=LN_KSCALE)
            B.act(thr, cum_cdh, AF.Exp)
            B.act(dec, tot_cdh, AF.Exp, scale=-1.0)

        def state_mm(d, kt, vpv):
            for hp in range(2):
                bk = PS_DC[hp]
                for h2 in range(2):
                    h = hp * 2 + h2
                    B.mm(bk[:, h2 * 256:(h2 + 1) * 256], kt[:, h * 128:(h + 1) * 128], vpv[:, h, 0:256])
                    B.mm(p_dn[d][:, h:h + 1], kt[:, h * 128:(h + 1) * 128], vpv[:, h, 256:257])

        def state_dve(d, dprev):
            E3 = E32[d].re("p (h c) -> p h c", c=257)
            for hp in range(2):
                bk = PS_DC[hp]
                for h2 in range(2):
                    h = hp * 2 + h2
                    B.stt("dve", E3[:, h, 0:256], E3[:, h, 0:256], dprev[:, h:h + 1], bk[:, h2 * 256:(h2 + 1) * 256],
                          ALU.mult, ALU.add)
            En = E3[:, :, 256]
            B.tt("dve", En, En, dprev, ALU.mult)
            B.tt("dve", En, En, p_dn[d], ALU.add)

        def state_update(d, kt, vpv, dprev):
            state_mm(d, kt, vpv)
            state_dve(d, dprev)

        snapd = [Res("snapd%d" % i) for i in range(9)]
        kvscr = self.kvscr
        kvdk = [Res("kvdk%d" % i) for i in range(33)]
        kvdv = [Res("kvdv%d" % i) for i in range(33)]
        kvstk_ch = [[Res("kvstk%d_%d" % (a_, b_)) for b_ in range(4)] for a_ in range(2)]
        kvstv_ch = [[Res("kvstv%d_%d" % (a_, b_)) for b_ in range(4)] for a_ in range(2)]
        kvldk_ch = [Res("kvldk%d" % b_) for b_ in range(4)]
        kvldv_ch = [Res("kvldv%d" % b_) for b_ in range(4)]
        use_kv_cache = self.do_prepass
        if self.do_prepass:
            wk = load_resident("in_k", pre_k)
            wv0 = load_resident("in_v0", pre_v0)
            wv1 = load_resident("in_v1", pre_v1)
            wg = load_resident("in_g", pre_g)
            for par in range(2):
                for i in range(4):
                    B.memset("pool", pp_vext2[par][i][:, :, 256:257], 1.0)
            sc.tag = "pre"
            pxn = [xnT, pp_xn2]
            prc = [rcol, rcolP]
            pst = {"dprev": ones4}

            def stageA(pb, defer=False):
                par = pb % 2
                load_block_T(hbuf[par][0], pb * 512, 4, hbuf[par][1])
                return norm(hbuf[par][0], 0, 512, pxn[par], want_rb=False, rc=prc[par], defer=defer)

            def stageB_chunk(pb, ch):
                par = pb % 2
                xn_ = pxn[par]
                rs = prc[par][:, ch:ch + 1]
                kt_, ve_, ga_ = pp_ktok2[par], pp_vext2[par], pp_gall2[par]
                linB(wk, xn_, ch, 512, lambda b_: B.act(kt_[ch], b_, AF.Copy, scale=rs))
                linB(wv0, xn_, ch, 512, lambda b_: B.ts("dve", ve_[ch][:, 0:2, 0:256], b_.re("p (h c) -> p h c", c=256), rs, ALU.mult))
                linB(wv1, xn_, ch, 512, lambda b_: B.act(ve_[ch][:, 2:4, 0:256], b_.re("p (h c) -> p h c", c=256), AF.Copy, scale=rs))
                bk = B.bank()
                for kc in range(8):
                    B.mm(bk[:, 0:16], xn_[kc][:, ch * 128:(ch + 1) * 128], wg[:, kc, :], start=(kc == 0), stop=(kc == 7))
                B.stt("dve", ga_[:, ch, :], bk[:, 0:16], rs, bg, ALU.mult, ALU.add)
                cg = pb * 4 + ch
                if cg <= 4 * self.n_main:
                    B.dma("sp", kvscr[cg, :, 0:512], kt_[ch].ap, kvstk_ch[par][ch], reads=[kt_[ch]], writes=[V([kvdk[cg]], None)])
                    B.dma("sp", kvscr[cg, :, 512:1540], ve_[ch].ap.rearrange("p h c -> p (h c)"), kvstv_ch[par][ch],
                          reads=[ve_[ch]], writes=[V([kvdv[cg]], None)])

            def stageC_vp(pb, ch):
                par = pb % 2
                vpv = pp_vp[ch % 2]
                wv_ = pp_sm2[par]["w"][:, ch, 1, :]
                B.tt("dve", vpv, pp_vext2[par][ch], V(wv_.res, wv_.ap.unsqueeze(2).to_broadcast([128, 4, 257])), ALU.mult)

            def stageC_chunk(pb, ch):
                par = pb % 2
                cg = pb * 4 + ch
                vpv = pp_vp[ch % 2]
                state_update(1, pp_ktok2[par][ch], vpv, pst["dprev"])
                pst["dprev"] = pp_sm2[par]["dec"][:, ch, 1, :]
                si = None
                if cg == 4 * self.n_main + 1:
                    si = 8
                elif cg % 4 == 0 and cg <= 4 * self.n_main:
                    si = cg // 4 - 1
                if si is not None:
                    E3 = E32[1].re("p (h c) -> p h c", c=257)
                    S3 = snapstg.re("p (h c) -> p h c", c=257)
                    dpv = pst["dprev"]
                    for h in range(4):
                        B.act(S3[:, h, :], E3[:, h, :], AF.Copy, scale=dpv[:, h:h + 1])
                    B.dma("sp", snap[si], snapstg.ap, snapstg.res[0], reads=[snapstg], writes=[V([snapd[si]], None)])

            PB = self.pre_blocks
            stageA(PB)
            if PB - 1 >= 1:
                stageA(PB - 1)
            for ch in range(4):
                stageB_chunk(PB, ch)
            gate_prep(pp_gall2[PB % 2], pp_sm2[PB % 2], 4)
            stageC_vp(PB, 3)
            nsl = len(slabs)
            for pb in range(PB - 1, 0, -1):
                fin = None
                if pb - 1 >= 1:
                    fin = stageA(pb - 1, defer=True)
                if PB == 15:
                    k = PB - 1 - pb
                    pc = pace_t[k]
                    B.memset("dve", pc, 0.0)
                    i0_ = 4 + 4 * k
                    i1_ = nsl if pb == 1 else min(nsl, 4 + 4 * (k + 1))
                    emit_prologue(i0_, i1_, pace=pc)
                for i in range(4):
                    stageC_chunk(pb + 1, 3 - i)
                    if i < 3:
                        stageC_vp(pb + 1, 2 - i)
                    stageB_chunk(pb, i)
                    if i == 0 and fin is not None:
                        fin()
                gate_prep(pp_gall2[pb % 2], pp_sm2[pb % 2], 4)
                stageC_vp(pb, 3)
            for i in range(4):
                stageC_chunk(1, 3 - i)
                if i < 3:
                    stageC_vp(1, 2 - i)
            if self.stop == "prepass":
                dpv = pst["dprev"]
                E3 = E32[1].re("p (h c) -> p h c", c=257)
                B.tt("dve", E3, E3, V(dpv.res, dpv.ap.unsqueeze(2).to_broadcast([128, 4, 257])), ALU.mult)
                B.dump(E32[1], 1028)
                raise _Stop()

        def mlp_and_ple(hT, layer, T, tok0, nch, pTl, pstage_tok0):
            sc.tag = "L%d.mlp" % layer
            norm(hT, 2 + layer, T, xnT, pieces_done=True)
            def p_load(ch):
                pb_ = pstg[ch % 2]
                B.dma("pool", pb_.ap, p[layer, pstage_tok0 + ch * 128: pstage_tok0 + (ch + 1) * 128, :], pb_.res[0], writes=[pb_])

            def p_transpose(ch):
                pb_ = pstg[ch % 2]
                bk = B.bank()
                for c2 in range(2):
                    B.tr(bk[:, c2 * 128:(c2 + 1) * 128], pb_[:, c2 * 128:(c2 + 1) * 128], id32)
                for c2 in range(2):
                    B.copy("act", pTl[c2][:, ch * 128:(ch + 1) * 128], bk[:, c2 * 128:(c2 + 1) * 128])
            for ch in range(min(2, nch)):
                p_load(ch)
            B.check_stop("l0i2")
            for j in range(8):
                if j == 1:
                    B.check_stop("l0i3")
                if j == 4:
                    B.check_stop("l0i4")
                if j >= 5:
                    B.check_stop("l0i%d" % j)
                sl = use_slab("w1_%d_%d" % (layer, j))

                def ev(m, b_):
                    r_ = rtmp[m % 2]
                    B.stt("dve", r_[:, 0:T], b_, 0.0, rb[:, 0:T], ALU.max, ALU.mult)
                    B.act(h1T[m][:, 0:T], r_[:, 0:T], AF.Square)
                linA(sl, j * 4, 4, xnT, T, ev)
                if j == 1:
                    for ch in range(min(2, nch)):
                        p_transpose(ch)
                        if ch + 2 < nch:
                            p_load(ch + 2)
                if j == 3:
                    for ch in range(2, nch):
                        p_transpose(ch)
            B.check_stop("l0j")
            for j in range(8):
                sl = use_slab("w2_%d_%d" % (layer, j))
                def evw2(m, b_):
                    B.tt("dve", hT[m][:, 0:T], b_, hT[m][:, 0:T], ALU.add)
                    norm_piece(hT, 4 + layer, T, xnT, m)
                linA(sl, j, 1, h1T, T, evw2)
            B.check_stop("l0k")
            if self.stop == "l0b0" and layer == 0:
                B.dump(V(sum([h_.res for h_ in hT], []), hbuf[0][1].ap), 4096)
            sc.tag = "L%d.ple" % layer
            norm(hT, 4 + layer, T, xnT, pieces_done=True)
            B.check_stop("l0l")
            def evg(m, b_):
                t_ = rtmp[m % 2]
                B.tt("dve", t_[:, 0:T], b_, rb[:, 0:T], ALU.mult)
                B.act(gt[m][:, 0:T], t_[:, 0:T], AF.Sigmoid, bias=vecs3[:, 9 + layer, m:m + 1])

            def ev2(m, b_):
                t_ = rtmp[m % 2]
                B.tt("dve", t_[:, 0:T], b_, gt[m][:, 0:T], ALU.mult)
                B.tt("dve", hT[m][:, 0:T], t_[:, 0:T], hT[m][:, 0:T], ALU.add)
                if layer == 1:
                    B.act(sq[m][:, 0:T], hT[m][:, 0:T], AF.Square)
                    B.ts("dve", hT[m][:, 0:T], hT[m][:, 0:T], vecs3[:, 6, m:m + 1], ALU.mult)
            slg = [use_slab("pg_%d_0" % layer), None]
            slw = use_slab("pw_%d" % layer)
            for m in range(8):
                j = m // 4
                if slg[j] is None:
                    slg[j] = use_slab("pg_%d_%d" % (layer, j))
                bk = B.bank()
                for kc in range(8):
                    B.mm(bk[:, 0:T], slg[j][:, kc, (m % 4) * 128:(m % 4 + 1) * 128], xnT[kc][:, 0:T], start=(kc == 0), stop=(kc == 7))
                evg(m, bk[:, 0:T])
                bk2 = B.bank()
                for kc in range(2):
                    B.mm(bk2[:, 0:T], slw[:, kc, m * 128:(m + 1) * 128], pTl[kc][:, 0:T], start=(kc == 0), stop=(kc == 1))
                ev2(m, bk2[:, 0:T])
            B.check_stop("l0n")

        def layer0(hT, blk, nch, snap_i, first, hwhole):
            sc.tag = "L0.load"
            T = 128 * nch
            tok0 = blk * 512
            load_block_T(hT, tok0, nch, hwhole)
            if self.do_prepass:
                B.dma("sp", E32[1].ap, snap[snap_i], E32[1].res[0], reads=[V([snapd[snap_i]], None)], writes=[E32[1]])
            B.check_stop("l0a")
            norm(hT, 0, T, xnT, want_rb=True, rc=rcol)
            sc.tag = "L0.proj"
            B.check_stop("l0b")
            sl = use_slab("in_q")
            linA(sl, 0, 4, xnT, T, lambda m, b_: B.tt("dve", qT[m][:, 0:T], b_, rb[:, 0:T], ALU.mult))
            sl = use_slab("in_k")
            linA(sl, 0, 4, xnT, T, lambda m, b_: B.tt("dve", kT[m][:, 0:T], b_, rb[:, 0:T], ALU.mult))
            if use_kv_cache and blk >= 1:
                for ch in range(nch):
                    cg = blk * 4 + ch
                    B.dma("sp", ktok[ch].ap, kvscr[cg, :, 0:512], kvldk_ch[ch], reads=[V([kvdk[cg]], None)], writes=[ktok[ch]])
                    B.dma("sp", vext[ch].ap.rearrange("p h c -> p (h c)"), kvscr[cg, :, 512:1540], kvldv_ch[ch],
                          reads=[V([kvdv[cg]], None)], writes=[vext[ch]])
            else:
                for ch in range(nch):
                    linB(sl, xnT, ch, 512, lambda b_, ch=ch: B.act(ktok[ch], b_, AF.Copy, scale=rcol[:, ch:ch + 1]))
                sl = use_slab("in_v0")
                for ch in range(nch):
                    linB(sl, xnT, ch, 512, lambda b_, ch=ch: B.ts("dve", vext[ch][:, 0:2, 0:256], b_.re("p (h c) -> p h c", c=256), rcol[:, ch:ch + 1], ALU.mult))
                sl = use_slab("in_v1")
                for ch in range(nch):
                    linB(sl, xnT, ch, 512, lambda b_, ch=ch: B.act(vext[ch][:, 2:4, 0:256], b_.re("p (h c) -> p h c", c=256), AF.Copy, scale=rcol[:, ch:ch + 1]))
                for ch in range(nch):
                    B.memset("dve", vext[ch][:, :, 256:257], 1.0)
            B.check_stop("l0c")
            sl = use_slab("in_g")
            for ch in range(nch):
                bk = B.bank()
                for kc in range(8):
                    B.mm(bk[:, 0:16], xnT[kc][:, ch * 128:(ch + 1) * 128], sl[:, kc, :], start=(kc == 0), stop=(kc == 7))
                B.stt("dve", gall[:, ch, :], bk[:, 0:16], rcol[:, ch:ch + 1], bg, ALU.mult, ALU.add)
            B.check_stop("l0d")
            gate_prep(gall, sm, nch)
            B.check_stop("l0e")
            sc.tag = "L0.scores"
            for ch in range(nch):
                stb = PS_ST if ch % 2 == 0 else PSB[5]
                for h in range(4):
                    B.mm(stb[:, h * 128:(h + 1) * 128], kT[h][:, ch * 128:(ch + 1) * 128], qT[h][:, ch * 128:(ch + 1) * 128])
                st3 = stb.re("p (h t) -> p h t", t=128)
                B.tt("dve", sTm[0][ch].re("p (h t) -> p h t", t=128), st3,
                     V(mF.res, mF.ap.unsqueeze(1).to_broadcast([128, 4, 128])), ALU.mult)
                B.tt("dve", sTm[1][ch].re("p (h t) -> p h t", t=128), st3,
                     V(mB.res, mB.ap.unsqueeze(1).to_broadcast([128, 4, 128])), ALU.mult)
            B.check_stop("l0f")
            sc.tag = "L0.scan"
            dprev = [dprevF, ones4]
            slots = [(s_, d) for s_ in range(nch) for d in range(2)]

            def ch_of(s_, d):
                return s_ if d == 0 else nch - 1 - s_

            def emit_vp(s_, d):
                ch_ = ch_of(s_, d)
                vpv_ = vp[d][s_ % 2]
                wv_ = sm["w"][:, ch_, d, :]
                for h in range(4):
                    B.act(vpv_[:, h, :], vext[ch_][:, h, :], AF.Copy, scale=wv_[:, h:h + 1])

            def emit_c3(d, dpv):
                E3 = E32[d].re("p (h c) -> p h c", c=257)
                C3 = Cbf[d].re("p (h c) -> p h c", c=257)
                if d == 0:
                    B.tt("dve", C3, E3, V(dpv.res, dpv.ap.unsqueeze(2).to_broadcast([128, 4, 257])), ALU.mult)
                else:
                    for h in range(4):
                        B.act(C3[:, h, :], E3[:, h, :], AF.Copy, scale=dpv[:, h:h + 1])

            sqj2 = [tmpA.bitcast(BF16)[:, 0:1024], tmpA.bitcast(BF16)[:, 1024:2048]]
            sqj_sel = {"v": sqj2}

            def hnorm_chunks(chs, bank_override, part="all", hgsel=None):
                if part in ("all", "ew"):
                    hnorm_ew(chs, hgsel)
                if part in ("all", "pe"):
                    hnorm_pe(chs, bank_override, hgsel)

            def hnorm_ew(chs, hgsel):
                sqs = sqj_sel["v"]
                for i_, ch_ in enumerate(chs):
                    B.act(sqs[i_ % len(sqs)], hs[ch_], AF.Square)
                for i_, ch_ in enumerate(chs):
                    B.reduce_sum("dve", ssq4[:, ch_, :], sqs[i_ % len(sqs)].re("p (h c) -> p h c", c=256))
                if len(chs) == nch:
                    B.act(rn4[:, 0:nch, :], ssq4[:, 0:nch, :], AF.Identity, bias=EPS, scale=1.0 / 256.0)
                    B.tt("pool", rn4[:, 0:nch, :], rn4[:, 0:nch, :], V(mhalf.res, mhalf.ap.unsqueeze(2).to_broadcast([128, nch, 4])), ALU.pow)
                else:
                    for ch_ in chs:
                        B.act(rn4[:, ch_, :], ssq4[:, ch_, :], AF.Identity, bias=EPS, scale=1.0 / 256.0)
                        B.tt("pool", rn4[:, ch_, :], rn4[:, ch_, :], V(mhalf.res, mhalf.ap.to_broadcast([128, 4])), ALU.pow)
                for i_, ch_ in enumerate(chs):
                    hs4_ = hs[ch_].re("p (h c) -> p h c", c=256)
                    hg_ = (hgsel or hg)[i_ % len(hgsel or hg)]
                    for h in range(4):
                        B.stt("dve", hg_[:, h * 256:(h + 1) * 256], hs4_[:, h, :], rn4[:, ch_, h:h + 1],
                              og[ch_][:, h * 256:(h + 1) * 256], ALU.mult, ALU.mult)

            def hnorm_pe(chs, bank_override, hgsel):
                for i_, ch_ in enumerate(chs):
                    hg_ = (hgsel or hg)[i_ % len(hgsel or hg)]
                    for half in range(2):
                        bk = bank_override if bank_override is not None else B.bank()
                        bkb = bk.bitcast(BF16)
                        for c4 in range(4):
                            c = half * 4 + c4
                            B.tr(bkb[:, c4 * 128:(c4 + 1) * 128], hg_[:, c * 128:(c + 1) * 128], idb)
                        dstv = V(sum([hgT[half * 4 + c4].res for c4 in range(4)], []),
                                 hgT_all3.ap[:, half * 4:half * 4 + 4, ch_ * 128:(ch_ + 1) * 128])
                        B.copy("act" if half == 0 else "dve", dstv, bkb[:, 0:512].re("p (c t) -> p c t", t=128))

            emit_c3(0, dprev[0])
            emit_c3(1, dprev[1])
            emit_vp(*slots[0])
            done = set()
            if nch == 4:
                opieces = [(0, 1), (0, 2), (1, 1), (1, 2), (0, 3), (0, 0), (1, 3), (1, 0)]
            else:
                opieces = [(j, ch_) for j in range(2) for ch_ in range(nch)]
            oslab = {}
            for si_, (s_, d) in enumerate(slots):
                ch = ch_of(s_, d)
                vpv = vp[d][s_ % 2]
                C3 = Cbf[d].re("p (h c) -> p h c", c=257)
                O = PS_O[d]
                for h in range(4):
                    o_ = O[:, h // 2, (h % 2) * 256:(h % 2 + 1) * 256]
                    l1 = sTm[d][ch][:, h * 128:(h + 1) * 128]
                    l2 = qT[h][:, ch * 128:(ch + 1) * 128]
                    B.mm(o_, l1, vpv[:, h, 0:256], start=True, stop=False)
                    B.mm(o_, l2, C3[:, h, 0:256], start=False, stop=True)
                    dn_ = p_den[d][:, h:h + 1]
                    B.mm(dn_, l1, vpv[:, h, 256:257], start=True, stop=False)
                    B.mm(dn_, l2, C3[:, h, 256:257], start=False, stop=True)
                B.act(rr[d], p_den[d], AF.Abs)
                B.tt("dve", rr[d], rr[d], sm["thr"][:, ch, d, :], ALU.max)
                B.recip(rr[d], rr[d])
                state_mm(d, ktok[ch], vpv)
                if si_ + 1 < len(slots):
                    emit_vp(*slots[si_ + 1])
                oj, och = opieces[si_]
                if oj not in oslab:
                    oslab[oj] = use_slab("in_o%d" % oj)
                linB(oslab[oj], xnT, och, 512,
                     lambda b_, oj=oj, och=och: B.act(og[och][:, oj * 512:(oj + 1) * 512], b_, AF.Sigmoid, scale=rcol[:, och:och + 1]),
                     bk=PSB[6])
                if oj == 1:
                    B.tt("pool", og[och], og[och], hnb, ALU.mult)
                state_dve(d, dprev[d])
                dprev[d] = sm["dec"][:, ch, d, :]
                if s_ + 1 < nch:
                    emit_c3(d, dprev[d])
                O4 = O.re("p b (h c) -> p (b h) c", c=256)
                hs4 = hs[ch].re("p (h c) -> p h c", c=256)
                if ch not in done:
                    rbv = V(rr[d].res, rr[d].ap.unsqueeze(2).to_broadcast([128, 4, 256]))
                    B.tt("dve", hs4, O4, rbv, ALU.mult)
                    done.add(ch)
                else:
                    for h in range(4):
                        B.stt("dve", hs4[:, h, :], O4[:, h, :], rr[d][:, h:h + 1], hs4[:, h, :], ALU.mult, ALU.add)
            B.copy("dve", dprevF, dprev[0])
            B.check_stop("l0g")
            if self.stop == "l0b0" and blk == 0:
                for ch in range(nch):
                    B.dump(hs[ch], 1024)
            sc.tag = "L0.hnorm"
            sqj_sel["v"] = [V(vp[d_][i_].res, vp[d_][i_].ap.rearrange("p h c -> p (h c)")[:, 0:1024]) for d_ in range(2) for i_ in range(2)]
            hg4 = [hg[0], hg[1], ar.at(sTm_off, 2048, BF16), ar.at(sTm_off + 2048, 2048, BF16)]
            hnorm_chunks(([1, 2, 3, 0] if nch == 4 else list(range(nch))), None, hgsel=hg4)
            sc.tag = "L0.wout"
            B.check_stop("l0h")
            for j in range(2):
                sl = use_slab("out%d" % j)
                def evo(m, b_):
                    B.tt("dve", hT[m][:, 0:T], b_, hT[m][:, 0:T], ALU.add)
                    norm_piece(hT, 2, T, xnT, m)
                linA(sl, j * 4, 4, hgT, T, evo)
            B.check_stop("l0i")
            if self.stop == "l0b0" and blk == 0:
                B.dump(V(sum([h_.res for h_ in hT], []), hbuf[0][1].ap), 4096)
            mlp_and_ple(hT, 0, T, tok0, nch, pT[0], tok0)
            if self.stop == "l0b0" and blk == 0:
                B.dump(V(sum([h_.res for h_ in hT], []), hbuf[0][1].ap), 4096)
                raise _Stop()

        def layer1(hT, blk, hnext, first_chunk_edge):
            sc.tag = "L1.proj"
            T = 512
            tok0 = blk * 512
            norm(hT, 1, T, xnT, want_rb=False, rc=rcol)
            norm(hnext, 1, 128, xnH, want_rb=False, rc=rcolH)
            if blk > 0:
                B.copy("dve", u[0], uprev)
            for j in range(2):
                sl = use_slab("pi%d" % j)
                for ch in range(4):
                    def evu(b_, ch=ch, j=j):
                        if ch % 2 == 0:
                            B.act(u[1 + ch][:, j * 512:(j + 1) * 512], b_, AF.Copy, scale=rcol[:, ch:ch + 1])
                        else:
                            B.ts("dve", u[1 + ch][:, j * 512:(j + 1) * 512], b_, rcol[:, ch:ch + 1], ALU.mult)
                    linB(sl, xnT, ch, 512, evu)
                linB(sl, xnH, 0, 512, lambda b_, j=j: B.act(u[5][:, j * 512:(j + 1) * 512], b_, AF.Copy, scale=rcolH[:, 0:1]))
            B.copy("dve", uprev, u[4])
            sc.tag = "L1.pool"
            for m in range(8):
                g = m // 2
                bk = B.bank()
                for ch in range(4):
                    o_ = bk[:, ch * 128:(ch + 1) * 128]
                    fs = slice(m * 128, (m + 1) * 128)
                    if blk == 0 and ch == 0:
                        B.mm(o_, u[1][:, fs], c_pt[12 + g], start=True, stop=False)
                    else:
                        B.mm(o_, u[ch][:, fs], c_pt[g], start=True, stop=False)
                        B.mm(o_, u[1 + ch][:, fs], c_pt[4 + g], start=False, stop=False)
                    B.mm(o_, u[2 + ch][:, fs], c_pt[8 + g], start=False, stop=True)
                B.copy("act" if m % 2 == 0 else "dve", yT[m], bk)
            sl = use_slab("grp")
            for g in range(4):
                for mo in range(2):
                    bk = B.bank()
                    for kc in range(2):
                        B.mm(bk, sl[:, g * 2 + kc, mo * 128:(mo + 1) * 128], yT[2 * g + kc], start=(kc == 0), stop=(kc == 1))
                    B.act(zT[2 * g + mo], bk, AF.Copy, scale=vecs3[:, 8, 2 * g + mo:2 * g + mo + 1])
            for j in range(2):
                sl = use_slab("po%d" % j)
                def evo1(m, b_):
                    B.tt("dve", hT[m][:, 0:T], b_, hT[m][:, 0:T], ALU.add)
                    norm_piece(hT, 3, T, xnT, m)
                linA(sl, j * 4, 4, zT, T, evo1)
            mlp_and_ple(hT, 1, T, tok0, 4, pT[1], tok0)
            sc.tag = "L1.final"
            bk = B.bank()
            for ch in range(4):
                for c in range(8):
                    B.mm(bk[:, ch:ch + 1], sq[c][:, ch * 128:(ch + 1) * 128], onesb[:, 0:1], start=(c == 0), stop=(c == 7))
            B.act(rcol[:, 0:4], bk[:, 0:4], AF.Identity, bias=EPS, scale=1.0 / 1024.0)
            B.tt("pool", rcol[:, 0:4], rcol[:, 0:4], V(mhalf.res, mhalf.ap.to_broadcast([128, 4])), ALU.pow)
            for ch in range(4):
                for half in range(2):
                    bk = B.bank()
                    for c4 in range(4):
                        c = half * 4 + c4
                        B.tr(bk[:, c4 * 128:(c4 + 1) * 128], hT[c][:, ch * 128:(ch + 1) * 128], id32)
                    if half == 0:
                        B.act(ost_half[0], bk, AF.Copy, scale=rcol[:, ch:ch + 1])
                    else:
                        B.ts("dve", ost_half[1], bk, rcol[:, ch:ch + 1], ALU.mult)
                    B.dma("sp", out[tok0 + ch * 128: tok0 + (ch + 1) * 128, half * 512:(half + 1) * 512], ost_half[half].ap,
                          ost_res[half], reads=[ost_half[half]])

        nmain = self.n_main
        for it in range(nmain + 1):
            hT = hbuf[it % 2][0]
            if it < nmain:
                layer0(hT, it, 4, it, it == 0, hbuf[it % 2][1])
            else:
                layer0(hT, it, 1, 8, False, hbuf[it % 2][1])
            if it >= 1 and self.do_l1:
                layer1(hbuf[(it - 1) % 2][0], it - 1, hT, it - 1 == 0)
        if self.debug:
            pass


_NC_CACHE = {}


def _make_ptab(flip):
    wins = (2, 4, 8, 16)
    n = 384
    tabs = np.zeros((16, 128, 128), np.float32)
    for g, w in enumerate(wins):
        for edge in (False, True):
            P = np.zeros((n, n), np.float32)
            for t in range(n):
                if not flip:
                    lo, hi = t - w // 2, t + w - w // 2
                else:
                    lo, hi = t - w // 2 + 1, t + w // 2 + 1
                if edge:
                    lo = max(lo, 128)
                cnt = hi - lo
                if cnt <= 0:
                    continue
                lo_c, hi_c = max(lo, 0), min(hi, n)
                P[lo_c:hi_c, t] = 1.0 / cnt
                P[t, t] -= 1.0
            if not edge:
                tabs[g] = P[0:128, 128:256]
                tabs[4 + g] = P[128:256, 128:256]
                tabs[8 + g] = P[256:384, 128:256]
            else:
                tabs[12 + g] = P[128:256, 128:256]
    return tabs


def kernel(x, p, norm_mix, norm_mlp, norm_ple, norm_final, mlstm_w_in, mlstm_b_gates, mlstm_head_norm,
           mlstm_w_out, pool_w_in, pool_w_grp, pool_scale, pool_w_out, mlp_w1, mlp_w2, ple_w, ple_gate_w,
           ple_gate_b):
    f = lambda a: np.ascontiguousarray(np.asarray(a, dtype=np.float32))
    x = f(x)
    p = f(p)
    if "nc" not in _NC_CACHE:
        _NC_CACHE["nc"] = Builder().build()
    nc = _NC_CACHE["nc"]
    w_in = f(mlstm_w_in)[0]
    w_in_flip = w_in.copy()
    w_in_flip[:, 3072:3080] = w_in[:, 3080:3088]
    w_in_flip[:, 3080:3088] = w_in[:, 3072:3080]
    bgt = f(mlstm_b_gates)[0]
    bg_n = bgt.reshape(16)
    bg_f = bgt[[2, 3, 0, 1]].reshape(16)
    common = {
        "norm_mix": f(norm_mix), "norm_mlp": f(norm_mlp), "norm_ple": f(norm_ple), "norm_final": f(norm_final),
        "mlstm_head_norm": f(mlstm_head_norm)[0], "mlstm_w_out": f(mlstm_w_out)[0],
        "pool_w_in": f(pool_w_in)[0], "pool_w_grp": f(pool_w_grp)[0], "pool_scale": f(pool_scale)[0],
        "pool_w_out": f(pool_w_out)[0], "mlp_w1": f(mlp_w1), "mlp_w2": f(mlp_w2), "ple_w": f(ple_w),
        "ple_gate_w": f(ple_gate_w), "ple_gate_b": f(ple_gate_b),
    }
    pt = [_make_ptab(False), _make_ptab(True)]
    in_maps = []
    for c in range(8):
        b, half = c // 2, c % 2
        if half == 0:
            xl = x[b]
            pl = p[:, b, 0:TOK_OWN + 128]
        else:
            xl = np.ascontiguousarray(x[b, ::-1])
            pl = np.ascontiguousarray(p[:, b, ::-1][:, 0:TOK_OWN + 128])
        m = dict(common)
        m["x"] = np.ascontiguousarray(xl)
        m["p"] = np.ascontiguousarray(pl)
        m["mlstm_w_in"] = w_in if half == 0 else w_in_flip
        m["mlstm_b_gates"] = bg_n if half == 0 else bg_f
        m["ptab"] = pt[half]
        in_maps.append(m)
    res = run_bass_kernel_spmd(nc, in_maps, core_ids=list(range(8)))
    outp = np.empty((NB, S, D), np.float32)
    for c in range(8):
        b, half = c // 2, c % 2
        o = res.results[c]["out"]
        if half == 0:
            outp[b, 0:TOK_OWN] = o
        else:
            outp[b, TOK_OWN:] = o[::-1]
    return outp
```

```python
import numpy as np
import concourse.bass as bass
import concourse.mybir as mybir
from concourse.bass_utils import run_bass_kernel_spmd

F32 = mybir.dt.float32
BF16 = mybir.dt.bfloat16
AF = mybir.ActivationFunctionType
ALU = mybir.AluOpType
AX = mybir.AxisListType

D = 1024
S = 8192
NB = 4
H = 4
DK = 128
DV = 256
DFF = 4096
PLE = 256
EPS = 1e-6
TOK_OWN = 4096
NCH_OWN = 32
LN_KSCALE = float(-0.5 * np.log(128.0))

ENGS = ("pe", "act", "dve", "pool", "sp")


class Res:
    __slots__ = ("name", "lw", "rd", "rd_dma", "sem", "ndma", "excl")

    def __init__(self, name, excl=False):
        self.name = name
        self.excl = excl
        self.lw = None
        self.rd = {}
        self.rd_dma = []
        self.sem = None
        self.ndma = 0


class V:
    __slots__ = ("res", "ap")

    def __init__(self, res, ap):
        self.res = res
        self.ap = ap

    def __getitem__(self, key):
        return V(self.res, self.ap[key])

    def re(self, s, **kw):
        return V(self.res, self.ap.rearrange(s, **kw))

    def bc(self, shape):
        return V(self.res, self.ap.to_broadcast(shape))

    def bitcast(self, dt):
        return V(self.res, self.ap.bitcast(dt))


class Op:
    __slots__ = ("eng", "fn", "deps", "is_dma", "chan", "dval", "needs_inc", "incval", "tag")

    def __init__(self, eng, fn):
        self.eng = eng
        self.fn = fn
        self.deps = []
        self.is_dma = False
        self.chan = None
        self.dval = 0
        self.needs_inc = False
        self.incval = 0


class Sched:
    def __init__(self):
        self.ops = {e: [] for e in ENGS}
        self.chans = []
        self.tag = ""

    def add(self, eng, fn, reads=(), writes=(), chan=None, nowaw=False):
        op = Op(eng, fn)
        op.tag = self.tag
        deps = {}
        is_dma = chan is not None
        xr = [v for v in reads if any(r.excl for r in v.res)]
        if xr:
            writes = list(writes) + xr

        def dep(d, raw):
            if d is None or d is op:
                return
            if (not is_dma) and (not d.is_dma) and d.eng == eng and eng == "pe" and not raw:
                return
            deps[id(d)] = d

        for v in reads:
            for r in v.res:
                dep(r.lw, True)
        for v in writes:
            for r in v.res:
                if not (nowaw and r.lw is not None and r.lw.is_dma and r.lw.chan is chan):
                    dep(r.lw, False)
                for x in r.rd.values():
                    dep(x, False)
                for x in r.rd_dma:
                    dep(x, False)
        op.deps = list(deps.values())
        if is_dma:
            op.is_dma = True
            op.chan = chan
            if chan.sem is None:
                chan.sem = True
                self.chans.append(chan)
            chan.ndma += 1
            op.dval = 16 * chan.ndma
        for v in reads:
            for r in v.res:
                if is_dma:
                    r.rd_dma.append(op)
                else:
                    r.rd[eng] = op
        for v in writes:
            for r in v.res:
                r.lw = op
                r.rd = {}
                r.rd_dma = []
        self.ops[eng].append(op)
        return op

    def emit(self, nc, stack):
        for e in ENGS:
            for op in self.ops[e]:
                for d in op.deps:
                    if not d.is_dma:
                        d.needs_inc = True
        import os
        tagmap = {} if os.environ.get("KTAGS") else None
        self.tagmap = tagmap
        esem = {e: stack.enter_context(nc.semaphore("es_" + e)) for e in ENGS}
        for i, c in enumerate(self.chans):
            c.sem = stack.enter_context(nc.semaphore("dc%d_%s" % (i, c.name)))
        for e in ENGS:
            cnt = 0
            for op in self.ops[e]:
                if op.needs_inc and not op.is_dma:
                    cnt += 1
                    op.incval = cnt
        handles = {"pe": "tensor", "act": "scalar", "dve": "vector", "pool": "gpsimd", "sp": "sync"}
        block = stack.enter_context(nc.Block())
        final_waits = [(c.sem, 16 * c.ndma) for c in self.chans]

        def mk(e):
            def body(eng):
                seen = {}
                for op in self.ops[e]:
                    waits = {}
                    for d in op.deps:
                        if d.is_dma:
                            s, v = d.chan.sem, d.dval
                        else:
                            s, v = esem[d.eng], d.incval
                        k = id(s)
                        if seen.get(k, 0) >= v:
                            continue
                        if k not in waits or waits[k][1] < v:
                            waits[k] = (s, v)
                    for s, v in waits.values():
                        eng.wait_ge(s, v)
                        seen[id(s)] = v
                    ins = op.fn(eng)
                    if tagmap is not None:
                        try:
                            tagmap[ins.ins.name] = op.tag
                        except Exception:
                            pass
                    if op.is_dma:
                        ins.then_inc(op.chan.sem, 16)
                    elif op.needs_inc:
                        ins.then_inc(esem[e], 1)
                if e == "sp":
                    for s, v in final_waits:
                        eng.wait_ge(s, v)
            return body

        for e in ENGS:
            getattr(block, handles[e])(mk(e))


class _Stop(Exception):
    pass


class Builder:
    def dump(self, v, n, bf16=False):
        if self.dbg is None:
            return
        i = self.ndump
        self.ndump += 1
        r = Res("dump%d" % i)
        self.dma("pool" if bf16 else "sp", self.dbg[i, :, 0:n], v.ap, r, reads=[v])
        return i

    def check_stop(self, name):
        if self.stop == name:
            raise _Stop()

    def __init__(self, n_main_blocks=8, do_prepass=True, do_l1=True, debug=False, stop=None, pre_blocks=15):
        self.stop = stop
        self.pre_blocks = pre_blocks
        self.ndump = 0
        self.n_main = n_main_blocks
        self.do_prepass = do_prepass
        self.do_l1 = do_l1
        self.debug = debug
        self.nc = bass.Bass("TRN2", target_bir_lowering=False)
        self.sc = Sched()
        self.sb_off = 0
        self._bank = 0

    def alloc(self, name, nbytes, nres=1):
        nbytes = (nbytes + 31) // 32 * 32
        off = self.sb_off
        self.sb_off += nbytes
        return off, [Res("%s%d" % (name, i)) for i in range(nres)]

    def view(self, off, nbytes, dt, res, shape=None):
        a = self.SB[:, off // 4:(off + nbytes) // 4]
        if dt is not F32:
            a = a.bitcast(dt)
        v = V(res, a)
        if shape is not None:
            v = v.re(shape[0], **shape[1])
        return v

    def tile(self, name, dt, free_elems, nsub=1):
        esz = 4 if dt is F32 else 2
        off, res = self.alloc(name, nsub * free_elems * esz, nsub)
        subs = [self.view(off + i * free_elems * esz, free_elems * esz, dt, [res[i]]) for i in range(nsub)]
        whole = self.view(off, nsub * free_elems * esz, dt, res)
        return subs, whole, off

    def bank(self):
        b = self._bank
        self._bank = (self._bank + 1) % 4
        return self.PSB[b]

    def mm(self, out, lhsT, rhs, start=True, stop=True):
        self.sc.add("pe", lambda e, o=out.ap, l=lhsT.ap, r=rhs.ap: e.matmul(o, lhsT=l, rhs=r, start=start, stop=stop),
                    reads=[lhsT, rhs], writes=[out])

    def tr(self, out, in_, ident):
        self.sc.add("pe", lambda e, o=out.ap, i=in_.ap, d=ident.ap: e.transpose(o, i, d),
                    reads=[in_, ident], writes=[out])

    def act(self, out, in_, func, bias=None, scale=None, eng="act"):
        reads = [in_]
        kw = {}
        if bias is not None:
            if isinstance(bias, V):
                reads.append(bias)
                kw["bias"] = bias.ap
            else:
                kw["bias"] = bias
        if scale is not None:
            if isinstance(scale, V):
                reads.append(scale)
                kw["scale"] = scale.ap
            else:
                kw["scale"] = scale
        self.sc.add("act", lambda e, o=out.ap, i=in_.ap: e.activation(o, i, func, **kw), reads=reads, writes=[out])

    def tt(self, eng, out, in0, in1, op):
        self.sc.add(eng, lambda e, o=out.ap, a=in0.ap, b=in1.ap: e.tensor_tensor(o, a, b, op),
                    reads=[in0, in1], writes=[out])

    def ts(self, eng, out, in0, s1, op0, s2=None, op1=None):
        reads = [in0]
        a1 = s1
        if isinstance(s1, V):
            reads.append(s1)
            a1 = s1.ap
        a2 = s2
        if isinstance(s2, V):
            reads.append(s2)
            a2 = s2.ap
        if op1 is None:
            self.sc.add(eng, lambda e, o=out.ap, a=in0.ap: e.tensor_scalar(o, a, a1, None, op0), reads=reads, writes=[out])
        else:
            self.sc.add(eng, lambda e, o=out.ap, a=in0.ap: e.tensor_scalar(o, a, a1, a2, op0, op1), reads=reads, writes=[out])

    def stt(self, eng, out, in0, scalar, in1, op0, op1):
        reads = [in0, in1]
        sa = scalar
        if isinstance(scalar, V):
            reads.append(scalar)
            sa = scalar.ap
        self.sc.add(eng, lambda e, o=out.ap, a=in0.ap, b=in1.ap: e.scalar_tensor_tensor(o, a, sa, b, op0, op1),
                    reads=reads, writes=[out])

    def copy(self, eng, out, in_):
        if eng == "act":
            self.sc.add(eng, lambda e, o=out.ap, i=in_.ap: e.copy(o, i), reads=[in_], writes=[out])
        else:
            self.sc.add(eng, lambda e, o=out.ap, i=in_.ap: e.tensor_copy(o, i), reads=[in_], writes=[out])

    def memset(self, eng, out, val):
        self.sc.add(eng, lambda e, o=out.ap: e.memset(o, val), writes=[out])

    def reduce_sum(self, eng, out, in_):
        self.sc.add(eng, lambda e, o=out.ap, i=in_.ap: e.tensor_reduce(o, i, AX.X, ALU.add), reads=[in_], writes=[out])

    def recip(self, out, in_):
        self.sc.add("dve", lambda e, o=out.ap, i=in_.ap: e.reciprocal(o, i), reads=[in_], writes=[out])

    def dma(self, eng, out, in_, chan, reads=(), writes=(), nowaw=False, slow=False):
        kw = {"allow_slow_non_contiguous": True} if slow else {}
        self.sc.add(eng, lambda e, o=out, i=in_: e.dma_start(out=o, in_=i, **kw), reads=reads, writes=writes,
                    chan=chan, nowaw=nowaw)

    def build(self):
        from contextlib import ExitStack
        nc = self.nc
        dt = nc.dram_tensor

        def din(name, shape):
            return dt(name, shape, F32, kind="ExternalInput").ap()

        x = din("x", [S, D])
        p = din("p", [2, TOK_OWN + 128, PLE])
        norm_mix = din("norm_mix", [2, D])
        norm_mlp = din("norm_mlp", [2, D])
        norm_ple = din("norm_ple", [2, D])
        norm_final = din("norm_final", [D])
        w_in = din("mlstm_w_in", [D, 3088])
        b_gates = din("mlstm_b_gates", [16])
        head_norm = din("mlstm_head_norm", [D])
        w_out = din("mlstm_w_out", [D, D])
        pool_w_in = din("pool_w_in", [D, D])
        pool_w_grp = din("pool_w_grp", [4, 256, 256])
        pool_scale = din("pool_scale", [D])
        pool_w_out = din("pool_w_out", [D, D])
        mlp_w1 = din("mlp_w1", [2, D, DFF])
        mlp_w2 = din("mlp_w2", [2, DFF, D])
        ple_w = din("ple_w", [2, PLE, D])
        ple_gate_w = din("ple_gate_w", [2, D, D])
        ple_gate_b = din("ple_gate_b", [2, D])
        ptab = din("ptab", [16, 128, 128])
        out = dt("out", [TOK_OWN, D], F32, kind="ExternalOutput").ap()
        self.dbg = None
        if self.debug:
            self.dbg = dt("dbg", [16, 128, 4096], F32, kind="ExternalOutput").ap()

        NSLAB = 64
        wscr = dt("wscr", [NSLAB, 128, 4096], BF16, kind="Internal").ap()
        snap = dt("snap", [9, 128, 4 * 257], F32, kind="Internal").ap()
        self.kvscr = dt("kvscr", [33, 128, 512 + 4 * 257], BF16, kind="Internal").ap()

        st = ExitStack()
        self.st = st
        with st:
            SBYTES = 212000
            sbt = st.enter_context(nc.sbuf_tensor("sb", [128, SBYTES // 4], F32))
            self.SB = sbt[:, :]
            pst = st.enter_context(nc.psum_tensor("ps", [128, 8, 512], F32))
            self.PS = pst
            self.PSB = [V([Res("bank%d" % i, excl=True)], pst[:, i, :]) for i in range(8)]
            try:
                self._build_body(x, p, norm_mix, norm_mlp, norm_ple, norm_final, w_in, b_gates, head_norm, w_out,
                                 pool_w_in, pool_w_grp, pool_scale, pool_w_out, mlp_w1, mlp_w2, ple_w, ple_gate_w,
                                 ple_gate_b, ptab, out, wscr, snap)
            except _Stop:
                pass
            assert self.sb_off <= SBYTES, self.sb_off
            self.sc.emit(nc, st)
        return nc

    def _build_body(self, x, p, norm_mix, norm_mlp, norm_ple, norm_final, w_in, b_gates, head_norm, w_out,
                    pool_w_in, pool_w_grp, pool_scale, pool_w_out, mlp_w1, mlp_w2, ple_w, ple_gate_w,
                    ple_gate_b, ptab, out, wscr, snap):
        B = self
        sc = self.sc
        PSB = self.PSB

        slabs = []

        def slab_cols(key, W, c0, w):
            K = W.shape[0]
            kc = K // 128
            pieces = []
            if kc > 8:
                for k0 in range(0, kc, 8):
                    pieces.append((k0 * w, 8, w, W[k0 * 128:(k0 + 8) * 128, c0:c0 + w].rearrange("(kc p) w -> p kc w", p=128)))
            else:
                pieces.append((0, kc, w, W[:, c0:c0 + w].rearrange("(kc p) w -> p kc w", p=128)))
            slabs.append((key, pieces, kc * w, kc, w))

        slab_cols("in_k", w_in, 512, 512)
        slab_cols("in_v0", w_in, 1024, 512)
        slab_cols("in_v1", w_in, 1536, 512)
        slab_cols("in_g", w_in, 3072, 16)
        slab_cols("in_q", w_in, 0, 512)
        slab_cols("in_o0", w_in, 2048, 512)
        slab_cols("in_o1", w_in, 2560, 512)
        slab_cols("out0", w_out, 0, 512)
        slab_cols("out1", w_out, 512, 512)
        for l in range(2):
            for j in range(8):
                slab_cols("w1_%d_%d" % (l, j), mlp_w1[l], j * 512, 512)
            for j in range(8):
                slab_cols("w2_%d_%d" % (l, j), mlp_w2[l], j * 128, 128)
            slab_cols("pg_%d_0" % l, ple_gate_w[l], 0, 512)
            slab_cols("pg_%d_1" % l, ple_gate_w[l], 512, 512)
            slab_cols("pw_%d" % l, ple_w[l], 0, 1024)
        slab_cols("pi0", pool_w_in, 0, 512)
        slab_cols("pi1", pool_w_in, 512, 512)
        slabs.append(("grp", [(0, 8, 256, pool_w_grp.rearrange("g (kc p) w -> p (g kc) w", p=128))], 2048, 8, 256))
        slab_cols("po0", pool_w_out, 0, 512)
        slab_cols("po1", pool_w_out, 512, 512)
        slab_idx = {s[0]: i for i, s in enumerate(slabs)}
        assert len(slabs) <= 64
        slab_res = [Res("slab_" + s[0]) for s in slabs]
        NGRP = 8
        grp_res = [Res("pgrp%d" % i) for i in range(NGRP)]

        def grp_of(i):
            if i < 4:
                return 0
            return 1 + min(NGRP - 2, (i - 4) * (NGRP - 1) // (len(slabs) - 4))

        def emit_prologue(i0=0, i1=None, pace=None):
            for i, (key, pieces, nel, kc, w) in enumerate(slabs):
                if i < i0 or (i1 is not None and i >= i1):
                    continue
                g = grp_res[grp_of(i)]
                for (eo, kcn, ww, src) in pieces:
                    dst = wscr[i, :, eo:eo + kcn * ww].rearrange("p (kc w) -> p kc w", w=ww)
                    B.dma("pool", dst, src, g, reads=([pace] if pace is not None else []), writes=[V([g], None)], nowaw=True)

        c_id32, _, _ = B.tile("id32", F32, 128)
        c_idb, _, _ = B.tile("idb", BF16, 128)
        c_onesb, _, _ = B.tile("onesb", BF16, 128)
        c_ones32, _, _ = B.tile("ones32", F32, 128)
        c_triF, _, _ = B.tile("triF", F32, 128)
        c_triB, _, _ = B.tile("triB", F32, 128)
        c_mF, _, _ = B.tile("mF", BF16, 128)
        c_mB, _, _ = B.tile("mB", BF16, 128)
        c_pt, c_pt_all, _ = B.tile("ptab", BF16, 128, nsub=16)
        id32, idb, onesb, ones32, triF, triB, mF, mB = (c_id32[0], c_idb[0], c_onesb[0], c_ones32[0], c_triF[0],
                                                       c_triB[0], c_mF[0], c_mB[0])
        _, vecs, _ = B.tile("vecs", F32, 8 * 11)
        vecs3 = vecs.re("p (n c) -> p n c", c=8)
        _, vraw, _ = B.tile("vraw", F32, 8 * 11)
        vraw3 = vraw.re("p (n c) -> p n c", c=8)
        _, bg, _ = B.tile("bg", F32, 16)
        _, hnb, _ = B.tile("hnb", BF16, 1024)
        _, ones4, _ = B.tile("ones4", F32, 4)
        _, mhalf, _ = B.tile("mhalf", F32, 1)
        _, rcol, _ = B.tile("rcol", F32, 4)
        _, rcolH, _ = B.tile("rcolH", F32, 4)
        _, rcolP, _ = B.tile("rcolP", F32, 4)
        pace_t = [B.tile("pace%d" % i, F32, 1)[1] for i in range(16)]

        hbuf = [B.tile("hA", F32, 512, nsub=8), B.tile("hB", F32, 512, nsub=8)]
        xs = [B.tile("xs%d" % i, F32, 1024)[0][0] for i in range(2)]
        pstg = [B.tile("pstg%d" % i, F32, 256)[0][0] for i in range(2)]
        pT = [B.tile("pT%d" % i, BF16, 512, nsub=2)[0] for i in range(2)]
        xnT, xnT_all, _ = B.tile("xnT", BF16, 512, nsub=8)
        sq, sq_all, _ = B.tile("sq", BF16, 512, nsub=8)
        _, rb, _ = B.tile("rb", F32, 512)
        xnH, _, _ = B.tile("xnH", BF16, 128, nsub=8)
        _, ostage_full, _ = B.tile("ostage", F32, 1028)
        ostage = ostage_full[:, 0:1024]
        ost_res = [Res("ostA"), Res("ostB")]
        ost_half = [V([ost_res[0]], ostage_full.ap[:, 0:512]), V([ost_res[1]], ostage_full.ap[:, 512:1024])]
        E32 = [B.tile("E32_%d" % d, F32, 4 * 257)[1] for d in range(2)]
        Cbf = [B.tile("Cbf_%d" % d, BF16, 4 * 257)[1] for d in range(2)]
        _, dprevF, _ = B.tile("dprevF", F32, 4)
        snapstg = V(ost_res, ostage_full.ap)
        ring = [B.tile("ring%d" % i, BF16, 4096)[1] for i in range(4)]
        arena_off = self.sb_off
        GR = 1024
        ARENA = 212000 - arena_off
        ARENA = ARENA // GR * GR
        ngr = ARENA // GR
        ares = [Res("ar%d" % i) for i in range(ngr)]
        self.sb_off += ARENA

        class Arena:
            def __init__(s):
                s.off = 0

            def reset(s, o=0):
                s.off = o

            def at(s, o, nbytes, dtp, shape=None):
                g0 = o // GR
                g1 = (o + nbytes - 1) // GR
                v = B.view(arena_off + o, nbytes, dtp, ares[g0:g1 + 1])
                if shape is not None:
                    v = v.re(shape[0], **shape[1])
                return v

            def get(s, nbytes, dtp, shape=None):
                nb = (nbytes + 31) // 32 * 32
                o = s.off
                s.off += nb
                assert s.off <= ARENA, (s.off, ARENA)
                g0 = o // GR
                g1 = (o + nb - 1) // GR
                v = B.view(arena_off + o, nbytes, dtp, ares[g0:g1 + 1])
                if shape is not None:
                    v = v.re(shape[0], **shape[1])
                return v

        ar = Arena()
        qT = [ar.get(512 * 2, BF16) for _ in range(4)]
        kT = [ar.get(512 * 2, BF16) for _ in range(4)]
        ktok = [ar.get(512 * 2, BF16) for _ in range(4)]
        vext = [ar.get(4 * 257 * 2, BF16, ("p (h c) -> p h c", dict(c=257))) for _ in range(4)]
        og = [ar.get(1024 * 2, BF16) for _ in range(4)]
        sTm_off = ar.off
        sTm = [[ar.get(512 * 2, BF16) for _ in range(4)] for _ in range(2)]
        vp = [[ar.get(4 * 257 * 2, BF16, ("p (h c) -> p h c", dict(c=257))) for _ in range(2)] for _ in range(2)]
        hs = [ar.get(1024 * 4, F32) for _ in range(4)]
        tmpA = ar.get(1024 * 4, F32)
        hg = [ar.get(1024 * 2, BF16) for _ in range(2)]
        hg_extra = hg
        hgT_off = ar.off
        hgT = [ar.get(512 * 2, BF16) for _ in range(8)]
        hgT_all3 = ar.at(hgT_off, 8 * 1024, BF16, ("p (c t) -> p c t", dict(t=512)))
        gall = ar.get(4 * 16 * 4, F32, ("p (c g) -> p c g", dict(g=16)))
        sm = {}
        for nm in ("ef", "sp", "warg", "w", "thr", "dec"):
            sm[nm] = ar.get(32 * 4, F32, ("p (c d h) -> p c d h", dict(d=2, h=4)))
        rr = [ar.get(4 * 4, F32) for _ in range(2)]
        ssq4 = ar.get(16 * 4, F32, ("p (c h) -> p c h", dict(h=4)))
        rn4 = ar.get(16 * 4, F32, ("p (c h) -> p c h", dict(h=4)))
        mixer_end = ar.off
        ar.reset(0)
        h1T = [ar.get(512 * 2, BF16) for _ in range(32)]
        rtmp = [ar.get(512 * 4, F32) for _ in range(2)]
        gt = [ar.get(512 * 2, BF16) for _ in range(8)]
        mlp_end = ar.off
        ar.reset(0)
        pre_k = ar.get(4096 * 2, BF16)
        pre_v0 = ar.get(4096 * 2, BF16)
        pre_v1 = ar.get(4096 * 2, BF16)
        pre_g = ar.get(128 * 2, BF16)
        pp_ktok2 = [[ar.get(512 * 2, BF16) for _ in range(4)] for _ in range(2)]
        pp_vext2 = [[ar.get(4 * 257 * 2, BF16, ("p (h c) -> p h c", dict(c=257))) for _ in range(4)] for _ in range(2)]
        pp_vp = [ar.get(4 * 257 * 2, BF16, ("p (h c) -> p h c", dict(c=257))) for _ in range(2)]
        pp_gall2 = [ar.get(4 * 16 * 4, F32, ("p (c g) -> p c g", dict(g=16))) for _ in range(2)]
        pp_sm2 = []
        for _ in range(2):
            d_ = {}
            for nm in ("ef", "sp", "warg", "w", "thr", "dec"):
                d_[nm] = ar.get(32 * 4, F32, ("p (c d h) -> p c d h", dict(d=2, h=4)))
            pp_sm2.append(d_)
        pp_xn2 = [ar.get(512 * 2, BF16) for _ in range(8)]
        ar.reset(0)
        u = [ar.get(1024 * 2, BF16) for _ in range(6)]
        yT = [ar.get(512 * 2, BF16) for _ in range(8)]
        zT = [ar.get(512 * 2, BF16) for _ in range(8)]
        ar.reset(max(mixer_end, mlp_end, ar.off))
        uprev = ar.get(1024 * 2, BF16)

        ps7 = self.PS[:, 7, :]
        ps6 = self.PS[:, 6, :]
        r7 = PSB[7].res
        r6 = PSB[6].res
        p_cum = V(r7, ps7[:, 0:32].rearrange("p (d c h) -> p d c h", d=2, h=4))
        p_tot = V(r7, ps7[:, 32:64].rearrange("p (d c h) -> p d c h", d=2, h=4))
        p_den = [V(r7, ps7[:, 64 + 4 * d:68 + 4 * d]) for d in range(2)]
        p_dn = [V(r7, ps7[:, 72 + 4 * d:76 + 4 * d]) for d in range(2)]
        PS_ST = PSB[6]
        PS_O = [V(PSB[0].res + PSB[1].res, self.PS[:, 0:2, :]), V(PSB[2].res + PSB[3].res, self.PS[:, 2:4, :])]
        PS_DC = [PSB[4], PSB[5]]

        B.memset("pool", ones32, 1.0)
        B.memset("pool", onesb, 1.0)
        B.memset("pool", ones4, 1.0)
        B.memset("pool", mhalf, -0.5)
        B.memset("pool", id32, 1.0)
        sc.add("pool", lambda e, a=id32.ap: e.affine_select(out=a, in_=a, pattern=[[-1, 128]], compare_op=ALU.is_equal,
                                                          fill=0.0, base=0, channel_multiplier=1),
               reads=[id32], writes=[id32])
        B.copy("pool", idb, id32)
        B.memset("pool", triF, 1.0)
        sc.add("pool", lambda e, a=triF.ap: e.affine_select(out=a, in_=a, pattern=[[1, 128]], compare_op=ALU.is_ge,
                                                          fill=0.0, base=0, channel_multiplier=-1),
               reads=[triF], writes=[triF])
        B.memset("pool", triB, 1.0)
        sc.add("pool", lambda e, a=triB.ap: e.affine_select(out=a, in_=a, pattern=[[-1, 128]], compare_op=ALU.is_ge,
                                                          fill=0.0, base=0, channel_multiplier=1),
               reads=[triB], writes=[triB])
        B.copy("pool", mF, triF)
        B.copy("pool", mB, triB)
        B.dma("pool", c_pt_all.re("p (n c) -> p n c", c=128).ap, ptab.rearrange("n p c -> p n c"), c_pt_all.res[0], writes=[c_pt_all])
        def emit_vec_loads():
            vec_srcs = [norm_mix[0], norm_mix[1], norm_mlp[0], norm_mlp[1], norm_ple[0], norm_ple[1], norm_final,
                        head_norm, pool_scale, ple_gate_b[0], ple_gate_b[1]]
            for i, vsrc in enumerate(vec_srcs):
                B.dma("sp", vraw3.ap[:, i, :], vsrc.rearrange("(c p) -> p c", p=128), vraw.res[0], writes=[vraw], nowaw=True, slow=True)
            B.dma("sp", bg.ap, b_gates.partition_broadcast(128), bg.res[0], writes=[bg], slow=True)
            B.dma("pool", hnb.ap, head_norm.partition_broadcast(128), hnb.res[0], writes=[hnb], slow=True)
            B.ts("dve", vecs3[:, 0:7, :], vraw3[:, 0:7, :], 1.0, ALU.mult)
            B.ts("dve", vecs3[:, 7:8, :], vraw3[:, 7:8, :], 1.0, ALU.mult)
            B.ts("dve", vecs3[:, 8:11, :], vraw3[:, 8:11, :], 1.0, ALU.mult)
        for d in range(2):
            B.memset("pool", E32[d], 0.0)
            B.memset("pool", Cbf[d], 0.0)
        B.copy("pool", dprevF, ones4)
        if self.do_prepass and self.pre_blocks == 15:
            emit_prologue(0, 4)
        else:
            emit_prologue()
        if self.stop == "init":
            B.dump(id32, 128)
            B.dump(triF, 128)
            B.dump(triB, 128)
            B.dump(mF, 128, bf16=True)
            B.dump(vecs, 88)
            B.dump(bg, 16)
            B.dump(c_pt_all, 2048, bf16=True)

        ring_state = {"n": 0}

        def use_slab(key):
            i = slab_idx[key]
            nel = slabs[i][2]
            slot = ring[ring_state["n"] % 4]
            ring_state["n"] += 1
            g = grp_res[grp_of(i)]
            B.dma("sp", slot.ap[:, 0:nel], wscr[i, :, 0:nel], slot.res[0], reads=[V([g], None)], writes=[slot])
            kc, w = slabs[i][3], slabs[i][4]
            return slot[:, 0:nel].re("p (kc w) -> p kc w", w=w)

        def load_resident(key, dstv):
            i = slab_idx[key]
            nel = slabs[i][2]
            g = grp_res[grp_of(i)]
            B.dma("sp", dstv.ap[:, 0:nel], wscr[i, :, 0:nel], dstv.res[0], reads=[V([g], None)], writes=[dstv])
            kc, w = slabs[i][3], slabs[i][4]
            return dstv[:, 0:nel].re("p (kc w) -> p kc w", w=w)

        if self.stop == "init":
            for key in ("in_k", "w2_0_3", "grp", "pw_1"):
                sl = use_slab(key)
                i = slab_idx[key]
                B.dump(V(sl.res, ring[(ring_state["n"] - 1) % 4].ap), 4096, bf16=True)
            raise _Stop()
        xseq = []
        if self.do_prepass:
            for pb_ in range(self.pre_blocks, 0, -1):
                xseq += [pb_ * 512 + c_ * 128 for c_ in range(4)]
        for blk_ in range(self.n_main + 1):
            xseq += [blk_ * 512 + c_ * 128 for c_ in range(4 if blk_ < self.n_main else 1)]
        xq = {"issued": 0, "used": 0}

        def x_issue():
            i = xq["issued"]
            if i >= len(xseq):
                return
            xb = xs[i % 2]
            t0_ = xseq[i]
            B.dma("sp", xb.ap, x[t0_: t0_ + 128, :], xb.res[0], writes=[xb])
            xq["issued"] = i + 1

        x_issue()
        x_issue()
        emit_vec_loads()

        def load_block_T(hT, tok0, nch, hwhole):
            hw3 = hwhole.re("p (c t) -> p c t", t=512)
            for ch in range(nch):
                i = xq["used"]
                assert xseq[i] == tok0 + ch * 128, (xseq[i], tok0, ch)
                while xq["issued"] < i + 1:
                    x_issue()
                xb = xs[i % 2]
                xq["used"] = i + 1
                for half in range(2):
                    bk = B.bank()
                    for c4 in range(4):
                        c = half * 4 + c4
                        B.tr(bk[:, c4 * 128:(c4 + 1) * 128], xb[:, c * 128:(c + 1) * 128], id32)
                    eng = "act" if half == 0 else "dve"
                    dstv = V(sum([hT[half * 4 + c4].res for c4 in range(4)], []),
                             hw3.ap[:, half * 4:half * 4 + 4, ch * 128:(ch + 1) * 128])
                    B.copy(eng, dstv, bk.re("p (c t) -> p c t", t=128))
                while xq["issued"] < min(len(xseq), i + 3):
                    x_issue()

        def norm_piece(hT, gi, T, dst, c):
            B.ts("dve", dst[c][:, 0:T], hT[c][:, 0:T], vecs3[:, gi, c:c + 1], ALU.mult)
            B.act(sq[c][:, 0:T], hT[c][:, 0:T], AF.Square)

        def norm(hT, gi, T, dst, want_rb=True, rc=None, pieces_done=False, defer=False):
            nch = T // 128
            if not pieces_done:
                for c in range(8):
                    norm_piece(hT, gi, T, dst, c)

            def stats():
                if want_rb:
                    bk = B.bank()
                    for c in range(8):
                        B.mm(bk[:, 0:T], onesb, sq[c][:, 0:T], start=(c == 0), stop=(c == 7))
                    B.act(rb[:, 0:T], bk[:, 0:T], AF.Sqrt, bias=EPS, scale=1.0 / 1024.0)
                    B.recip(rb[:, 0:T], rb[:, 0:T])
                if rc is not None:
                    bk = B.bank()
                    for ch in range(nch):
                        for c in range(8):
                            B.mm(bk[:, ch:ch + 1], sq[c][:, ch * 128:(ch + 1) * 128], onesb[:, 0:1], start=(c == 0), stop=(c == 7))
                    B.act(rc[:, 0:nch], bk[:, 0:nch], AF.Identity, bias=EPS, scale=1.0 / 1024.0)
                    B.tt("pool", rc[:, 0:nch], rc[:, 0:nch], V(mhalf.res, mhalf.ap.to_broadcast([128, nch])), ALU.pow)
            if defer:
                return stats
            stats()

        def linA(slabv, m0, nm, rhs_list, T, evac):
            kcn = len(rhs_list)
            for m in range(nm):
                bk = B.bank()
                for kc in range(kcn):
                    B.mm(bk[:, 0:T], slabv[:, kc, m * 128:(m + 1) * 128], rhs_list[kc][:, 0:T],
                         start=(kc == 0), stop=(kc == kcn - 1))
                evac(m0 + m, bk[:, 0:T])

        def linB(slabv, lhs_list, ch, ncols, evac, bk=None):
            kcn = len(lhs_list)
            if bk is None:
                bk = B.bank()
            for kc in range(kcn):
                B.mm(bk[:, 0:ncols], lhs_list[kc][:, ch * 128:(ch + 1) * 128], slabv[:, kc, 0:ncols],
                     start=(kc == 0), stop=(kc == kcn - 1))
            evac(bk[:, 0:ncols])

        def gate_prep(gallv, smd, nch):
            g5 = gallv.re("p c (d k h) -> p c d k h", d=2, k=2)
            ipre = g5[:, 0:nch, :, 0, :]
            fpre = g5[:, 0:nch, :, 1, :]
            ef, sp_, warg, w_, thr, dec = (smd[k][:, 0:nch] for k in ("ef", "sp", "warg", "w", "thr", "dec"))
            B.act(ef, fpre, AF.Exp, scale=-1.0)
            B.act(sp_, ef, AF.Ln, bias=1.0)
            cum = V(p_cum.res, p_cum.ap[:, :, 0:nch, :])
            tot = V(p_tot.res, p_tot.ap[:, :, 0:nch, :])
            for d in range(2):
                B.mm(cum[:, d], triF if d == 0 else triB, sp_[:, :, d, :])
                B.mm(tot[:, d], ones32, sp_[:, :, d, :])
            cum_cdh = V(cum.res, cum.ap.rearrange("p d c h -> p c d h"))
            tot_cdh = V(tot.res, tot.ap.rearrange("p d c h -> p c d h"))
            B.tt("dve", warg, ipre, cum_cdh, ALU.add)
            B.act(w_, warg, AF.Exp, bias=LN_KSCALE)
            B.act(thr, cum_cdh, AF.Exp)
            B.act(dec, tot_cdh, AF.Exp, scale=-1.0)

        def state_mm(d, kt, vpv):
            for hp in range(2):
                bk = PS_DC[hp]
                for h2 in range(2):
                    h = hp * 2 + h2
                    B.mm(bk[:, h2 * 256:(h2 + 1) * 256], kt[:, h * 128:(h + 1) * 128], vpv[:, h, 0:256])
                    B.mm(p_dn[d][:, h:h + 1], kt[:, h * 128:(h + 1) * 128], vpv[:, h, 256:257])

        def state_dve(d, dprev):
            E3 = E32[d].re("p (h c) -> p h c", c=257)
            for hp in range(2):
                bk = PS_DC[hp]
                for h2 in range(2):
                    h = hp * 2 + h2
                    B.stt("dve", E3[:, h, 0:256], E3[:, h, 0:256], dprev[:, h:h + 1], bk[:, h2 * 256:(h2 + 1) * 256],
                          ALU.mult, ALU.add)
            En = E3[:, :, 256]
            B.tt("dve", En, En, dprev, ALU.mult)
            B.tt("dve", En, En, p_dn[d], ALU.add)

        def state_update(d, kt, vpv, dprev):
            state_mm(d, kt, vpv)
            state_dve(d, dprev)

        snapd = [Res("snapd%d" % i) for i in range(9)]
        kvscr = self.kvscr
        kvdk = [Res("kvdk%d" % i) for i in range(33)]
        kvdv = [Res("kvdv%d" % i) for i in range(33)]
        kvstk_ch = [[Res("kvstk%d_%d" % (a_, b_)) for b_ in range(4)] for a_ in range(2)]
        kvstv_ch = [[Res("kvstv%d_%d" % (a_, b_)) for b_ in range(4)] for a_ in range(2)]
        kvldk_ch = [Res("kvldk%d" % b_) for b_ in range(4)]
        kvldv_ch = [Res("kvldv%d" % b_) for b_ in range(4)]
        use_kv_cache = self.do_prepass
        if self.do_prepass:
            wk = load_resident("in_k", pre_k)
            wv0 = load_resident("in_v0", pre_v0)
            wv1 = load_resident("in_v1", pre_v1)
            wg = load_resident("in_g", pre_g)
            for par in range(2):
                for i in range(4):
                    B.memset("pool", pp_vext2[par][i][:, :, 256:257], 1.0)
            sc.tag = "pre"
            pxn = [xnT, pp_xn2]
            prc = [rcol, rcolP]
            pst = {"dprev": ones4}

            def stageA(pb, defer=False):
                par = pb % 2
                load_block_T(hbuf[par][0], pb * 512, 4, hbuf[par][1])
                return norm(hbuf[par][0], 0, 512, pxn[par], want_rb=False, rc=prc[par], defer=defer)

            def stageB_chunk(pb, ch):
                par = pb % 2
                xn_ = pxn[par]
                rs = prc[par][:, ch:ch + 1]
                kt_, ve_, ga_ = pp_ktok2[par], pp_vext2[par], pp_gall2[par]
                linB(wk, xn_, ch, 512, lambda b_: B.act(kt_[ch], b_, AF.Copy, scale=rs))
                linB(wv0, xn_, ch, 512, lambda b_: B.ts("dve", ve_[ch][:, 0:2, 0:256], b_.re("p (h c) -> p h c", c=256), rs, ALU.mult))
                linB(wv1, xn_, ch, 512, lambda b_: B.act(ve_[ch][:, 2:4, 0:256], b_.re("p (h c) -> p h c", c=256), AF.Copy, scale=rs))
                bk = B.bank()
                for kc in range(8):
                    B.mm(bk[:, 0:16], xn_[kc][:, ch * 128:(ch + 1) * 128], wg[:, kc, :], start=(kc == 0), stop=(kc == 7))
                B.stt("dve", ga_[:, ch, :], bk[:, 0:16], rs, bg, ALU.mult, ALU.add)
                cg = pb * 4 + ch
                if cg <= 4 * self.n_main:
                    B.dma("sp", kvscr[cg, :, 0:512], kt_[ch].ap, kvstk_ch[par][ch], reads=[kt_[ch]], writes=[V([kvdk[cg]], None)])
                    B.dma("sp", kvscr[cg, :, 512:1540], ve_[ch].ap.rearrange("p h c -> p (h c)"), kvstv_ch[par][ch],
                          reads=[ve_[ch]], writes=[V([kvdv[cg]], None)])

            def stageC_vp(pb, ch):
                par = pb % 2
                vpv = pp_vp[ch % 2]
                wv_ = pp_sm2[par]["w"][:, ch, 1, :]
                B.tt("dve", vpv, pp_vext2[par][ch], V(wv_.res, wv_.ap.unsqueeze(2).to_broadcast([128, 4, 257])), ALU.mult)

            def stageC_chunk(pb, ch):
                par = pb % 2
                cg = pb * 4 + ch
                vpv = pp_vp[ch % 2]
                state_update(1, pp_ktok2[par][ch], vpv, pst["dprev"])
                pst["dprev"] = pp_sm2[par]["dec"][:, ch, 1, :]
                si = None
                if cg == 4 * self.n_main + 1:
                    si = 8
                elif cg % 4 == 0 and cg <= 4 * self.n_main:
                    si = cg // 4 - 1
                if si is not None:
                    E3 = E32[1].re("p (h c) -> p h c", c=257)
                    S3 = snapstg.re("p (h c) -> p h c", c=257)
                    dpv = pst["dprev"]
                    for h in range(4):
                        B.act(S3[:, h, :], E3[:, h, :], AF.Copy, scale=dpv[:, h:h + 1])
                    B.dma("sp", snap[si], snapstg.ap, snapstg.res[0], reads=[snapstg], writes=[V([snapd[si]], None)])

            PB = self.pre_blocks
            stageA(PB)
            if PB - 1 >= 1:
                stageA(PB - 1)
            for ch in range(4):
                stageB_chunk(PB, ch)
            gate_prep(pp_gall2[PB % 2], pp_sm2[PB % 2], 4)
            stageC_vp(PB, 3)
            nsl = len(slabs)
            for pb in range(PB - 1, 0, -1):
                fin = None
                if pb - 1 >= 1:
                    fin = stageA(pb - 1, defer=True)
                if PB == 15:
                    k = PB - 1 - pb
                    pc = pace_t[k]
                    B.memset("dve", pc, 0.0)
                    i0_ = 4 + 4 * k
                    i1_ = nsl if pb == 1 else min(nsl, 4 + 4 * (k + 1))
                    emit_prologue(i0_, i1_, pace=pc)
                for i in range(4):
                    stageC_chunk(pb + 1, 3 - i)
                    if i < 3:
                        stageC_vp(pb + 1, 2 - i)
                    stageB_chunk(pb, i)
                    if i == 0 and fin is not None:
                        fin()
                gate_prep(pp_gall2[pb % 2], pp_sm2[pb % 2], 4)
                stageC_vp(pb, 3)
            for i in range(4):
                stageC_chunk(1, 3 - i)
                if i < 3:
                    stageC_vp(1, 2 - i)
            if self.stop == "prepass":
                dpv = pst["dprev"]
                E3 = E32[1].re("p (h c) -> p h c", c=257)
                B.tt("dve", E3, E3, V(dpv.res, dpv.ap.unsqueeze(2).to_broadcast([128, 4, 257])), ALU.mult)
                B.dump(E32[1], 1028)
                raise _Stop()

        def mlp_and_ple(hT, layer, T, tok0, nch, pTl, pstage_tok0):
            sc.tag = "L%d.mlp" % layer
            norm(hT, 2 + layer, T, xnT, pieces_done=True)
            def p_load(ch):
                pb_ = pstg[ch % 2]
                B.dma("pool", pb_.ap, p[layer, pstage_tok0 + ch * 128: pstage_tok0 + (ch + 1) * 128, :], pb_.res[0], writes=[pb_])

            def p_transpose(ch):
                pb_ = pstg[ch % 2]
                bk = B.bank()
                for c2 in range(2):
                    B.tr(bk[:, c2 * 128:(c2 + 1) * 128], pb_[:, c2 * 128:(c2 + 1) * 128], id32)
                for c2 in range(2):
                    B.copy("act", pTl[c2][:, ch * 128:(ch + 1) * 128], bk[:, c2 * 128:(c2 + 1) * 128])
            for ch in range(min(2, nch)):
                p_load(ch)
            B.check_stop("l0i2")
            for j in range(8):
                if j == 1:
                    B.check_stop("l0i3")
                if j == 4:
                    B.check_stop("l0i4")
                if j >= 5:
                    B.check_stop("l0i%d" % j)
                sl = use_slab("w1_%d_%d" % (layer, j))

                def ev(m, b_):
                    r_ = rtmp[m % 2]
                    B.stt("dve", r_[:, 0:T], b_, 0.0, rb[:, 0:T], ALU.max, ALU.mult)
                    B.act(h1T[m][:, 0:T], r_[:, 0:T], AF.Square)
                linA(sl, j * 4, 4, xnT, T, ev)
                if j == 1:
                    for ch in range(min(2, nch)):
                        p_transpose(ch)
                        if ch + 2 < nch:
                            p_load(ch + 2)
                if j == 3:
                    for ch in range(2, nch):
                        p_transpose(ch)
            B.check_stop("l0j")
            for j in range(8):
                sl = use_slab("w2_%d_%d" % (layer, j))
                def evw2(m, b_):
                    B.tt("dve", hT[m][:, 0:T], b_, hT[m][:, 0:T], ALU.add)
                    norm_piece(hT, 4 + layer, T, xnT, m)
                linA(sl, j, 1, h1T, T, evw2)
            B.check_stop("l0k")
            if self.stop == "l0b0" and layer == 0:
                B.dump(V(sum([h_.res for h_ in hT], []), hbuf[0][1].ap), 4096)
            sc.tag = "L%d.ple" % layer
            norm(hT, 4 + layer, T, xnT, pieces_done=True)
            B.check_stop("l0l")
            def evg(m, b_):
                t_ = rtmp[m % 2]
                B.tt("dve", t_[:, 0:T], b_, rb[:, 0:T], ALU.mult)
                B.act(gt[m][:, 0:T], t_[:, 0:T], AF.Sigmoid, bias=vecs3[:, 9 + layer, m:m + 1])

            def ev2(m, b_):
                t_ = rtmp[m % 2]
                B.tt("dve", t_[:, 0:T], b_, gt[m][:, 0:T], ALU.mult)
                B.tt("dve", hT[m][:, 0:T], t_[:, 0:T], hT[m][:, 0:T], ALU.add)
                if layer == 1:
                    B.act(sq[m][:, 0:T], hT[m][:, 0:T], AF.Square)
                    B.ts("dve", hT[m][:, 0:T], hT[m][:, 0:T], vecs3[:, 6, m:m + 1], ALU.mult)
            slg = [use_slab("pg_%d_0" % layer), None]
            slw = use_slab("pw_%d" % layer)
            for m in range(8):
                j = m // 4
                if slg[j] is None:
                    slg[j] = use_slab("pg_%d_%d" % (layer, j))
                bk = B.bank()
                for kc in range(8):
                    B.mm(bk[:, 0:T], slg[j][:, kc, (m % 4) * 128:(m % 4 + 1) * 128], xnT[kc][:, 0:T], start=(kc == 0), stop=(kc == 7))
                evg(m, bk[:, 0:T])
                bk2 = B.bank()
                for kc in range(2):
                    B.mm(bk2[:, 0:T], slw[:, kc, m * 128:(m + 1) * 128], pTl[kc][:, 0:T], start=(kc == 0), stop=(kc == 1))
                ev2(m, bk2[:, 0:T])
            B.check_stop("l0n")

        def layer0(hT, blk, nch, snap_i, first, hwhole):
            sc.tag = "L0.load"
            T = 128 * nch
            tok0 = blk * 512
            load_block_T(hT, tok0, nch, hwhole)
            if self.do_prepass:
                B.dma("sp", E32[1].ap, snap[snap_i], E32[1].res[0], reads=[V([snapd[snap_i]], None)], writes=[E32[1]])
            B.check_stop("l0a")
            norm(hT, 0, T, xnT, want_rb=True, rc=rcol)
            sc.tag = "L0.proj"
            B.check_stop("l0b")
            sl = use_slab("in_q")
            linA(sl, 0, 4, xnT, T, lambda m, b_: B.tt("dve", qT[m][:, 0:T], b_, rb[:, 0:T], ALU.mult))
            sl = use_slab("in_k")
            linA(sl, 0, 4, xnT, T, lambda m, b_: B.tt("dve", kT[m][:, 0:T], b_, rb[:, 0:T], ALU.mult))
            if use_kv_cache and blk >= 1:
                for ch in range(nch):
                    cg = blk * 4 + ch
                    B.dma("sp", ktok[ch].ap, kvscr[cg, :, 0:512], kvldk_ch[ch], reads=[V([kvdk[cg]], None)], writes=[ktok[ch]])
                    B.dma("sp", vext[ch].ap.rearrange("p h c -> p (h c)"), kvscr[cg, :, 512:1540], kvldv_ch[ch],
                          reads=[V([kvdv[cg]], None)], writes=[vext[ch]])
            else:
                for ch in range(nch):
                    linB(sl, xnT, ch, 512, lambda b_, ch=ch: B.act(ktok[ch], b_, AF.Copy, scale=rcol[:, ch:ch + 1]))
                sl = use_slab("in_v0")
                for ch in range(nch):
                    linB(sl, xnT, ch, 512, lambda b_, ch=ch: B.ts("dve", vext[ch][:, 0:2, 0:256], b_.re("p (h c) -> p h c", c=256), rcol[:, ch:ch + 1], ALU.mult))
                sl = use_slab("in_v1")
                for ch in range(nch):
                    linB(sl, xnT, ch, 512, lambda b_, ch=ch: B.act(vext[ch][:, 2:4, 0:256], b_.re("p (h c) -> p h c", c=256), AF.Copy, scale=rcol[:, ch:ch + 1]))
                for ch in range(nch):
                    B.memset("dve", vext[ch][:, :, 256:257], 1.0)
            B.check_stop("l0c")
            sl = use_slab("in_g")
            for ch in range(nch):
                bk = B.bank()
                for kc in range(8):
                    B.mm(bk[:, 0:16], xnT[kc][:, ch * 128:(ch + 1) * 128], sl[:, kc, :], start=(kc == 0), stop=(kc == 7))
                B.stt("dve", gall[:, ch, :], bk[:, 0:16], rcol[:, ch:ch + 1], bg, ALU.mult, ALU.add)
            B.check_stop("l0d")
            gate_prep(gall, sm, nch)
            B.check_stop("l0e")
            sc.tag = "L0.scores"
            for ch in range(nch):
                stb = PS_ST if ch % 2 == 0 else PSB[5]
                for h in range(4):
                    B.mm(stb[:, h * 128:(h + 1) * 128], kT[h][:, ch * 128:(ch + 1) * 128], qT[h][:, ch * 128:(ch + 1) * 128])
                st3 = stb.re("p (h t) -> p h t", t=128)
                B.tt("dve", sTm[0][ch].re("p (h t) -> p h t", t=128), st3,
                     V(mF.res, mF.ap.unsqueeze(1).to_broadcast([128, 4, 128])), ALU.mult)
                B.tt("dve", sTm[1][ch].re("p (h t) -> p h t", t=128), st3,
                     V(mB.res, mB.ap.unsqueeze(1).to_broadcast([128, 4, 128])), ALU.mult)
            B.check_stop("l0f")
            sc.tag = "L0.scan"
            dprev = [dprevF, ones4]
            slots = [(s_, d) for s_ in range(nch) for d in range(2)]

            def ch_of(s_, d):
                return s_ if d == 0 else nch - 1 - s_

            def emit_vp(s_, d):
                ch_ = ch_of(s_, d)
                vpv_ = vp[d][s_ % 2]
                wv_ = sm["w"][:, ch_, d, :]
                for h in range(4):
                    B.act(vpv_[:, h, :], vext[ch_][:, h, :], AF.Copy, scale=wv_[:, h:h + 1])

            def emit_c3(d, dpv):
                E3 = E32[d].re("p (h c) -> p h c", c=257)
                C3 = Cbf[d].re("p (h c) -> p h c", c=257)
                if d == 0:
                    B.tt("dve", C3, E3, V(dpv.res, dpv.ap.unsqueeze(2).to_broadcast([128, 4, 257])), ALU.mult)
                else:
                    for h in range(4):
                        B.act(C3[:, h, :], E3[:, h, :], AF.Copy, scale=dpv[:, h:h + 1])

            sqj2 = [tmpA.bitcast(BF16)[:, 0:1024], tmpA.bitcast(BF16)[:, 1024:2048]]
            sqj_sel = {"v": sqj2}

            def hnorm_chunks(chs, bank_override, part="all", hgsel=None):
                if part in ("all", "ew"):
                    hnorm_ew(chs, hgsel)
                if part in ("all", "pe"):
                    hnorm_pe(chs, bank_override, hgsel)

            def hnorm_ew(chs, hgsel):
                sqs = sqj_sel["v"]
                for i_, ch_ in enumerate(chs):
                    B.act(sqs[i_ % len(sqs)], hs[ch_], AF.Square)
                for i_, ch_ in enumerate(chs):
                    B.reduce_sum("dve", ssq4[:, ch_, :], sqs[i_ % len(sqs)].re("p (h c) -> p h c", c=256))
                if len(chs) == nch:
                    B.act(rn4[:, 0:nch, :], ssq4[:, 0:nch, :], AF.Identity, bias=EPS, scale=1.0 / 256.0)
                    B.tt("pool", rn4[:, 0:nch, :], rn4[:, 0:nch, :], V(mhalf.res, mhalf.ap.unsqueeze(2).to_broadcast([128, nch, 4])), ALU.pow)
                else:
                    for ch_ in chs:
                        B.act(rn4[:, ch_, :], ssq4[:, ch_, :], AF.Identity, bias=EPS, scale=1.0 / 256.0)
                        B.tt("pool", rn4[:, ch_, :], rn4[:, ch_, :], V(mhalf.res, mhalf.ap.to_broadcast([128, 4])), ALU.pow)
                for i_, ch_ in enumerate(chs):
                    hs4_ = hs[ch_].re("p (h c) -> p h c", c=256)
                    hg_ = (hgsel or hg)[i_ % len(hgsel or hg)]
                    for h in range(4):
                        B.stt("dve", hg_[:, h * 256:(h + 1) * 256], hs4_[:, h, :], rn4[:, ch_, h:h + 1],
                              og[ch_][:, h * 256:(h + 1) * 256], ALU.mult, ALU.mult)

            def hnorm_pe(chs, bank_override, hgsel):
                for i_, ch_ in enumerate(chs):
                    hg_ = (hgsel or hg)[i_ % len(hgsel or hg)]
                    for half in range(2):
                        bk = bank_override if bank_override is not None else B.bank()
                        bkb = bk.bitcast(BF16)
                        for c4 in range(4):
                            c = half * 4 + c4
                            B.tr(bkb[:, c4 * 128:(c4 + 1) * 128], hg_[:, c * 128:(c + 1) * 128], idb)
                        dstv = V(sum([hgT[half * 4 + c4].res for c4 in range(4)], []),
                                 hgT_all3.ap[:, half * 4:half * 4 + 4, ch_ * 128:(ch_ + 1) * 128])
                        B.copy("act" if half == 0 else "dve", dstv, bkb[:, 0:512].re("p (c t) -> p c t", t=128))

            emit_c3(0, dprev[0])
            emit_c3(1, dprev[1])
            emit_vp(*slots[0])
            done = set()
            if nch == 4:
                opieces = [(0, 1), (0, 2), (1, 1), (1, 2), (0, 3), (0, 0), (1, 3), (1, 0)]
            else:
                opieces = [(j, ch_) for j in range(2) for ch_ in range(nch)]
            oslab = {}
            for si_, (s_, d) in enumerate(slots):
                ch = ch_of(s_, d)
                vpv = vp[d][s_ % 2]
                C3 = Cbf[d].re("p (h c) -> p h c", c=257)
                O = PS_O[d]
                for h in range(4):
                    o_ = O[:, h // 2, (h % 2) * 256:(h % 2 + 1) * 256]
                    l1 = sTm[d][ch][:, h * 128:(h + 1) * 128]
                    l2 = qT[h][:, ch * 128:(ch + 1) * 128]
                    B.mm(o_, l1, vpv[:, h, 0:256], start=True, stop=False)
                    B.mm(o_, l2, C3[:, h, 0:256], start=False, stop=True)
                    dn_ = p_den[d][:, h:h + 1]
                    B.mm(dn_, l1, vpv[:, h, 256:257], start=True, stop=False)
                    B.mm(dn_, l2, C3[:, h, 256:257], start=False, stop=True)
                B.act(rr[d], p_den[d], AF.Abs)
                B.tt("dve", rr[d], rr[d], sm["thr"][:, ch, d, :], ALU.max)
                B.recip(rr[d], rr[d])
                state_mm(d, ktok[ch], vpv)
                if si_ + 1 < len(slots):
                    emit_vp(*slots[si_ + 1])
                oj, och = opieces[si_]
                if oj not in oslab:
                    oslab[oj] = use_slab("in_o%d" % oj)
                linB(oslab[oj], xnT, och, 512,
                     lambda b_, oj=oj, och=och: B.act(og[och][:, oj * 512:(oj + 1) * 512], b_, AF.Sigmoid, scale=rcol[:, och:och + 1]),
                     bk=PSB[6])
                if oj == 1:
                    B.tt("pool", og[och], og[och], hnb, ALU.mult)
                state_dve(d, dprev[d])
                dprev[d] = sm["dec"][:, ch, d, :]
                if s_ + 1 < nch:
                    emit_c3(d, dprev[d])
                O4 = O.re("p b (h c) -> p (b h) c", c=256)
                hs4 = hs[ch].re("p (h c) -> p h c", c=256)
                if ch not in done:
                    rbv = V(rr[d].res, rr[d].ap.unsqueeze(2).to_broadcast([128, 4, 256]))
                    B.tt("dve", hs4, O4, rbv, ALU.mult)
                    done.add(ch)
                else:
                    for h in range(4):
                        B.stt("dve", hs4[:, h, :], O4[:, h, :], rr[d][:, h:h + 1], hs4[:, h, :], ALU.mult, ALU.add)
            B.copy("dve", dprevF, dprev[0])
            B.check_stop("l0g")
            if self.stop == "l0b0" and blk == 0:
                for ch in range(nch):
                    B.dump(hs[ch], 1024)
            sc.tag = "L0.hnorm"
            sqj_sel["v"] = [V(vp[d_][i_].res, vp[d_][i_].ap.rearrange("p h c -> p (h c)")[:, 0:1024]) for d_ in range(2) for i_ in range(2)]
            hg4 = [hg[0], hg[1], ar.at(sTm_off, 2048, BF16), ar.at(sTm_off + 2048, 2048, BF16)]
            hnorm_chunks(([1, 2, 3, 0] if nch == 4 else list(range(nch))), None, hgsel=hg4)
            sc.tag = "L0.wout"
            B.check_stop("l0h")
            for j in range(2):
                sl = use_slab("out%d" % j)
                def evo(m, b_):
                    B.tt("dve", hT[m][:, 0:T], b_, hT[m][:, 0:T], ALU.add)
                    norm_piece(hT, 2, T, xnT, m)
                linA(sl, j * 4, 4, hgT, T, evo)
            B.check_stop("l0i")
            if self.stop == "l0b0" and blk == 0:
                B.dump(V(sum([h_.res for h_ in hT], []), hbuf[0][1].ap), 4096)
            mlp_and_ple(hT, 0, T, tok0, nch, pT[0], tok0)
            if self.stop == "l0b0" and blk == 0:
                B.dump(V(sum([h_.res for h_ in hT], []), hbuf[0][1].ap), 4096)
                raise _Stop()

        def layer1(hT, blk, hnext, first_chunk_edge):
            sc.tag = "L1.proj"
            T = 512
            tok0 = blk * 512
            norm(hT, 1, T, xnT, want_rb=False, rc=rcol)
            norm(hnext, 1, 128, xnH, want_rb=False, rc=rcolH)
            if blk > 0:
                B.copy("dve", u[0], uprev)
            for j in range(2):
                sl = use_slab("pi%d" % j)
                for ch in range(4):
                    def evu(b_, ch=ch, j=j):
                        if ch % 2 == 0:
                            B.act(u[1 + ch][:, j * 512:(j + 1) * 512], b_, AF.Copy, scale=rcol[:, ch:ch + 1])
                        else:
                            B.ts("dve", u[1 + ch][:, j * 512:(j + 1) * 512], b_, rcol[:, ch:ch + 1], ALU.mult)
                    linB(sl, xnT, ch, 512, evu)
                linB(sl, xnH, 0, 512, lambda b_, j=j: B.act(u[5][:, j * 512:(j + 1) * 512], b_, AF.Copy, scale=rcolH[:, 0:1]))
            B.copy("dve", uprev, u[4])
            sc.tag = "L1.pool"
            for m in range(8):
                g = m // 2
                bk = B.bank()
                for ch in range(4):
                    o_ = bk[:, ch * 128:(ch + 1) * 128]
                    fs = slice(m * 128, (m + 1) * 128)
                    if blk == 0 and ch == 0:
                        B.mm(o_, u[1][:, fs], c_pt[12 + g], start=True, stop=False)
                    else:
                        B.mm(o_, u[ch][:, fs], c_pt[g], start=True, stop=False)
                        B.mm(o_, u[1 + ch][:, fs], c_pt[4 + g], start=False, stop=False)
                    B.mm(o_, u[2 + ch][:, fs], c_pt[8 + g], start=False, stop=True)
                B.copy("act" if m % 2 == 0 else "dve", yT[m], bk)
            sl = use_slab("grp")
            for g in range(4):
                for mo in range(2):
                    bk = B.bank()
                    for kc in range(2):
                        B.mm(bk, sl[:, g * 2 + kc, mo * 128:(mo + 1) * 128], yT[2 * g + kc], start=(kc == 0), stop=(kc == 1))
                    B.act(zT[2 * g + mo], bk, AF.Copy, scale=vecs3[:, 8, 2 * g + mo:2 * g + mo + 1])
            for j in range(2):
                sl = use_slab("po%d" % j)
                def evo1(m, b_):
                    B.tt("dve", hT[m][:, 0:T], b_, hT[m][:, 0:T], ALU.add)
                    norm_piece(hT, 3, T, xnT, m)
                linA(sl, j * 4, 4, zT, T, evo1)
            mlp_and_ple(hT, 1, T, tok0, 4, pT[1], tok0)
            sc.tag = "L1.final"
            bk = B.bank()
            for ch in range(4):
                for c in range(8):
                    B.mm(bk[:, ch:ch + 1], sq[c][:, ch * 128:(ch + 1) * 128], onesb[:, 0:1], start=(c == 0), stop=(c == 7))
            B.act(rcol[:, 0:4], bk[:, 0:4], AF.Identity, bias=EPS, scale=1.0 / 1024.0)
            B.tt("pool", rcol[:, 0:4], rcol[:, 0:4], V(mhalf.res, mhalf.ap.to_broadcast([128, 4])), ALU.pow)
            for ch in range(4):
                for half in range(2):
                    bk = B.bank()
                    for c4 in range(4):
                        c = half * 4 + c4
                        B.tr(bk[:, c4 * 128:(c4 + 1) * 128], hT[c][:, ch * 128:(ch + 1) * 128], id32)
                    if half == 0:
                        B.act(ost_half[0], bk, AF.Copy, scale=rcol[:, ch:ch + 1])
                    else:
                        B.ts("dve", ost_half[1], bk, rcol[:, ch:ch + 1], ALU.mult)
                    B.dma("sp", out[tok0 + ch * 128: tok0 + (ch + 1) * 128, half * 512:(half + 1) * 512], ost_half[half].ap,
                          ost_res[half], reads=[ost_half[half]])

        nmain = self.n_main
        for it in range(nmain + 1):
            hT = hbuf[it % 2][0]
            if it < nmain:
                layer0(hT, it, 4, it, it == 0, hbuf[it % 2][1])
            else:
                layer0(hT, it, 1, 8, False, hbuf[it % 2][1])
            if it >= 1 and self.do_l1:
                layer1(hbuf[(it - 1) % 2][0], it - 1, hT, it - 1 == 0)
        if self.debug:
            pass


_NC_CACHE = {}


def _make_ptab(flip):
    wins = (2, 4, 8, 16)
    n = 384
    tabs = np.zeros((16, 128, 128), np.float32)
    for g, w in enumerate(wins):
        for edge in (False, True):
            P = np.zeros((n, n), np.float32)
            for t in range(n):
                if not flip:
                    lo, hi = t - w // 2, t + w - w // 2
                else:
                    lo, hi = t - w // 2 + 1, t + w // 2 + 1
                if edge:
                    lo = max(lo, 128)
                cnt = hi - lo
                if cnt <= 0:
                    continue
                lo_c, hi_c = max(lo, 0), min(hi, n)
                P[lo_c:hi_c, t] = 1.0 / cnt
                P[t, t] -= 1.0
            if not edge:
                tabs[g] = P[0:128, 128:256]
                tabs[4 + g] = P[128:256, 128:256]
                tabs[8 + g] = P[256:384, 128:256]
            else:
                tabs[12 + g] = P[128:256, 128:256]
    return tabs


def kernel(x, p, norm_mix, norm_mlp, norm_ple, norm_final, mlstm_w_in, mlstm_b_gates, mlstm_head_norm,
           mlstm_w_out, pool_w_in, pool_w_grp, pool_scale, pool_w_out, mlp_w1, mlp_w2, ple_w, ple_gate_w,
           ple_gate_b):
    f = lambda a: np.ascontiguousarray(np.asarray(a, dtype=np.float32))
    x = f(x)
    p = f(p)
    if "nc" not in _NC_CACHE:
        _NC_CACHE["nc"] = Builder().build()
    nc = _NC_CACHE["nc"]
    w_in = f(mlstm_w_in)[0]
    w_in_flip = w_in.copy()
    w_in_flip[:, 3072:3080] = w_in[:, 3080:3088]
    w_in_flip[:, 3080:3088] = w_in[:, 3072:3080]
    bgt = f(mlstm_b_gates)[0]
    bg_n = bgt.reshape(16)
    bg_f = bgt[[2, 3, 0, 1]].reshape(16)
    common = {
        "norm_mix": f(norm_mix), "norm_mlp": f(norm_mlp), "norm_ple": f(norm_ple), "norm_final": f(norm_final),
        "mlstm_head_norm": f(mlstm_head_norm)[0], "mlstm_w_out": f(mlstm_w_out)[0],
        "pool_w_in": f(pool_w_in)[0], "pool_w_grp": f(pool_w_grp)[0], "pool_scale": f(pool_scale)[0],
        "pool_w_out": f(pool_w_out)[0], "mlp_w1": f(mlp_w1), "mlp_w2": f(mlp_w2), "ple_w": f(ple_w),
        "ple_gate_w": f(ple_gate_w), "ple_gate_b": f(ple_gate_b),
    }
    pt = [_make_ptab(False), _make_ptab(True)]
    in_maps = []
    for c in range(8):
        b, half = c // 2, c % 2
        if half == 0:
            xl = x[b]
            pl = p[:, b, 0:TOK_OWN + 128]
        else:
            xl = np.ascontiguousarray(x[b, ::-1])
            pl = np.ascontiguousarray(p[:, b, ::-1][:, 0:TOK_OWN + 128])
        m = dict(common)
        m["x"] = np.ascontiguousarray(xl)
        m["p"] = np.ascontiguousarray(pl)
        m["mlstm_w_in"] = w_in if half == 0 else w_in_flip
        m["mlstm_b_gates"] = bg_n if half == 0 else bg_f
        m["ptab"] = pt[half]
        in_maps.append(m)
    res = run_bass_kernel_spmd(nc, in_maps, core_ids=list(range(8)))
    outp = np.empty((NB, S, D), np.float32)
    for c in range(8):
        b, half = c // 2, c % 2
        o = res.results[c]["out"]
        if half == 0:
            outp[b, 0:TOK_OWN] = o
        else:
            outp[b, TOK_OWN:] = o[::-1]
    return outp
```
